# Optimizing a Trainium2 kernel written in Bass

```python
import math
import jax, jax.numpy as jnp
from jax import lax
import numpy as np

D_MODEL = 1024
BATCH = 4
SEQ = 8192
DEPTH = 2

CHUNK = 64
N_MEM = 256
EPS = 1e-6
ROPE_BASE = 10000.0
MAX_POS_OFFSET = 16384

RET_HEADS = 4
RET_DK = 96
RET_DV = 96
RET_QK = RET_HEADS * RET_DK
RET_VW = RET_HEADS * RET_DV
HG_HEADS = 4
HG_DK = 128
HG_DV = 96
HG_QK = HG_HEADS * HG_DK
HG_VW = HG_HEADS * HG_DV
LRU_WIDTH = 256
LRU_BLOCKS = 4
LRU_BW = LRU_WIDTH // LRU_BLOCKS
CONV_W = 4
LRU_C = 8.0
D_IN = 2 * RET_QK + 2 * RET_VW + 2 * HG_QK + 2 * HG_VW + 2 * LRU_WIDTH
D_MIX = RET_VW + HG_VW + LRU_WIDTH
X_HEADS = 4
X_HD = D_MODEL // X_HEADS
D_FF = 2816

kernel_name = "hymba_ret_hgrn2_rglru_macaron_encoder"

F32 = jnp.float32


def rms_norm(x, g):
    xf = x.astype(F32)
    y = xf * lax.rsqrt(jnp.mean(xf * xf, axis=-1, keepdims=True) + EPS)
    return (y * g.astype(F32)).astype(x.dtype)


def head_group_norm(o, g):
    of = o.astype(F32)
    c = of - jnp.mean(of, axis=-1, keepdims=True)
    y = c * lax.rsqrt(jnp.mean(c * c, axis=-1, keepdims=True) + EPS)
    return y.reshape(o.shape[0], o.shape[1], -1) * g.astype(F32)


def head_rms_norm(o, g):
    of = o.astype(F32)
    y = of * lax.rsqrt(jnp.mean(of * of, axis=-1, keepdims=True) + EPS)
    return y.reshape(o.shape[0], o.shape[1], -1) * g.astype(F32)


def swiglu(h, w1, w3, w2):
    return (jax.nn.silu(h @ w1) * (h @ w3)) @ w2


def apply_rope(t, positions):
    d = t.shape[-1]
    half = d // 2
    inv_freq = jnp.exp(-math.log(ROPE_BASE) * jnp.arange(half, dtype=F32) / half)
    ang = positions.astype(F32)[..., None] * inv_freq
    cos = jnp.cos(ang)[:, :, None, :]
    sin = jnp.sin(ang)[:, :, None, :]
    t1 = t[..., :half].astype(F32)
    t2 = t[..., half:].astype(F32)
    return jnp.concatenate([t1 * cos - t2 * sin, t1 * sin + t2 * cos], axis=-1).astype(t.dtype)


def retention(q, k, v, positions):
    b, s, h, dk = q.shape
    dv = v.shape[-1]
    n = s // CHUNK
    q = apply_rope(q, positions) * (dk ** -0.5)
    k = apply_rope(k, positions)
    log_gamma = jnp.log1p(-jnp.exp2(-5.0 - jnp.arange(h, dtype=F32)))
    idx = jnp.arange(CHUNK, dtype=F32)
    dist = jnp.abs(idx[:, None] - idx[None, :])
    intra_decay = jnp.exp(log_gamma[:, None, None] * dist)
    q_in = jnp.exp(log_gamma[None, :] * (idx[:, None] + 1.0))
    k_out = jnp.exp(log_gamma[None, :] * (CHUNK - 1.0 - idx[:, None]))
    chunk_decay = jnp.exp(log_gamma * CHUNK)
    qc = q.reshape(b, n, CHUNK, h, dk)
    kc = k.reshape(b, n, CHUNK, h, dk)
    vc = v.reshape(b, n, CHUNK, h, dv)
    scores = jnp.einsum('bnihd,bnjhd->bnhij', qc, kc) * intra_decay
    intra = jnp.einsum('bnhij,bnjhe->bnihe', scores, vc)
    kv_local = jnp.einsum('bnjhd,jh,bnjhe->nbhde', kc, k_out, vc)

    def step(state, kv):
        return chunk_decay[None, :, None, None] * state + kv, state

    _, prev = lax.scan(step, jnp.zeros(kv_local.shape[1:], kv_local.dtype), kv_local)
    inter = jnp.einsum('bnihd,ih,nbhde->bnihe', qc, q_in, prev)
    return (intra + inter).reshape(b, s, h, dv)


def hgrn2(q, f_logit, inp, lb):
    b, s, h, dk = q.shape
    dv = inp.shape[-1]
    n = s // CHUNK
    lbh = lb.reshape(h, dk).astype(F32)
    f = lbh + (1.0 - lbh) * jax.nn.sigmoid(f_logit.astype(F32))
    log_f = jnp.log(f)
    k = 1.0 - f
    q = jax.nn.silu(q)

    def to_chunks(t):
        return t.reshape(b, n, CHUNK, h, t.shape[-1]).swapaxes(0, 1)

    def step(state, xs):
        qc, kc, ic, lfc = xs
        cum = jnp.cumsum(lfc, axis=1)
        decay = jnp.exp(-jnp.abs(cum[:, :, None] - cum[:, None, :]))
        scores = jnp.einsum('bthd,bjhd,btjhd->bhtj', qc, kc, decay)
        intra = jnp.einsum('bhtj,bjhe->bthe', scores, ic)
        inter = jnp.einsum('bthd,bhde->bthe', qc * jnp.exp(cum), state)
        last = cum[:, -1]
        new_state = jnp.exp(last)[..., None] * state + jnp.einsum(
            'bjhd,bjhe->bhde', kc * jnp.exp(last[:, None] - cum), ic)
        return new_state, (intra + inter).astype(F32)

    state0 = jnp.zeros((b, h, dk, dv), F32)
    _, out = lax.scan(step, state0, (to_chunks(q), to_chunks(k), to_chunks(inp), to_chunks(log_f)))
    return out.swapaxes(0, 1).reshape(b, s, h, dv)


def _linear_combine(c1, c2):
    a1, b1 = c1
    a2, b2 = c2
    return a1 * a2, a2 * b1 + b2


def rg_lru_branch(xb, gb, conv_w, conv_b, wa, ba, wx, bx, lam):
    b, s, w = xb.shape
    xc = lax.conv_general_dilated(
        xb, conv_w[:, None, :].astype(xb.dtype), window_strides=(1,),
        padding=[(CONV_W - 1, 0)], dimension_numbers=('NWC', 'WIO', 'NWC'),
        feature_group_count=w) + conv_b
    xr = xc.reshape(b, s, LRU_BLOCKS, LRU_BW)
    r = jax.nn.sigmoid(jnp.einsum('bsnc,ncd->bsnd', xr, wa).reshape(b, s, w) + ba).astype(F32)
    i = jax.nn.sigmoid(jnp.einsum('bsnc,ncd->bsnd', xr, wx).reshape(b, s, w) + bx).astype(F32)
    log_a = -LRU_C * jax.nn.softplus(-lam.astype(F32)) * r
    a = jnp.exp(log_a)
    u = jnp.sqrt(-jnp.expm1(2.0 * log_a)) * (i * xc.astype(F32))
    _, hs = lax.associative_scan(_linear_combine, (a, u), axis=1)
    return (hs * jax.nn.gelu(gb.astype(F32))).astype(xb.dtype)


def cross_attention(h, mem_n, wq, wkv, wo):
    b, s, _ = h.shape
    m = mem_n.shape[1]
    q = (h @ wq).reshape(b, s, X_HEADS, X_HD)
    kv = mem_n @ wkv
    k = kv[..., :D_MODEL].reshape(b, m, X_HEADS, X_HD)
    v = kv[..., D_MODEL:].reshape(b, m, X_HEADS, X_HD)
    sc = jnp.einsum('bshd,bmhd->bhsm', q, k).astype(F32) * (X_HD ** -0.5)
    p = jax.nn.softmax(sc, axis=-1).astype(v.dtype)
    o = jnp.einsum('bhsm,bmhd->bshd', p, v).reshape(b, s, D_MODEL)
    return o @ wo


def setup_inputs(seed: int = 0) -> dict:
    key = jax.random.key(seed)
    ks = iter(jax.random.split(key, 40))

    def nrm(shape, scale):
        return jax.random.normal(next(ks), shape, F32) * scale

    def gain(shape):
        return 1.0 + 0.02 * jax.random.normal(next(ks), shape, F32)

    x = nrm((BATCH, SEQ, D_MODEL), 1.0)
    mem = nrm((BATCH, N_MEM, D_MODEL), 1.0)
    start = jax.random.randint(next(ks), (BATCH, 1), 0, MAX_POS_OFFSET, dtype=jnp.int32)
    positions = (start + jnp.arange(SEQ, dtype=jnp.int32)[None, :]).astype(jnp.int32)
    ffn1_norm_g = gain((DEPTH, D_MODEL))
    ffn1_w1 = nrm((DEPTH, D_MODEL, D_FF), D_MODEL ** -0.5)
    ffn1_w3 = nrm((DEPTH, D_MODEL, D_FF), D_MODEL ** -0.5)
    ffn1_w2 = nrm((DEPTH, D_FF, D_MODEL), D_FF ** -0.5)
    mix_norm_g = gain((DEPTH, D_MODEL))
    w_in = nrm((DEPTH, D_MODEL, D_IN), D_MODEL ** -0.5)
    ret_gn_g = gain((DEPTH, RET_VW))
    hg_lb_param = nrm((DEPTH, HG_QK), 0.1)
    hg_norm_g = gain((DEPTH, HG_VW))
    lru_conv_w = nrm((DEPTH, CONV_W, LRU_WIDTH), CONV_W ** -0.5)
    lru_conv_b = nrm((DEPTH, LRU_WIDTH), 0.01)
    lru_wa = nrm((DEPTH, LRU_BLOCKS, LRU_BW, LRU_BW), LRU_BW ** -0.5)
    lru_ba = nrm((DEPTH, LRU_WIDTH), 0.01)
    lru_wx = nrm((DEPTH, LRU_BLOCKS, LRU_BW, LRU_BW), LRU_BW ** -0.5)
    lru_bx = nrm((DEPTH, LRU_WIDTH), 0.01)
    a_pow = jax.random.uniform(next(ks), (DEPTH, LRU_WIDTH), F32, minval=0.9, maxval=0.999)
    a0 = a_pow ** (1.0 / LRU_C)
    lru_lambda = jnp.log(a0) - jnp.log1p(-a0)
    w_out = nrm((DEPTH, D_MIX, D_MODEL), D_MIX ** -0.5)
    xattn_norm_g = gain((DEPTH, D_MODEL))
    xattn_mem_g = gain((DEPTH, D_MODEL))
    xattn_wq = nrm((DEPTH, D_MODEL, D_MODEL), D_MODEL ** -0.5)
    xattn_wkv = nrm((DEPTH, D_MODEL, 2 * D_MODEL), D_MODEL ** -0.5)
    xattn_wo = nrm((DEPTH, D_MODEL, D_MODEL), D_MODEL ** -0.5)
    ffn2_norm_g = gain((DEPTH, D_MODEL))
    ffn2_w1 = nrm((DEPTH, D_MODEL, D_FF), D_MODEL ** -0.5)
    ffn2_w3 = nrm((DEPTH, D_MODEL, D_FF), D_MODEL ** -0.5)
    ffn2_w2 = nrm((DEPTH, D_FF, D_MODEL), D_FF ** -0.5)
    final_norm_g = gain((D_MODEL,))
    return {
        "x": x, "mem": mem, "positions": positions,
        "ffn1_norm_g": ffn1_norm_g, "ffn1_w1": ffn1_w1, "ffn1_w3": ffn1_w3, "ffn1_w2": ffn1_w2,
        "mix_norm_g": mix_norm_g, "w_in": w_in, "ret_gn_g": ret_gn_g,
        "hg_lb_param": hg_lb_param, "hg_norm_g": hg_norm_g,
        "lru_conv_w": lru_conv_w, "lru_conv_b": lru_conv_b, "lru_wa": lru_wa, "lru_ba": lru_ba,
        "lru_wx": lru_wx, "lru_bx": lru_bx, "lru_lambda": lru_lambda, "w_out": w_out,
        "xattn_norm_g": xattn_norm_g, "xattn_mem_g": xattn_mem_g, "xattn_wq": xattn_wq,
        "xattn_wkv": xattn_wkv, "xattn_wo": xattn_wo,
        "ffn2_norm_g": ffn2_norm_g, "ffn2_w1": ffn2_w1, "ffn2_w3": ffn2_w3, "ffn2_w2": ffn2_w2,
        "final_norm_g": final_norm_g,
    }


def reference(x, mem, positions, ffn1_norm_g, ffn1_w1, ffn1_w3, ffn1_w2, mix_norm_g, w_in,
              ret_gn_g, hg_lb_param, hg_norm_g, lru_conv_w, lru_conv_b, lru_wa, lru_ba,
              lru_wx, lru_bx, lru_lambda, w_out, xattn_norm_g, xattn_mem_g, xattn_wq,
              xattn_wkv, xattn_wo, ffn2_norm_g, ffn2_w1, ffn2_w3, ffn2_w2, final_norm_g):
    b, s, _ = x.shape
    sizes = (RET_QK, RET_QK, RET_VW, RET_VW, HG_QK, HG_QK, HG_VW, HG_VW, LRU_WIDTH, LRU_WIDTH)
    points = [sum(sizes[:j + 1]) for j in range(len(sizes) - 1)]
    lb_all = jnp.cumsum(jax.nn.softmax(hg_lb_param.astype(F32), axis=0), axis=0)
    lb_all = lb_all - lb_all[0:1]
    for l in range(DEPTH):
        x = x + 0.5 * swiglu(rms_norm(x, ffn1_norm_g[l]), ffn1_w1[l], ffn1_w3[l], ffn1_w2[l])
        h = rms_norm(x, mix_norm_g[l])
        proj = h @ w_in[l]
        rq, rk, rv, rg, hq, hf, hi, hg, lx, lg = jnp.split(proj, points, axis=-1)
        ret = retention(rq.reshape(b, s, RET_HEADS, RET_DK), rk.reshape(b, s, RET_HEADS, RET_DK),
                        rv.reshape(b, s, RET_HEADS, RET_DV), positions)
        ret = (head_group_norm(ret, ret_gn_g[l]) * jax.nn.silu(rg.astype(F32))).astype(x.dtype)
        hgo = hgrn2(hq.reshape(b, s, HG_HEADS, HG_DK), hf.reshape(b, s, HG_HEADS, HG_DK),
                    hi.reshape(b, s, HG_HEADS, HG_DV), lb_all[l])
        hgo = (head_rms_norm(hgo, hg_norm_g[l]) * jax.nn.silu(hg.astype(F32))).astype(x.dtype)
        lru = rg_lru_branch(lx, lg, lru_conv_w[l], lru_conv_b[l], lru_wa[l], lru_ba[l],
                            lru_wx[l], lru_bx[l], lru_lambda[l])
        x = x + jnp.concatenate([ret, hgo, lru], axis=-1) @ w_out[l]
        x = x + cross_attention(rms_norm(x, xattn_norm_g[l]), rms_norm(mem, xattn_mem_g[l]),
                                xattn_wq[l], xattn_wkv[l], xattn_wo[l])
        x = x + 0.5 * swiglu(rms_norm(x, ffn2_norm_g[l]), ffn2_w1[l], ffn2_w3[l], ffn2_w2[l])
    return rms_norm(x, final_norm_g)
```

```python
import math
from contextlib import ExitStack

import numpy as np
import concourse.bass as bass
import concourse.mybir as mybir
from concourse.bass_utils import run_bass_kernel_spmd

F32 = mybir.dt.float32
BF16 = mybir.dt.bfloat16
I32 = mybir.dt.int32
AF = mybir.ActivationFunctionType
ALU = mybir.AluOpType
AX = mybir.AxisListType

D = 1024
DFF = 2816
NFC = DFF // 128
EPS = 1e-6

ENGS = ("pe", "act", "dve", "pool", "sp")
import os as _os
SAME_ENGINE_SYNC = _os.environ.get("SES", "1") == "1"


class Prog:
    def __init__(self, nc, stack, same_engine_sync=SAME_ENGINE_SYNC):
        self.nc = nc
        self.stack = stack
        self.ops = []
        self.eng_ops = {e: [] for e in ENGS}
        self.last_write = {}
        self.readers = {}
        self.seen = {e: {} for e in ENGS}
        self.dsems = {}
        self.same_engine_sync = same_engine_sync
        self.n_dsem = 0
        self.excl = set()

    def dsem(self, name):
        if name not in self.dsems:
            h = self.stack.enter_context(self.nc.semaphore("d_" + name))
            self.dsems[name] = {"h": h, "val": 0, "name": name}
        return self.dsems[name]

    def add(self, eng, fn, reads=(), writes=(), dsem=None, after=(), inc=16):
        oid = len(self.ops)
        seq = len(self.eng_ops[eng])
        writes = list(writes) + [r for r in reads if r in self.excl and r not in writes]
        reads = [r for r in reads if r not in self.excl]
        deps = set()
        for r in reads:
            lw = self.last_write.get(r)
            if lw is not None:
                deps.add(lw)
        for w in writes:
            lw = self.last_write.get(w)
            if lw is not None:
                deps.add(lw)
            for rd in self.readers.get(w, {}).values():
                deps.add(rd)
        seen = self.seen[eng]
        waits = []
        deps.update(after)
        for d in sorted(deps):
            dop = self.ops[d]
            if dop["dsem"] is not None:
                key = ("d", dop["dsem"]["name"])
                val = dop["dval"]
            else:
                if dop["eng"] == eng and (eng == "pe" or not self.same_engine_sync) and d not in after:
                    continue
                key = ("e", dop["eng"])
                val = dop["seq"]
            if seen.get(key, -1) >= val:
                continue
            waits.append(d)
            dop["marked"] = True
            for k, v in dop["know"].items():
                if seen.get(k, -1) < v:
                    seen[k] = v
            if seen.get(key, -1) < val:
                seen[key] = val
        op = {"eng": eng, "fn": fn, "seq": seq, "waits": waits, "marked": False, "dsem": None, "dval": None}
        if dsem is not None:
            ds = self.dsem(dsem)
            ds["val"] += inc
            op["inc"] = inc
            op["dsem"] = ds
            op["dval"] = ds["val"]
            op["marked"] = True
            know = dict(seen)
            know[("d", ds["name"])] = ds["val"]
        else:
            know = dict(seen)
            know[("e", eng)] = seq
        op["know"] = know
        self.ops.append(op)
        self.eng_ops[eng].append(oid)
        for r in reads:
            self.readers.setdefault(r, {})[eng if dsem is None else ("dma", dsem)] = oid
        for w in writes:
            self.last_write[w] = oid
            self.readers[w] = {}
        return oid

    def mark(self, name):
        self.marks = getattr(self, "marks", {})
        self.marks.setdefault(name, len(self.ops))

    def barrier(self):
        start = getattr(self, "emitted", 0)
        deps = []
        for e in ENGS:
            if self.eng_ops[e] and self.eng_ops[e][-1] >= start:
                deps.append(self.eng_ops[e][-1])
        lastd = {}
        for oid in range(start, len(self.ops)):
            op = self.ops[oid]
            if op["dsem"] is not None:
                lastd[op["dsem"]["name"]] = oid
        deps += list(lastd.values())
        b = self.add("sp", lambda h: h.nop(), after=tuple(deps))
        for e in ENGS:
            if e != "sp":
                self.add(e, lambda h: h.nop(), after=(b,))
        self.last_write = {}
        self.readers = {}

    def emit(self, limit=None):
        nc = self.nc
        start = getattr(self, "emitted", 0)
        if limit is None:
            limit = len(self.ops)
        if not hasattr(self, "esem"):
            self.esem = {e: self.stack.enter_context(nc.semaphore("e_" + e)) for e in ENGS}
            self.ecnt = {e: 0 for e in ENGS}
        esem = self.esem
        for e in ENGS:
            for oid in self.eng_ops[e]:
                if oid < start:
                    continue
                op = self.ops[oid]
                if op["dsem"] is None and op["marked"]:
                    self.ecnt[e] += 1
                    op["cnt"] = self.ecnt[e]
        ops = self.ops
        eng_ops = self.eng_ops
        self.emitted = limit

        def run(e, h):
            for oid in eng_ops[e]:
                if oid < start:
                    continue
                if oid >= limit:
                    break
                op = ops[oid]
                for d in op["waits"]:
                    dop = ops[d]
                    if dop["dsem"] is not None:
                        h.wait_ge(dop["dsem"]["h"], dop["dval"])
                    else:
                        h.wait_ge(esem[dop["eng"]], dop["cnt"])
                inst = op["fn"](h)
                if op["dsem"] is not None:
                    inst.then_inc(op["dsem"]["h"], op["inc"])
                elif op["marked"]:
                    inst.then_inc(esem[e], 1)

        with nc.Block() as block:
            @block.tensor
            def _(h):
                run("pe", h)

            @block.scalar
            def _(h):
                run("act", h)

            @block.vector
            def _(h):
                run("dve", h)

            @block.gpsimd
            def _(h):
                run("pool", h)

            @block.sync
            def _(h):
                run("sp", h)

    def finish_wait(self, eng, keys):
        self.add(eng, lambda h: h.nop(), reads=tuple(keys))


class Ctx:
    def __init__(self, nc, stack):
        self.nc = nc
        self.stack = stack
        self.n = 0

    def sb(self, shape, dt, name=None):
        self.n += 1
        return self.stack.enter_context(self.nc.sbuf_tensor(f"{name or 't'}_{self.n}", list(shape), dt))

    def ps(self, shape, dt, name=None):
        self.n += 1
        return self.stack.enter_context(self.nc.psum_tensor(f"{name or 'p'}_{self.n}", list(shape), dt))


def bcast_rows(ap_1d, nparts):
    n = ap_1d.shape[0]
    return bass.AP(ap_1d.tensor, ap_1d.offset, [[0, nparts], [1, n]])


def make_identity(P, cx, dt=BF16):
    nc = cx.nc
    it = cx.sb([128, 128], I32, "iota")
    ident = cx.sb([128, 128], dt, "ident")
    P.add("pool", lambda h: h.iota(it[:], pattern=[[1, 128]], base=0, channel_multiplier=-1), writes=["iota_t"])
    P.add("dve", lambda h: h.tensor_scalar(out=ident[:], in0=it[:], scalar1=0.0, scalar2=None, op0=ALU.is_equal),
          reads=["iota_t"], writes=["ident"])
    return ident


def rms_rstd(P, cx, x_ap, xkey, n, scratch, ss, rstd, tag, junk_keys=()):
    P.add("act", lambda h: h.activation(out=scratch, in_=x_ap, func=AF.Square, accum_out=ss),
          reads=[xkey], writes=[tag + "_junk", tag + "_ss"] + list(junk_keys))
    P.add("dve", lambda h: h.tensor_scalar(out=ss, in0=ss, scalar1=1.0 / n, scalar2=EPS, op0=ALU.mult, op1=ALU.add),
          reads=[tag + "_ss"], writes=[tag + "_ss"])
    P.add("act", lambda h: h.activation(out=ss, in_=ss, func=AF.Sqrt), reads=[tag + "_ss"], writes=[tag + "_ss"])
    P.add("dve", lambda h: h.reciprocal(out=rstd, in_=ss), reads=[tag + "_ss"], writes=[tag + "_rstd"])


def ffn_w13_key(fc):
    return f"w13v{fc // 11}"


def ffn_w2_key(fc):
    return f"w2v{fc // 11}"


def load_ffn_weights(P, cx, w1, w3, w2, tag=""):
    w1s = cx.sb([128, 8, DFF], BF16, "w1s")
    w3s = cx.sb([128, 8, DFF], BF16, "w3s")
    w2s = cx.sb([128, NFC, D], BF16, "w2s")
    HALF = DFF // 2
    for hf in range(2):
        for (ws, w) in ((w1s, w1), (w3s, w3)):
            P.add("pool",
                  lambda h, ws=ws, w=w, hf=hf: h.dma_start(
                      out=ws[:, :, hf * HALF:(hf + 1) * HALF],
                      in_=w[:, hf * HALF:(hf + 1) * HALF].rearrange("(c p) n -> p c n", p=128)),
                  writes=[f"w13v{hf}"], dsem=f"w13v{hf}")
    for wv in range(2):
        P.add("pool", lambda h, wv=wv: h.dma_start(
            out=w2s[:, wv * 11:(wv + 1) * 11, :],
            in_=w2[wv * 11 * 128:(wv + 1) * 11 * 128, :].rearrange("(c p) n -> p c n", p=128)),
            writes=[f"w2v{wv}"], dsem=f"w2v{wv}")
    return w1s, w3s, w2s


def build_ffn(T, final_norm=False):
    nc = bass.Bass("TRN2", target_bir_lowering=False)
    x = nc.dram_tensor("x", [T, D], F32, kind="ExternalInput").ap()
    g = nc.dram_tensor("g", [D], F32, kind="ExternalInput").ap()
    w1 = nc.dram_tensor("w1", [D, DFF], F32, kind="ExternalInput").ap()
    w3 = nc.dram_tensor("w3", [D, DFF], F32, kind="ExternalInput").ap()
    w2 = nc.dram_tensor("w2", [DFF, D], F32, kind="ExternalInput").ap()
    gf = nc.dram_tensor("gf", [D], F32, kind="ExternalInput").ap() if final_norm else None
    y = nc.dram_tensor("y", [T, D], F32, kind="ExternalOutput").ap()
    with ExitStack() as stack:
        P = Prog(nc, stack)
        cx = Ctx(nc, stack)
        ident = make_identity(P, cx)
        stage_ffn(P, nc, ident, T, x, y, g, w1, w3, w2, "final" if final_norm else "plain", gx=gf)
    return nc


def load_w_bf16(P, cx, w, kin, nout, name, col0=0):
    kc = kin // 128
    ws = cx.sb([128, kc, nout], BF16, name)
    step = 1024
    for n0 in range(0, nout, step):
        n1 = min(nout, n0 + step)
        P.add("pool", lambda h, n0=n0, n1=n1: h.dma_start(
            out=ws[:, :, n0:n1], in_=w[:, col0 + n0:col0 + n1].rearrange("(c p) n -> p c n", p=128)),
            writes=[name], dsem=name)
    return ws


def xattn_body(P, nc, cx, ident, T, A, fused):
    x = A["x"]; mx_in = A.get("mixed"); mem = A["mem"]; gx = A["gx"]; gm = A["gm"]; w_out = A["w_out"]
    wq = A["wq"]; wkv = A["wkv"]; wo = A["wo"]; y = A["y"]
    NG = T // 512
    SC = 1.0 / 16.0
    if True:
        gxbc = cx.sb([128, D], F32, "gxbc")
        gmbc = cx.sb([128, D], F32, "gmbc")
        P.add("sp", lambda h: h.dma_start(out=gxbc[:], in_=bcast_rows(gx, 128)), writes=["gxbc"], dsem="gxbc")
        P.add("sp", lambda h: h.dma_start(out=gmbc[:], in_=bcast_rows(gm, 128)), writes=["gmbc"], dsem="gmbc")
        wkvs = load_w_bf16(P, cx, wkv, D, 2 * D, "wkvs")
        w_outs = load_w_bf16(P, cx, w_out, D, D, "w_outs")
        wqs = load_w_bf16(P, cx, wq, D, D, "wqs")
        wos = load_w_bf16(P, cx, wo, D, D, "wos")

        xin = [cx.sb([128, D], F32, "xin") for _ in range(2)]
        junk = cx.sb([128, D], BF16, "junk")
        st = [cx.sb([128, 16], F32, "st") for _ in range(2)]
        hb = [cx.sb([128, D], BF16, "hb") for _ in range(2)]
        pT = [cx.ps([128, D], BF16, "pT") for _ in range(2)]
        pA = [cx.ps([128, 512], F32, "pA") for _ in range(2)]
        pS = cx.ps([128, 1024], F32, "pS")
        pY = [cx.ps([128, 512], F32, "pY") for _ in range(2)]
        memT = cx.sb([128, 8, 256], BF16, "memT")
        kT = cx.sb([128, 8, 256], BF16, "kT")
        vv = cx.sb([128, 2, D], BF16, "vv")
        cnt = {"tr": 0, "a": 0, "y": 0}
        P.excl.update(["pT0", "pT1", "pA0", "pA1", "pS", "pY0", "pY1"])

        def transpose8(src_fn, srckeys, dst_fn, dstkey):
            b = cnt["tr"] % 2
            cnt["tr"] += 1
            for dc in range(8):
                P.add("pe", lambda h, dc=dc, b=b: h.transpose(
                    out=pT[b][:, dc * 128:(dc + 1) * 128], in_=src_fn(dc), identity=ident[:]),
                    reads=list(srckeys) + ["ident"], writes=[f"pT{b}"])
            P.add("act", lambda h, b=b: h.copy(out=dst_fn(), in_=pT[b][:].rearrange("p (c t) -> p c t", c=8)),
                  reads=[f"pT{b}"], writes=[dstkey])

        for mt in range(2):
            b = mt
            P.add("sp", lambda h, b=b, mt=mt: h.dma_start(out=xin[b][:], in_=mem[mt * 128:(mt + 1) * 128, :]),
                  writes=[f"xin{b}"], dsem=f"xin{b}")
            rms_rstd(P, cx, xin[b][:], f"xin{b}", D, junk[:], st[b][:, 0:1], st[b][:, 1:2], f"n{b}")
            P.add("dve", lambda h, b=b: h.scalar_tensor_tensor(
                out=hb[b][:], in0=xin[b][:], scalar=st[b][:, 1:2], op0=ALU.mult, in1=gmbc[:], op1=ALU.mult),
                reads=[f"xin{b}", f"n{b}_rstd", "gmbc"], writes=[f"hb{b}"])
            transpose8(lambda dc, b=b: hb[b][:, dc * 128:(dc + 1) * 128], [f"hb{b}"],
                       lambda mt=mt: memT[:, :, mt * 128:(mt + 1) * 128], "memT")
        for c in range(8):
            b = cnt["a"] % 2
            cnt["a"] += 1
            for dc in range(8):
                P.add("pe", lambda h, b=b, c=c, dc=dc: h.matmul(
                    pA[b][:, 0:256], lhsT=wkvs[:, dc, c * 128:(c + 1) * 128], rhs=memT[:, dc, :],
                    start=(dc == 0), stop=(dc == 7)), reads=["wkvs", "memT"], writes=[f"pA{b}"])
            P.add("act", lambda h, b=b, c=c: h.copy(out=kT[:, c, :], in_=pA[b][:, 0:256]),
                  reads=[f"pA{b}"], writes=["kT"])
        for mc in range(2):
            for hf in range(2):
                b = cnt["a"] % 2
                cnt["a"] += 1
                for dc in range(8):
                    P.add("pe", lambda h, b=b, mc=mc, hf=hf, dc=dc: h.matmul(
                        pA[b][:], lhsT=memT[:, dc, mc * 128:(mc + 1) * 128],
                        rhs=wkvs[:, dc, D + hf * 512:D + (hf + 1) * 512],
                        start=(dc == 0), stop=(dc == 7)), reads=["wkvs", "memT"], writes=[f"pA{b}"])
                P.add("act", lambda h, b=b, mc=mc, hf=hf: h.copy(
                    out=vv[:, mc, hf * 512:(hf + 1) * 512], in_=pA[b][:]),
                    reads=[f"pA{b}"], writes=["vv"])

        xg = cx.sb([128, 4, D], F32, "xg")
        hbn = [cx.sb([128, D], BF16, f"hbn{i}") for i in range(4)]
        stn = cx.sb([128, 8], F32, "stn")
        if fused:
            cand = [cx.sb([128, 2, D], BF16, f"cand{i}") for i in range(2)]
        mT = cx.sb([128, 8, 512], BF16, "mT")
        hT = cx.sb([128, 8, 512], BF16, "hT")
        qT = cx.sb([128, 8, 512], BF16, "qT")
        ppT = cx.sb([128, 8, 512], BF16, "ppT")
        oT = cx.sb([128, 8, 512], BF16, "oT")
        pe_ = cx.sb([128, 4, 256], F32, "pexp")
        pn2 = [cx.sb([128, 4, 256], BF16, f"pn{i}") for i in range(2)]
        yo = [cx.sb([128, D], F32, "yo") for _ in range(2)]
        ti = 0
        tcnt = {"m": 0}

        def emit_mixed(gi):
            for j in range(4):
                t0 = gi * 512 + j * 128
                b = tcnt["m"] % 2
                tcnt["m"] += 1
                if not fused:
                    P.add("sp", lambda h, b=b, t0=t0: h.dma_start(out=xin[b][:], in_=mx_in[t0:t0 + 128, :]),
                          writes=[f"xin{b}"], dsem=f"xin{b}")
                    P.add("dve", lambda h, b=b: h.tensor_copy(out=hb[b][:], in_=xin[b][:]),
                          reads=[f"xin{b}"], writes=[f"hb{b}"])
                else:
                    k0 = t0 // 2048
                    row = t0 % 2048
                    for f_ in range(2):
                        for r_ in range(2):
                            P.add("sp", lambda h, b=b, f_=f_, r_=r_, k0=k0, row=row: h.dma_start(
                                out=cand[b][:, f_, r_ * 512:(r_ + 1) * 512],
                                in_=A["Mg"][2 * f_ + k0].ap()[r_ * 2048 + row:r_ * 2048 + row + 128, :]),
                                reads=[f"Mg{2 * f_ + k0}"], writes=[f"cand{b}"], dsem=f"cand{b}")
                    P.add("dve", lambda h, b=b: h.tensor_scalar(out=cand[b][:, 0, :], in0=cand[b][:, 0, :], scalar1=A["flg"][:, 0:1],
                                                                scalar2=None, op0=ALU.mult), reads=[f"cand{b}", "flg"], writes=[f"cand{b}"])
                    P.add("dve", lambda h, b=b: h.scalar_tensor_tensor(out=hb[b][:], in0=cand[b][:, 1, :], scalar=A["flg"][:, 1:2],
                                                                       op0=ALU.mult, in1=cand[b][:, 0, :], op1=ALU.add),
                          reads=[f"cand{b}", "flg"], writes=[f"hb{b}"])
                transpose8(lambda dc, b=b: hb[b][:, dc * 128:(dc + 1) * 128], [f"hb{b}"],
                           lambda j=j: mT[:, :, j * 128:(j + 1) * 128], "mT")

        emit_mixed(0)
        for gi in range(NG):
            for j in range(4):
                t0 = gi * 512 + j * 128
                P.add("sp", lambda h, j=j, t0=t0: h.dma_start(out=xg[:, j, :], in_=x[t0:t0 + 128, :]),
                      reads=["srcdram"], writes=[f"xg{j}"], dsem=f"xg{j}")
            for j in range(4):
                for hf in range(2):
                    b = cnt["y"] % 2
                    cnt["y"] += 1
                    for fc in range(8):
                        P.add("pe", lambda h, b=b, fc=fc, j=j, hf=hf: h.matmul(
                            pY[b][:], lhsT=mT[:, fc, j * 128:(j + 1) * 128], rhs=w_outs[:, fc, hf * 512:(hf + 1) * 512],
                            start=(fc == 0), stop=(fc == 7)), reads=["mT", "w_outs"], writes=[f"pY{b}"])
                    P.add("dve", lambda h, b=b, j=j, hf=hf: h.tensor_tensor(
                        out=xg[:, j, hf * 512:(hf + 1) * 512], in0=pY[b][:], in1=xg[:, j, hf * 512:(hf + 1) * 512],
                        op=ALU.add), reads=[f"pY{b}", f"xg{j}"], writes=[f"xg{j}"])
                rms_rstd(P, cx, xg[:, j, :], f"xg{j}", D, junk[:], stn[:, 2 * j:2 * j + 1], stn[:, 2 * j + 1:2 * j + 2], f"nn{j}")
                P.add("dve", lambda h, j=j: h.scalar_tensor_tensor(
                    out=hbn[j][:], in0=xg[:, j, :], scalar=stn[:, 2 * j + 1:2 * j + 2], op0=ALU.mult, in1=gxbc[:], op1=ALU.mult),
                    reads=[f"xg{j}", f"nn{j}_rstd", "gxbc"], writes=[f"hbn{j}"])
            for j in range(4):
                transpose8(lambda dc, j=j: hbn[j][:, dc * 128:(dc + 1) * 128], [f"hbn{j}"],
                           lambda j=j: hT[:, :, j * 128:(j + 1) * 128], "hT")
            for c in range(8):
                b = cnt["a"] % 2
                cnt["a"] += 1
                for dc in range(8):
                    P.add("pe", lambda h, b=b, c=c, dc=dc: h.matmul(
                        pA[b][:], lhsT=wqs[:, dc, c * 128:(c + 1) * 128], rhs=hT[:, dc, :],
                        start=(dc == 0), stop=(dc == 7)), reads=["wqs", "hT"], writes=[f"pA{b}"])
                P.add("act", lambda h, b=b, c=c: h.copy(out=qT[:, c, :], in_=pA[b][:]),
                      reads=[f"pA{b}"], writes=["qT"])
            def emit_scores(j):
                for hd in range(4):
                    for cc in range(2):
                        P.add("pe", lambda h, hd=hd, cc=cc, j=j: h.matmul(
                            pS[:, hd * 256:(hd + 1) * 256], lhsT=qT[:, 2 * hd + cc, j * 128:(j + 1) * 128],
                            rhs=kT[:, 2 * hd + cc, :], start=(cc == 0), stop=(cc == 1)),
                            reads=["qT", "kT"], writes=["pS"])

            emit_scores(0)
            for j in range(4):
                b = ti % 2
                ti += 1
                pnb = pn2[j % 2]
                mxs = st[b][:, 4:5]
                nb = st[b][:, 5:6]
                sm = st[b][:, 8:12]
                rs = st[b][:, 12:16]
                P.add("dve", lambda h, mxs=mxs: h.tensor_reduce(out=mxs, in_=pS[:], op=ALU.max, axis=AX.X),
                      reads=["pS"], writes=[f"mx{b}"])
                P.add("dve", lambda h, mxs=mxs, nb=nb: h.tensor_scalar(
                    out=nb, in0=mxs, scalar1=-SC, scalar2=None, op0=ALU.mult),
                    reads=[f"mx{b}"], writes=[f"nb{b}"])
                P.add("act", lambda h, nb=nb: h.activation(
                    out=pe_[:].rearrange("p a m -> p (a m)"), in_=pS[:], func=AF.Exp, bias=nb, scale=SC),
                    reads=["pS", f"nb{b}"], writes=["pexp"])
                if j + 1 < 4:
                    emit_scores(j + 1)
                P.add("dve", lambda h, sm=sm: h.tensor_reduce(out=sm, in_=pe_[:], op=ALU.add, axis=AX.X),
                      reads=["pexp"], writes=[f"sm{b}"])
                P.add("dve", lambda h, sm=sm, rs=rs: h.reciprocal(out=rs, in_=sm),
                      reads=[f"sm{b}"], writes=[f"rs{b}"])
                for hd in range(4):
                    P.add("dve", lambda h, hd=hd, rs=rs, pnb=pnb: h.tensor_scalar(
                        out=pnb[:, hd, :], in0=pe_[:, hd, :], scalar1=rs[:, hd:hd + 1], scalar2=None, op0=ALU.mult),
                        reads=["pexp", f"rs{b}"], writes=[f"pn{j % 2}"])
                transpose8(lambda c, pnb=pnb: pnb[:, c // 2, (c % 2) * 128:(c % 2 + 1) * 128], [f"pn{j % 2}"],
                           lambda j=j: ppT[:, :, j * 128:(j + 1) * 128], "ppT")
            for c in range(8):
                b = cnt["a"] % 2
                cnt["a"] += 1
                hd = c // 2
                for mc in range(2):
                    P.add("pe", lambda h, b=b, c=c, mc=mc, hd=hd: h.matmul(
                        pA[b][:], lhsT=vv[:, mc, c * 128:(c + 1) * 128], rhs=ppT[:, hd * 2 + mc, :],
                        start=(mc == 0), stop=(mc == 1)), reads=["vv", "ppT"], writes=[f"pA{b}"])
                P.add("act", lambda h, b=b, c=c: h.copy(out=oT[:, c, :], in_=pA[b][:]),
                      reads=[f"pA{b}"], writes=["oT"])
            if gi + 1 < NG:
                emit_mixed(gi + 1)
            for j in range(4):
                t0 = gi * 512 + j * 128
                ob = (gi * 4 + j) % 2
                for hf in range(2):
                    b = cnt["y"] % 2
                    cnt["y"] += 1
                    for c in range(8):
                        P.add("pe", lambda h, b=b, c=c, j=j, hf=hf: h.matmul(
                            pY[b][:], lhsT=oT[:, c, j * 128:(j + 1) * 128], rhs=wos[:, c, hf * 512:(hf + 1) * 512],
                            start=(c == 0), stop=(c == 7)), reads=["oT", "wos"], writes=[f"pY{b}"])
                    P.add("dve", lambda h, b=b, j=j, hf=hf, ob=ob: h.tensor_tensor(
                        out=yo[ob][:, hf * 512:(hf + 1) * 512], in0=pY[b][:], in1=xg[:, j, hf * 512:(hf + 1) * 512],
                        op=ALU.add), reads=[f"pY{b}", f"xg{j}"], writes=[f"yo{ob}"])
                P.add("sp", lambda h, ob=ob, t0=t0: h.dma_start(out=y[t0:t0 + 128, :], in_=yo[ob][:]),
                      reads=[f"yo{ob}"], writes=["ydram"], dsem=f"yst{ob}")
        if not fused:
            P.add("sp", lambda h: h.nop(), reads=[], writes=["yo0", "yo1"])


def build_xattn(T):
    nc = bass.Bass("TRN2", target_bir_lowering=False)
    di = lambda name, shape: nc.dram_tensor(name, list(shape), F32, kind="ExternalInput").ap()
    A = {"x": di("x", [T, D]), "mixed": di("mixed", [T, D]), "mem": di("mem", [256, D]), "gx": di("gx", [D]), "gm": di("gm", [D]),
         "w_out": di("w_out", [D, D]), "wq": di("wq", [D, D]), "wkv": di("wkv", [D, 2 * D]), "wo": di("wo", [D, D])}
    A["y"] = nc.dram_tensor("y", [T, D], F32, kind="ExternalOutput").ap()
    with ExitStack() as stack:
        P = Prog(nc, stack)
        cx = Ctx(nc, stack)
        ident = make_identity(P, cx)
        xattn_body(P, nc, cx, ident, T, A, False)
        P.emit()
    return nc


TWO_PI = 2.0 * math.pi
C1 = 6.28125
C2 = TWO_PI - C1
MAGIC = 12582912.0
GELU_C = 2.0 * math.sqrt(2.0 / math.pi)


def mixer_consts(hh):
    f = np.float32
    half = 48
    inv_freq = np.exp(-math.log(10000.0) * np.arange(half, dtype=np.float64) / half)
    invf = np.zeros((128, 1), f)
    for p in range(128):
        j = p % 64
        if j < half:
            invf[p, 0] = inv_freq[j]
    lg = np.log1p(-np.exp2(-5.0 - np.arange(4, dtype=np.float64)))
    hs = [2 * hh, 2 * hh + 1]
    idx = np.arange(128)
    same = (idx[:, None] // 64) == (idx[None, :] // 64)
    dist = np.abs(idx[:, None] - idx[None, :])
    dmask = np.zeros((128, 2, 128), f)
    for a, h in enumerate(hs):
        dmask[:, a, :] = np.where(same, np.exp(lg[h] * dist), 0.0) * (96 ** -0.5)
    t = np.arange(512)
    qin = np.zeros((128, 512), f)
    cd = np.zeros((128, 1), f)
    for p in range(128):
        h = hs[p // 64]
        qin[p] = np.exp(lg[h] * ((t % 64) + 1.0)) * (96 ** -0.5)
        cd[p, 0] = np.exp(lg[h] * 64.0)
    kout = np.zeros((128, 192), f)
    for a, h in enumerate(hs):
        kout[:, a * 96:(a + 1) * 96] = np.exp(lg[h] * (63.0 - (idx % 64)))[:, None]
    maskL = (same & (idx[None, :] >= idx[:, None])).astype(f)
    maskU = (same & (idx[None, :] < idx[:, None])).astype(f)
    reset = np.ones((128, 512), f)
    reset[:, ::64] = 0.0
    return {"c_invf": invf, "c_dmask": dmask, "c_qin": qin, "c_cd": cd, "c_kout": kout,
            "c_maskL": maskL, "c_maskU": maskU, "c_reset": reset}


NFM = 1280
NTM = 768


def mixer_body(P, nc, cx, ident, S, A, fused):
    x = A.get("x"); pos = A["pos"]; g = A.get("g"); wfm = A["wfm"]; wtm = A["wtm"]
    g_ret = A["g_ret"]; g_hg = A["g_hg"]; lbp = A["lbp"]; lflag = A["lflag"]; lru_p = A["lru_p"]
    wa_bd = A["wa_bd"]; wx_bd = A["wx_bd"]; cn = A["cn"]; mt = A.get("mt"); lruT = A.get("lruT")
    NB = S // 512
    if True:
        def const_tile(ap, shape, name, dt=F32):
            t = cx.sb(shape, dt, name)
            P.add("sp", lambda h: h.dma_start(out=t[:], in_=ap), writes=[name], dsem=name)
            return t

        gbc = None if fused else const_tile(bcast_rows(g, 128), [128, D], "gbc")
        gretbc = const_tile(bcast_rows(g_ret, 128), [128, 192], "gretbc")
        ghgbc = const_tile(bcast_rows(g_hg, 128), [128, 192], "ghgbc")
        invf = const_tile(cn["c_invf"], [128, 1], "invf")
        dmask = const_tile(cn["c_dmask"], [128, 2, 128], "dmask")
        qin = const_tile(cn["c_qin"], [128, 512], "qin")
        cdv = const_tile(cn["c_cd"], [128, 1], "cdv")
        kout = const_tile(cn["c_kout"], [128, 192], "kout")
        maskL = const_tile(cn["c_maskL"], [128, 128], "maskL")
        maskU = const_tile(cn["c_maskU"], [128, 128], "maskU")
        reset = const_tile(cn["c_reset"], [128, 512], "reset")
        lbp_t = const_tile(lbp, [128, 4], "lbp")
        lflag_t = const_tile(lflag, [128, 1], "lflag")
        lrup = const_tile(lru_p, [128, 8], "lrup")
        wfs = load_w_bf16(P, cx, wfm, D, NFM, "wfs")
        wts = load_w_bf16(P, cx, wtm, D, NTM, "wts")
        wab = cx.sb([128, 128], BF16, "wab")
        wxb = cx.sb([128, 128], BF16, "wxb")
        P.add("pool", lambda h: h.dma_start(out=wab[:], in_=wa_bd), writes=["wab"], dsem="wab")
        P.add("pool", lambda h: h.dma_start(out=wxb[:], in_=wx_bd), writes=["wxb"], dsem="wxb")

        sm = cx.sb([128, 16], F32, "smallp")
        for hd in range(2):
            P.add("dve", lambda h, hd=hd: h.tensor_tensor(out=sm[:, hd:hd + 1], in0=lbp_t[:, 2 * hd + 1:2 * hd + 2],
                                                           in1=lbp_t[:, 2 * hd:2 * hd + 1], op=ALU.subtract),
                  reads=["lbp"], writes=["sm_lb"])
        P.add("act", lambda h: h.activation(out=sm[:, 0:2], in_=sm[:, 0:2], func=AF.Sigmoid), reads=["sm_lb"], writes=["sm_lb"])
        P.add("dve", lambda h: h.tensor_scalar(out=sm[:, 0:2], in0=sm[:, 0:2], scalar1=lflag_t[:, 0:1], scalar2=None,
                                               op0=ALU.mult), reads=["sm_lb", "lflag"], writes=["sm_lb"])
        P.add("dve", lambda h: h.tensor_scalar(out=sm[:, 2:4], in0=sm[:, 0:2], scalar1=-1.0, scalar2=1.0,
                                               op0=ALU.mult, op1=ALU.add), reads=["sm_lb"], writes=["sm_oml"])
        P.add("act", lambda h: h.activation(out=sm[:, 4:5], in_=lrup[:, 7:8], func=AF.Exp, scale=-1.0),
              reads=["lrup"], writes=["sm_c"])
        P.add("dve", lambda h: h.tensor_scalar(out=sm[:, 4:5], in0=sm[:, 4:5], scalar1=1.0, scalar2=None, op0=ALU.add),
              reads=["sm_c"], writes=["sm_c"])
        P.add("act", lambda h: h.activation(out=sm[:, 4:5], in_=sm[:, 4:5], func=AF.Ln), reads=["sm_c"], writes=["sm_c"])
        P.add("dve", lambda h: h.tensor_scalar(out=sm[:, 4:5], in0=sm[:, 4:5], scalar1=-8.0, scalar2=None, op0=ALU.mult),
              reads=["sm_c"], writes=["sm_c"])

        if not fused:
            xin = [cx.sb([128, D], F32, "xin") for _ in range(2)]
            junk = cx.sb([128, D], BF16, "junk")
        else:
            mo4 = [cx.sb([128, 4, 512], BF16, f"mo4_{i}") for i in range(2)]
            lob = cx.sb([128, 512], BF16, "lob")
        st = [cx.sb([128, 4], F32, "st") for _ in range(2)]
        hb = [cx.sb([128, D], BF16, "hb") for _ in range(2)]
        hbq = [cx.sb([128, D], BF16, f"hbq{i}") for i in range(4)] if fused else None

        def emit_hload(Bx):
            for j in range(4):
                t0 = Bx * 512 + j * 128
                rk = t0 // (S // 2)
                ii_ = t0 % (S // 2)
                kk_ = ii_ // 1024
                row = rk * 1024 + ii_ % 1024
                P.add("sp", lambda h, j=j, kk_=kk_, row=row: h.dma_start(out=hbq[j][:], in_=A["Hg"][kk_].ap()[row:row + 128, :]),
                      reads=[f"Hg{kk_}"], writes=[f"hbq{j}"], dsem=f"hbld{j}")

        hT = cx.sb([128, 8, 512], BF16, "hT")
        pT = cx.ps([128, D], BF16, "pT")
        pF = [cx.ps([128, 512], F32, "pF") for _ in range(2)]
        pSr_ = cx.ps([128, 512], F32, "pSr")
        pSh_ = cx.ps([128, 512], F32, "pSh")
        pO_ = cx.ps([128, 512], F32, "pO")
        pKr_ = cx.ps([128, 512], F32, "pKr")
        pKh_ = cx.ps([128, 512], F32, "pKh")
        pSr = pSr_[:, 0:256].rearrange("p (a t) -> p a t", a=2)
        pSh = pSh_[:, 0:512].rearrange("p (a t) -> p a t", a=4)
        pO = pO_[:, 0:384].rearrange("p (a t) -> p a t", a=4)
        pKr = pKr_[:, 0:384].rearrange("p (a t) -> p a t", a=2)
        pKh = pKh_[:, 0:192].rearrange("p (a t) -> p a t", a=2)
        cnt = {"f": 0, "x": 0, "o": 0}
        P.excl.update(["pT", "pF0", "pF1", "pSr", "pSh", "pO", "pKr", "pKh"])
        fb = lambda name: cx.sb([128, 512], F32, name)
        bb = lambda name: cx.sb([128, 512], BF16, name)
        posi = cx.sb([128, 512], I32, "posi")
        ang = fb("ang"); angc = fb("angc"); tk = fb("tk"); r1 = fb("r1"); cosT = fb("cosT"); sinT = fb("sinT")
        raw = [fb(f"raw{i}") for i in range(4)]
        t1 = fb("t1"); t2 = fb("t2"); qrA = fb("qrA"); qrB = fb("qrB")
        t3 = fb("t3"); t4 = fb("t4")
        qsA = bb("qsA"); qsB = bb("qsB"); qiA = bb("qiA"); qiB = bb("qiB"); kbA = bb("kbA"); kbB = bb("kbB")
        qh = [fb(f"qh{i}") for i in range(2)]
        fg = [fb(f"fg{i}") for i in range(2)]
        lf = fb("lf"); kk = fb("kk"); cum = fb("cum"); cm = fb("cm"); eA = fb("eA"); eB = fb("eB")
        eE = [fb(f"eE{i}") for i in range(2)]
        eD = fb("eD")
        hqA = [bb(f"hqA{i}") for i in range(2)]; hqB = [bb(f"hqB{i}") for i in range(2)]
        hkA = [bb(f"hkA{i}") for i in range(2)]; hkB = [bb(f"hkB{i}") for i in range(2)]
        hqC = [bb(f"hqC{i}") for i in range(2)]; hkD = [bb(f"hkD{i}") for i in range(2)]
        lxb = cx.sb([128, 3 + 512], F32, "lxb")
        lgb = fb("lgb")
        xc = fb("xc"); xcb = bb("xcb"); rr = fb("rr"); ii = fb("ii"); aa = fb("aa"); a2 = fb("a2"); uu = fb("uu")
        hh_ = [fb(f"hscan{i}") for i in range(2)]
        gl = fb("gl"); lo = [fb(f"lo{i}") for i in range(2)]
        vb = [cx.sb([128, 192], BF16, f"vb{j}") for j in range(4)]
        vdec = [cx.sb([128, 192], BF16, f"vdec{j}") for j in range(4)]
        ib = [cx.sb([128, 192], BF16, f"ib{j}") for j in range(4)]
        gate = [cx.sb([128, 384], F32, f"gate{j}") for j in range(4)]
        ktok = [cx.sb([128, 2, 128], BF16, f"ktok{j}") for j in range(4)]
        kdtok = [cx.sb([128, 2, 128], BF16, f"kdtok{j}") for j in range(4)]
        scR = cx.sb([128, 2, 128], BF16, "scR")
        scH = cx.sb([128, 2, 128], BF16, "scH")
        mtmp = cx.sb([128, 4, 128], F32, "mtmp")
        SA = cx.sb([128, 2, 192], F32, "SA")
        SAb = [cx.sb([128, 2, 192], BF16, f"SAb{i}") for i in range(3)]
        SH = cx.sb([128, 2, 96], F32, "SH")
        SHb = [cx.sb([128, 2, 96], BF16, f"SHb{i}") for i in range(3)]
        gs = [cx.sb([128, 16], F32, f"gs{i}") for i in range(2)]
        sq = cx.sb([128, 384], F32, "sq")
        yn = cx.sb([128, 384], F32, "yn")
        mo = [cx.sb([128, 384], F32, f"mo{i}") for i in range(2)]
        P.add("dve", lambda h: h.memset(SA[:], 0.0), writes=["SA"])
        P.add("dve", lambda h: h.memset(SH[:], 0.0), writes=["SH"])
        P.add("pool", lambda h: h.memset(SAb[0][:], 0.0), writes=["SAb0"])
        P.add("pool", lambda h: h.memset(SHb[0][:], 0.0), writes=["SHb0"])
        P.add("pool", lambda h: h.memset(lxb[:, 0:3], 0.0), writes=["lxb"])

        def transpose_to(src_fn, n, srckeys, dst_ap_fn, dstkey, eng="act"):
            for i in range(n):
                P.add("pe", lambda h, i=i: h.transpose(out=pT[:, i * 128:(i + 1) * 128], in_=src_fn(i), identity=ident[:]),
                      reads=list(srckeys) + ["ident"], writes=["pT"])
            if eng == "act":
                P.add("act", lambda h: h.copy(out=dst_ap_fn(), in_=pT[:, 0:n * 128].rearrange("p (c t) -> p c t", c=n)),
                      reads=["pT"], writes=[dstkey])
            else:
                P.add("dve", lambda h: h.tensor_copy(out=dst_ap_fn(), in_=pT[:, 0:n * 128].rearrange("p (c t) -> p c t", c=n)),
                      reads=["pT"], writes=[dstkey])

        chunk_no = 0
        tile_no = 0
        def lru_gen(B):
            T0 = B * 512
            hb_i = B % 2
            P.add("dve", lambda h: h.tensor_scalar(out=xc[:], in0=lxb[:, 0:512], scalar1=lrup[:, 0:1], scalar2=lrup[:, 4:5], op0=ALU.mult, op1=ALU.add),
                  reads=["lxb", "lrup"], writes=["xc"])
            for w in range(1, 4):
                P.add("dve", lambda h, w=w: h.scalar_tensor_tensor(out=xc[:], in0=lxb[:, w:w + 512], scalar=lrup[:, w:w + 1], op0=ALU.mult, in1=xc[:], op1=ALU.add),
                      reads=["lxb", "lrup", "xc"], writes=["xc"])
            P.add("pool", lambda h: h.tensor_copy(out=lxb[:, 0:3], in_=lxb[:, 512:515]), reads=["lxb", "xc"], writes=["lxb"])
            P.add("pool", lambda h: h.tensor_tensor(out=gl[:], in0=lgb[:], in1=lgb[:], op=ALU.mult), reads=["lgb"], writes=["gl"])
            P.add("pool", lambda h: h.tensor_scalar(out=gl[:], in0=gl[:], scalar1=0.044715, scalar2=1.0, op0=ALU.mult, op1=ALU.add), reads=["gl"], writes=["gl"])
            P.add("pool", lambda h: h.tensor_tensor(out=gl[:], in0=gl[:], in1=lgb[:], op=ALU.mult), reads=["gl", "lgb"], writes=["gl"])
            yield
            P.add("act", lambda h: h.copy(out=xcb[:], in_=xc[:]), reads=["xc"], writes=["xcb"])
            for (wb, dst, bcol, nm) in ((wab, rr, 5, "rr"), (wxb, ii, 6, "ii")):
                b = cnt["f"] % 2
                cnt["f"] += 1
                P.add("pe", lambda h, b=b, wb=wb: h.matmul(pF[b][:], lhsT=wb[:], rhs=xcb[:], start=True, stop=True),
                      reads=["wab", "wxb", "xcb"], writes=[f"pF{b}"])
                P.add("act", lambda h, b=b, dst=dst, bcol=bcol: h.activation(out=dst[:], in_=pF[b][:], func=AF.Sigmoid, bias=lrup[:, bcol:bcol + 1]),
                      reads=[f"pF{b}", "lrup"], writes=[nm])
            P.add("act", lambda h: h.activation(out=aa[:], in_=rr[:], func=AF.Exp, scale=sm[:, 4:5]), reads=["rr", "sm_c"], writes=["aa"])
            P.add("pool", lambda h: h.tensor_tensor(out=a2[:], in0=aa[:], in1=aa[:], op=ALU.mult), reads=["aa"], writes=["a2"])
            P.add("pool", lambda h: h.tensor_scalar(out=a2[:], in0=a2[:], scalar1=-1.0, scalar2=1.0, op0=ALU.mult, op1=ALU.add), reads=["a2"], writes=["a2"])
            yield
            P.add("act", lambda h: h.activation(out=gl[:], in_=gl[:], func=AF.Sigmoid, scale=GELU_C), reads=["gl"], writes=["gl"])
            P.add("pool", lambda h: h.tensor_tensor(out=gl[:], in0=gl[:], in1=lgb[:], op=ALU.mult), reads=["gl", "lgb"], writes=["gl"])
            yield
            P.add("act", lambda h: h.activation(out=a2[:], in_=a2[:], func=AF.Sqrt), reads=["a2"], writes=["a2"])
            P.add("pool", lambda h: h.tensor_tensor(out=uu[:], in0=ii[:], in1=xc[:], op=ALU.mult), reads=["ii", "xc"], writes=["uu"])
            P.add("pool", lambda h: h.tensor_tensor(out=uu[:], in0=uu[:], in1=a2[:], op=ALU.mult), reads=["uu", "a2"], writes=["uu"])
            prev = hh_[1 - hb_i]
            init = 0.0 if B == 0 else prev[:, 511:512]
            P.add("dve", lambda h, hb_i=hb_i, init=init: h.tensor_tensor_scan(
                out=hh_[hb_i][:], data0=aa[:], data1=uu[:], initial=init, op0=ALU.mult, op1=ALU.add),
                reads=["aa", "uu", f"hs{1 - hb_i}"], writes=[f"hs{hb_i}"])
            P.add("pool", lambda h, hb_i=hb_i: h.tensor_tensor(out=lo[hb_i][:], in0=gl[:], in1=hh_[hb_i][:], op=ALU.mult),
                  reads=["gl", f"hs{hb_i}"], writes=[f"lo{hb_i}"])
            yield
            if not fused:
                P.add("sp", lambda h, hb_i=hb_i, T0=T0: h.dma_start(out=lruT[:, T0:T0 + 512], in_=lo[hb_i][:]),
                      reads=[f"lo{hb_i}"], writes=["lrudram"], dsem=f"lost{hb_i}")
            else:
                P.add("act", lambda h, hb_i=hb_i: h.copy(out=lob[:], in_=lo[hb_i][:]), reads=[f"lo{hb_i}"], writes=["lob"])
                transpose_to(lambda i: lob[:, i * 128:(i + 1) * 128], 4, ["lob"],
                             lambda B=B: mo4[B % 2][:, :, 384:512], f"mo4_{B % 2}", eng="dve")
                kM = T0 // 2048
                rM = T0 % 2048
                P.add("sp", lambda h, B=B, kM=kM, rM=rM: h.dma_start(
                    out=A["Mh"][kM].ap()[rM:rM + 512, :].rearrange("(j p) c -> p j c", p=128), in_=mo4[B % 2][:]),
                    reads=[f"mo4_{B % 2}"], writes=[f"Mh{kM}"], dsem=f"most{B % 2}")
                if rM + 512 == 2048:
                    P.add("pool", lambda h, kM=kM: h.collective_compute(
                        "AllGather", ALU.bypass, replica_groups=PAIRS, ins=[A["Mh"][kM].ap().opt()], outs=[A["Mg"][kM].ap().opt()]),
                        reads=[f"Mh{kM}"], writes=[f"Mg{kM}"], dsem="cc", inc=1)

        lru_state = {"g": None}

        def lru_step():
            if lru_state["g"] is not None:
                try:
                    next(lru_state["g"])
                except StopIteration:
                    lru_state["g"] = None

        if fused:
            emit_hload(0)
        P.mark('blocks')
        for B in range(NB):
            T0 = B * 512
            for j in range(4):
                t0 = T0 + j * 128
                b = cnt["x"] % 2
                cnt["x"] += 1
                if not fused:
                    P.add("sp", lambda h, b=b, t0=t0: h.dma_start(out=xin[b][:], in_=x[t0:t0 + 128, :]),
                          writes=[f"xin{b}"], dsem=f"xin{b}")
                    rms_rstd(P, cx, xin[b][:], f"xin{b}", D, junk[:], st[b][:, 0:1], st[b][:, 1:2], f"n{b}")
                    P.add("dve", lambda h, b=b: h.scalar_tensor_tensor(
                        out=hb[b][:], in0=xin[b][:], scalar=st[b][:, 1:2], op0=ALU.mult, in1=gbc[:], op1=ALU.mult),
                        reads=[f"xin{b}", f"n{b}_rstd", "gbc"], writes=[f"hb{b}"])
                if fused:
                    transpose_to(lambda i, j=j: hbq[j][:, i * 128:(i + 1) * 128], 8, [f"hbq{j}"],
                                 lambda j=j: hT[:, :, j * 128:(j + 1) * 128], "hT")
                else:
                    transpose_to(lambda i, b=b: hb[b][:, i * 128:(i + 1) * 128], 8, [f"hb{b}"],
                                 lambda j=j: hT[:, :, j * 128:(j + 1) * 128], "hT")
            if B > 0:
                lru_state["g"] = lru_gen(B - 1)
                lru_step()
            P.mark('rope_tables')
            P.add("sp", lambda h, T0=T0: h.dma_start(out=posi[:], in_=bcast_rows(pos[T0:T0 + 512], 128)),
                  writes=["posi"], dsem="posi")
            P.add("dve", lambda h: h.tensor_copy(out=ang[:], in_=posi[:]), reads=["posi"], writes=["ang"])
            P.add("dve", lambda h: h.tensor_scalar(out=ang[:], in0=ang[:], scalar1=invf[:, 0:1], scalar2=None, op0=ALU.mult),
                  reads=["ang", "invf"], writes=["ang"])
            P.add("dve", lambda h: h.tensor_scalar(out=angc[:], in0=ang[:], scalar1=math.pi / 2, scalar2=None, op0=ALU.add),
                  reads=["ang"], writes=["angc"])
            for (dst, src, skey, key) in ((sinT, ang, "ang", "sinT"), (cosT, angc, "angc", "cosT")):
                P.add("dve", lambda h, src=src: h.tensor_scalar(
                    out=tk[:], in0=src[:], scalar1=1.0 / TWO_PI, scalar2=MAGIC, op0=ALU.mult, op1=ALU.add),
                    reads=[skey], writes=["tk"])
                P.add("dve", lambda h: h.tensor_scalar(out=tk[:], in0=tk[:], scalar1=-MAGIC, scalar2=None, op0=ALU.add),
                      reads=["tk"], writes=["tk"])
                P.add("dve", lambda h, src=src: h.scalar_tensor_tensor(
                    out=r1[:], in0=tk[:], scalar=-C1, op0=ALU.mult, in1=src[:], op1=ALU.add),
                    reads=["tk", skey], writes=["r1"])
                P.add("dve", lambda h: h.scalar_tensor_tensor(
                    out=r1[:], in0=tk[:], scalar=-C2, op0=ALU.mult, in1=r1[:], op1=ALU.add),
                    reads=["tk", "r1"], writes=["r1"])
                P.add("dve", lambda h: h.tensor_scalar(
                    out=r1[:], in0=r1[:], scalar1=math.pi, scalar2=-math.pi, op0=ALU.min, op1=ALU.max),
                    reads=["r1"], writes=["r1"])
                P.add("act", lambda h, dst=dst: h.activation(out=dst[:], in_=r1[:], func=AF.Sin),
                      reads=["r1"], writes=[key])
            P.mark('fm_proj')
            for c in range(10):
                b = cnt["f"] % 2
                cnt["f"] += 1
                for dc in range(8):
                    P.add("pe", lambda h, b=b, c=c, dc=dc: h.matmul(
                        pF[b][:], lhsT=wfs[:, dc, c * 128:(c + 1) * 128], rhs=hT[:, dc, :],
                        start=(dc == 0), stop=(dc == 7)), reads=["wfs", "hT"], writes=[f"pF{b}"])
                if c < 4:
                    P.add("act", lambda h, b=b, c=c: h.copy(out=raw[c][:], in_=pF[b][:]), reads=[f"pF{b}"], writes=[f"raw{c}"])
                elif c < 6:
                    P.add("act", lambda h, b=b, c=c: h.activation(out=qh[c - 4][:], in_=pF[b][:], func=AF.Silu),
                          reads=[f"pF{b}"], writes=[f"qh{c - 4}"])
                elif c < 8:
                    P.add("act", lambda h, b=b, c=c: h.activation(out=fg[c - 6][:], in_=pF[b][:], func=AF.Sigmoid),
                          reads=[f"pF{b}"], writes=[f"fg{c - 6}"])
                elif c == 8:
                    P.add("act", lambda h, b=b: h.copy(out=lxb[:, 3:515], in_=pF[b][:]), reads=[f"pF{b}"], writes=["lxb"])
                else:
                    P.add("act", lambda h, b=b: h.copy(out=lgb[:], in_=pF[b][:]), reads=[f"pF{b}"], writes=["lgb"])
                if c in (1, 7, 8):
                    lru_step()
            P.mark('tm_proj')
            for j in range(4):
                for gidx in range(2):
                    b = cnt["f"] % 2
                    cnt["f"] += 1
                    for dc in range(8):
                        P.add("pe", lambda h, b=b, j=j, gidx=gidx, dc=dc: h.matmul(
                            pF[b][:, 0:384], lhsT=hT[:, dc, j * 128:(j + 1) * 128], rhs=wts[:, dc, gidx * 384:(gidx + 1) * 384],
                            start=(dc == 0), stop=(dc == 7)), reads=["wts", "hT"], writes=[f"pF{b}"])
                    if gidx == 0:
                        P.add("act", lambda h, b=b, j=j: h.copy(out=vb[j][:], in_=pF[b][:, 0:192]),
                              reads=[f"pF{b}"], writes=[f"vb{j}"])
                        P.add("dve", lambda h, b=b, j=j: h.tensor_tensor(out=vdec[j][:], in0=pF[b][:, 0:192], in1=kout[:], op=ALU.mult),
                              reads=[f"pF{b}", "kout"], writes=[f"vdec{j}"])
                        P.add("act", lambda h, b=b, j=j: h.copy(out=ib[j][:], in_=pF[b][:, 192:384]),
                              reads=[f"pF{b}"], writes=[f"ib{j}"])
                    else:
                        P.add("act", lambda h, b=b, j=j: h.activation(out=gate[j][:], in_=pF[b][:, 0:384], func=AF.Silu),
                              reads=[f"pF{b}"], writes=[f"gate{j}"])
                        if j == 1:
                            lru_step()
            if fused and B + 1 < NB:
                emit_hload(B + 1)
            P.mark('rope')
            def rope(eng, a, bq, oA, oB, ta, tb, keys_out):
                E = lambda fn, reads, writes: P.add(eng, fn, reads=reads, writes=writes)
                E(lambda h: h.tensor_tensor(out=ta[:], in0=raw[a][:], in1=cosT[:], op=ALU.mult), [f"raw{a}", "cosT"], [keys_out + "ta"])
                E(lambda h: h.tensor_tensor(out=tb[:], in0=raw[bq][:], in1=sinT[:], op=ALU.mult), [f"raw{bq}", "sinT"], [keys_out + "tb"])
                E(lambda h: h.tensor_tensor(out=oA[:], in0=ta[:], in1=tb[:], op=ALU.subtract), [keys_out + "ta", keys_out + "tb"], [keys_out + "A"])
                E(lambda h: h.tensor_tensor(out=ta[:], in0=raw[a][:], in1=sinT[:], op=ALU.mult), [f"raw{a}", "sinT"], [keys_out + "ta"])
                E(lambda h: h.tensor_tensor(out=tb[:], in0=raw[bq][:], in1=cosT[:], op=ALU.mult), [f"raw{bq}", "cosT"], [keys_out + "tb"])
                E(lambda h: h.tensor_tensor(out=oB[:], in0=ta[:], in1=tb[:], op=ALU.add), [keys_out + "ta", keys_out + "tb"], [keys_out + "B"])
            rope("dve", 0, 1, qrA, qrB, t1, t2, "qr")
            P.add("dve", lambda h: h.tensor_copy(out=qsA[:], in_=qrA[:]), reads=["qrA"], writes=["qsA"])
            P.add("dve", lambda h: h.tensor_copy(out=qsB[:], in_=qrB[:]), reads=["qrB"], writes=["qsB"])
            P.add("dve", lambda h: h.tensor_tensor(out=qiA[:], in0=qrA[:], in1=qin[:], op=ALU.mult), reads=["qrA", "qin"], writes=["qiA"])
            P.add("dve", lambda h: h.tensor_tensor(out=qiB[:], in0=qrB[:], in1=qin[:], op=ALU.mult), reads=["qrB", "qin"], writes=["qiB"])
            rope("dve", 2, 3, kbA, kbB, t3, t4, "kb")
            for j in range(4):
                transpose_to(lambda i, j=j: (kbA if i == 0 else kbB)[:, j * 128:(j + 1) * 128], 2, ["kbA", "kbB"],
                             lambda j=j: ktok[j][:], f"ktok{j}", eng="dve")
            P.mark('hg_elem')
            for hd in range(2):
                P.add("dve", lambda h, hd=hd: h.tensor_scalar(
                    out=fg[hd][:], in0=fg[hd][:], scalar1=sm[:, 2 + hd:3 + hd], scalar2=sm[:, hd:hd + 1], op0=ALU.mult, op1=ALU.add),
                    reads=[f"fg{hd}", "sm_lb", "sm_oml"], writes=[f"fg{hd}"])
                P.add("act", lambda h, hd=hd: h.activation(out=lf[:], in_=fg[hd][:], func=AF.Ln), reads=[f"fg{hd}"], writes=["lf"])
                P.add("dve", lambda h, hd=hd: h.tensor_scalar(out=kk[:], in0=fg[hd][:], scalar1=-1.0, scalar2=1.0, op0=ALU.mult, op1=ALU.add),
                      reads=[f"fg{hd}"], writes=["kk"])
                P.add("dve", lambda h: h.tensor_tensor_scan(out=cum[:], data0=reset[:], data1=lf[:], initial=0.0, op0=ALU.mult, op1=ALU.add),
                      reads=["reset", "lf"], writes=["cum"])
                c3 = cum[:].rearrange("p (c t) -> p c t", t=64)
                mid_b = bass.AP(cum[:].tensor, cum[:, 32:33].offset, [list(cum[:].ap[0]), [64, 8], [0, 64]])
                last_b = bass.AP(cum[:].tensor, cum[:, 63:64].offset, [list(cum[:].ap[0]), [64, 8], [0, 64]])
                P.add("dve", lambda h, c3=c3, mid_b=mid_b: h.tensor_tensor(
                    out=cm[:].rearrange("p (c t) -> p c t", t=64), in0=c3, in1=mid_b, op=ALU.subtract),
                    reads=["cum"], writes=["cm"])
                P.add("act", lambda h: h.activation(out=eA[:], in_=cm[:], func=AF.Exp), reads=["cm"], writes=["eA"])
                P.add("act", lambda h: h.activation(out=eB[:], in_=cm[:], func=AF.Exp, scale=-1.0), reads=["cm"], writes=["eB"])
                P.add("act", lambda h, hd=hd: h.activation(out=eE[hd][:], in_=cum[:], func=AF.Exp), reads=["cum"], writes=[f"eE{hd}"])
                P.add("dve", lambda h, c3=c3, last_b=last_b: h.tensor_tensor(
                    out=cm[:].rearrange("p (c t) -> p c t", t=64), in0=c3, in1=last_b, op=ALU.subtract),
                    reads=["cum", "eA", "eB"], writes=["cm"])
                P.add("act", lambda h: h.activation(out=eD[:], in_=cm[:], func=AF.Exp, scale=-1.0), reads=["cm"], writes=["eD"])
                P.add("dve", lambda h, hd=hd: h.tensor_tensor(out=hqA[hd][:], in0=qh[hd][:], in1=eA[:], op=ALU.mult), reads=[f"qh{hd}", "eA"], writes=[f"hqA{hd}"])
                P.add("dve", lambda h, hd=hd: h.tensor_tensor(out=hqB[hd][:], in0=qh[hd][:], in1=eB[:], op=ALU.mult), reads=[f"qh{hd}", "eB"], writes=[f"hqB{hd}"])
                P.add("dve", lambda h, hd=hd: h.tensor_tensor(out=hkA[hd][:], in0=kk[:], in1=eA[:], op=ALU.mult), reads=["kk", "eA"], writes=[f"hkA{hd}"])
                P.add("dve", lambda h, hd=hd: h.tensor_tensor(out=hkB[hd][:], in0=kk[:], in1=eB[:], op=ALU.mult), reads=["kk", "eB"], writes=[f"hkB{hd}"])
                P.add("dve", lambda h, hd=hd: h.tensor_tensor(out=hqC[hd][:], in0=qh[hd][:], in1=eE[hd][:], op=ALU.mult), reads=[f"qh{hd}", f"eE{hd}"], writes=[f"hqC{hd}"])
                P.add("dve", lambda h, hd=hd: h.tensor_tensor(out=hkD[hd][:], in0=kk[:], in1=eD[:], op=ALU.mult), reads=["kk", "eD"], writes=[f"hkD{hd}"])
            for j in range(4):
                transpose_to(lambda i, j=j: hkD[i][:, j * 128:(j + 1) * 128], 2, ["hkD0", "hkD1"],
                             lambda j=j: kdtok[j][:], f"kdtok{j}", eng="dve")
            P.mark('tiles')
            for j in range(4):
                tsl = slice(j * 128, (j + 1) * 128)
                t0 = T0 + j * 128
                prev_mm = ()
                for a in range(2):
                    psl = slice(a * 64, (a + 1) * 64)
                    P.add("pe", lambda h, a=a, psl=psl, tsl=tsl: h.matmul(pSr[:, a, :], lhsT=kbA[psl, tsl], rhs=qsA[psl, tsl], start=True, stop=False),
                          reads=["kbA", "qsA"], writes=["pSr"], after=prev_mm)
                    o2 = P.add("pe", lambda h, a=a, psl=psl, tsl=tsl: h.matmul(pSr[:, a, :], lhsT=kbB[psl, tsl], rhs=qsB[psl, tsl], start=False, stop=True),
                               reads=["kbB", "qsB"], writes=["pSr"])
                    prev_mm = (o2,)
                P.add("dve", lambda h: h.tensor_tensor(out=scR[:], in0=pSr, in1=dmask[:], op=ALU.mult),
                      reads=["pSr", "dmask"], writes=["scR"])
                for hd in range(2):
                    P.add("pe", lambda h, hd=hd, tsl=tsl: h.matmul(pSh[:, 2 * hd, :], lhsT=hkB[hd][:, tsl], rhs=hqA[hd][:, tsl], start=True, stop=True),
                          reads=[f"hkB{hd}", f"hqA{hd}"], writes=["pSh"])
                    P.add("pe", lambda h, hd=hd, tsl=tsl: h.matmul(pSh[:, 2 * hd + 1, :], lhsT=hkA[hd][:, tsl], rhs=hqB[hd][:, tsl], start=True, stop=True),
                          reads=[f"hkA{hd}", f"hqB{hd}"], writes=["pSh"])
                for hd in range(2):
                    P.add("dve", lambda h, hd=hd: h.tensor_tensor(out=mtmp[:, 2 * hd, :], in0=pSh[:, 2 * hd, :], in1=maskL[:], op=ALU.mult),
                          reads=["pSh", "maskL"], writes=["mtmp"])
                    P.add("dve", lambda h, hd=hd: h.tensor_tensor(out=mtmp[:, 2 * hd + 1, :], in0=pSh[:, 2 * hd + 1, :], in1=maskU[:], op=ALU.mult),
                          reads=["pSh", "maskU"], writes=["mtmp"])
                    P.add("pool", lambda h, hd=hd: h.tensor_tensor(out=scH[:, hd, :], in0=mtmp[:, 2 * hd, :], in1=mtmp[:, 2 * hd + 1, :], op=ALU.add),
                          reads=["mtmp"], writes=["scH"])
                for ci in range(2):
                    n = chunk_no + ci
                    cur, nxt = n % 3, (n + 1) % 3
                    csl = slice(ci * 64, (ci + 1) * 64)
                    ctok = slice(j * 128 + ci * 64, j * 128 + (ci + 1) * 64)
                    if ci == 0:
                        pass
                    P.add("pe", lambda h, csl=csl, j=j: h.matmul(pKr[:, 0, :], lhsT=ktok[j][csl, 0, :], rhs=vdec[j][csl, :], start=True, stop=True),
                          reads=[f"ktok{j}", f"vdec{j}"], writes=["pKr"])
                    P.add("pe", lambda h, csl=csl, j=j: h.matmul(pKr[:, 1, :], lhsT=ktok[j][csl, 1, :], rhs=vdec[j][csl, :], start=True, stop=True),
                          reads=[f"ktok{j}", f"vdec{j}"], writes=["pKr"])
                    for hd in range(2):
                        P.add("pe", lambda h, csl=csl, j=j, hd=hd: h.matmul(pKh[:, hd, :], lhsT=kdtok[j][csl, hd, :], rhs=ib[j][csl, hd * 96:(hd + 1) * 96], start=True, stop=True),
                              reads=[f"kdtok{j}", f"ib{j}"], writes=["pKh"])
                    P.add("dve", lambda h: h.scalar_tensor_tensor(out=SA[:], in0=SA[:], scalar=cdv[:, 0:1], op0=ALU.mult, in1=pKr, op1=ALU.add),
                          reads=["SA", "cdv", "pKr"], writes=["SA"])
                    P.add("act", lambda h, nxt=nxt: h.copy(out=SAb[nxt][:], in_=SA[:]), reads=["SA"], writes=[f"SAb{nxt}"])
                    for hd in range(2):
                        lastcol = j * 128 + ci * 64 + 63
                        P.add("dve", lambda h, hd=hd, lastcol=lastcol: h.scalar_tensor_tensor(
                            out=SH[:, hd, :], in0=SH[:, hd, :], scalar=eE[hd][:, lastcol:lastcol + 1], op0=ALU.mult, in1=pKh[:, hd, :], op1=ALU.add),
                            reads=["SH", f"eE{hd}", "pKh"], writes=["SH"])
                    P.add("act", lambda h, nxt=nxt: h.copy(out=SHb[nxt][:], in_=SH[:]), reads=["SH"], writes=[f"SHb{nxt}"])
                for a in range(2):
                    P.add("pe", lambda h, a=a, j=j: h.matmul(pO[:, a, :], lhsT=scR[:, a, :], rhs=vb[j][:, a * 96:(a + 1) * 96], start=True, stop=False),
                          reads=["scR", f"vb{j}"], writes=["pO"])
                    for ci in range(2):
                        n = chunk_no + ci
                        cur = n % 3
                        psl = slice(a * 64, (a + 1) * 64)
                        ctok = slice(j * 128 + ci * 64, j * 128 + (ci + 1) * 64)
                        osl = slice(ci * 64, (ci + 1) * 64)
                        last = (ci == 1)
                        P.add("pe", lambda h, a=a, cur=cur, psl=psl, ctok=ctok, osl=osl: h.matmul(
                            pO[osl, a, :], lhsT=qiA[psl, ctok], rhs=SAb[cur][psl, 0, a * 96:(a + 1) * 96], start=False, stop=False),
                            reads=["qiA", f"SAb{cur}"], writes=["pO"])
                        P.add("pe", lambda h, a=a, cur=cur, psl=psl, ctok=ctok, osl=osl, last=last: h.matmul(
                            pO[osl, a, :], lhsT=qiB[psl, ctok], rhs=SAb[cur][psl, 1, a * 96:(a + 1) * 96], start=False, stop=last),
                            reads=["qiB", f"SAb{cur}"], writes=["pO"])
                for hd in range(2):
                    P.add("pe", lambda h, hd=hd, j=j: h.matmul(pO[:, 2 + hd, :], lhsT=scH[:, hd, :], rhs=ib[j][:, hd * 96:(hd + 1) * 96], start=True, stop=False),
                          reads=["scH", f"ib{j}"], writes=["pO"])
                    for ci in range(2):
                        n = chunk_no + ci
                        cur = n % 3
                        ctok = slice(j * 128 + ci * 64, j * 128 + (ci + 1) * 64)
                        osl = slice(ci * 64, (ci + 1) * 64)
                        P.add("pe", lambda h, hd=hd, cur=cur, ctok=ctok, osl=osl, ci=ci: h.matmul(
                            pO[osl, 2 + hd, :], lhsT=hqC[hd][:, ctok], rhs=SHb[cur][:, hd, :], start=False, stop=(ci == 1)),
                            reads=[f"hqC{hd}", f"SHb{cur}"], writes=["pO"])
                chunk_no += 2
                ob = tile_no % 2
                tile_no += 1
                G = gs[ob]
                P.add("dve", lambda h, G=G: h.tensor_reduce(out=G[:, 0:2], in_=pO[:, 0:2, :], op=ALU.add, axis=AX.X),
                      reads=["pO"], writes=[f"G{ob}"])
                P.add("act", lambda h: h.activation(out=sq[:], in_=pO.rearrange("p a e -> p (a e)"), func=AF.Square),
                      reads=["pO"], writes=["sq"])
                P.add("dve", lambda h, G=G: h.tensor_reduce(out=G[:, 4:8], in_=sq[:].rearrange("p (a e) -> p a e", a=4), op=ALU.add, axis=AX.X),
                      reads=["sq"], writes=[f"G{ob}"])
                P.add("dve", lambda h, G=G: h.tensor_scalar(out=G[:, 0:2], in0=G[:, 0:2], scalar1=1.0 / 96, scalar2=None, op0=ALU.mult),
                      reads=[f"G{ob}"], writes=[f"G{ob}"])
                P.add("dve", lambda h, G=G: h.tensor_tensor(out=G[:, 2:4], in0=G[:, 0:2], in1=G[:, 0:2], op=ALU.mult),
                      reads=[f"G{ob}"], writes=[f"G{ob}"])
                P.add("dve", lambda h, G=G: h.tensor_scalar(out=G[:, 4:8], in0=G[:, 4:8], scalar1=1.0 / 96, scalar2=EPS, op0=ALU.mult, op1=ALU.add),
                      reads=[f"G{ob}"], writes=[f"G{ob}"])
                P.add("dve", lambda h, G=G: h.tensor_tensor(out=G[:, 4:6], in0=G[:, 4:6], in1=G[:, 2:4], op=ALU.subtract),
                      reads=[f"G{ob}"], writes=[f"G{ob}"])
                P.add("act", lambda h, G=G: h.activation(out=G[:, 4:8], in_=G[:, 4:8], func=AF.Sqrt), reads=[f"G{ob}"], writes=[f"G{ob}"])
                P.add("dve", lambda h, G=G: h.reciprocal(out=G[:, 8:12], in_=G[:, 4:8]), reads=[f"G{ob}"], writes=[f"G{ob}"])
                for a in range(2):
                    P.add("dve", lambda h, G=G, a=a: h.tensor_scalar(
                        out=yn[:, a * 96:(a + 1) * 96], in0=pO[:, a, :], scalar1=G[:, a:a + 1], scalar2=G[:, 8 + a:9 + a],
                        op0=ALU.subtract, op1=ALU.mult), reads=["pO", f"G{ob}"], writes=["yn"])
                for hd in range(2):
                    P.add("dve", lambda h, G=G, hd=hd: h.tensor_scalar(
                        out=yn[:, 192 + hd * 96:192 + (hd + 1) * 96], in0=pO[:, 2 + hd, :], scalar1=G[:, 10 + hd:11 + hd], scalar2=None,
                        op0=ALU.mult), reads=["pO", f"G{ob}"], writes=["yn"])
                P.add("pool", lambda h: h.tensor_tensor(out=yn[:, 0:192], in0=yn[:, 0:192], in1=gretbc[:], op=ALU.mult),
                      reads=["yn", "gretbc"], writes=["yn"])
                P.add("pool", lambda h: h.tensor_tensor(out=yn[:, 192:384], in0=yn[:, 192:384], in1=ghgbc[:], op=ALU.mult),
                      reads=["yn", "ghgbc"], writes=["yn"])
                if not fused:
                    P.add("pool", lambda h, ob=ob, j=j: h.tensor_tensor(out=mo[ob][:], in0=yn[:], in1=gate[j][:], op=ALU.mult),
                          reads=["yn", f"gate{j}"], writes=[f"mo{ob}"])
                    P.add("sp", lambda h, ob=ob, t0=t0: h.dma_start(out=mt[t0:t0 + 128, :], in_=mo[ob][:]),
                          reads=[f"mo{ob}"], writes=["mtdram"], dsem=f"most{ob}")
                else:
                    P.add("pool", lambda h, j=j, B=B: h.tensor_tensor(out=mo4[B % 2][:, j, 0:384], in0=yn[:], in1=gate[j][:], op=ALU.mult),
                          reads=["yn", f"gate{j}"], writes=[f"mo4_{B % 2}"])
        for _ in lru_gen(NB - 1):
            pass
        if not fused:
            P.add("sp", lambda h: h.nop(), reads=[], writes=["mo0", "mo1", "lo0", "lo1"])


def build_mixer(S):
    nc = bass.Bass("TRN2", target_bir_lowering=False)
    dt_in = lambda name, shape, dt=F32: nc.dram_tensor(name, list(shape), dt, kind="ExternalInput").ap()
    A = {"x": dt_in("x", [S, D]), "pos": dt_in("pos", [S], I32), "g": dt_in("g", [D]), "wfm": dt_in("wfm", [D, NFM]),
         "wtm": dt_in("wtm", [D, NTM]), "g_ret": dt_in("g_ret", [192]), "g_hg": dt_in("g_hg", [192]),
         "lbp": dt_in("lbp", [128, 4]), "lflag": dt_in("lflag", [128, 1]), "lru_p": dt_in("lru_p", [128, 8]),
         "wa_bd": dt_in("wa_bd", [128, 128]), "wx_bd": dt_in("wx_bd", [128, 128])}
    A["cn"] = {k: dt_in(k, v.shape) for k, v in mixer_consts(0).items()}
    A["mt"] = nc.dram_tensor("mt", [S, 384], F32, kind="ExternalOutput").ap()
    A["lruT"] = nc.dram_tensor("lruT", [128, S], F32, kind="ExternalOutput").ap()
    with ExitStack() as stack:
        P = Prog(nc, stack)
        cx = Ctx(nc, stack)
        ident = make_identity(P, cx)
        mixer_body(P, nc, cx, ident, S, A, False)
        P.emit()
    return nc


def mixer_pack(inp, l, hh):
    f = np.float32
    w_in = inp["w_in"][l]
    o_rq, o_rk, o_rv, o_rg, o_hq, o_hf, o_hi, o_hg, o_lx, o_lg = 0, 384, 768, 1152, 1536, 2048, 2560, 2944, 3328, 3584
    wfm = np.zeros((D, NFM), f)
    for a in range(2):
        h = 2 * hh + a
        for (base, cA, cB) in ((o_rq, 0, 1), (o_rk, 2, 3)):
            wfm[:, cA * 128 + a * 64: cA * 128 + a * 64 + 48] = w_in[:, base + h * 96: base + h * 96 + 48]
            wfm[:, cB * 128 + a * 64: cB * 128 + a * 64 + 48] = w_in[:, base + h * 96 + 48: base + h * 96 + 96]
        wfm[:, (4 + a) * 128:(5 + a) * 128] = w_in[:, o_hq + h * 128: o_hq + (h + 1) * 128]
        wfm[:, (6 + a) * 128:(7 + a) * 128] = w_in[:, o_hf + h * 128: o_hf + (h + 1) * 128]
    wfm[:, 8 * 128:9 * 128] = w_in[:, o_lx + hh * 128: o_lx + (hh + 1) * 128]
    wfm[:, 9 * 128:10 * 128] = w_in[:, o_lg + hh * 128: o_lg + (hh + 1) * 128]
    wtm = np.concatenate([w_in[:, o_rv + hh * 192: o_rv + (hh + 1) * 192], w_in[:, o_hi + hh * 192: o_hi + (hh + 1) * 192],
                          w_in[:, o_rg + hh * 192: o_rg + (hh + 1) * 192], w_in[:, o_hg + hh * 192: o_hg + (hh + 1) * 192]], axis=1)
    lbp = np.zeros((128, 4), f)
    for a in range(2):
        h = 2 * hh + a
        lbp[:, 2 * a] = inp["hg_lb_param"][0, h * 128:(h + 1) * 128]
        lbp[:, 2 * a + 1] = inp["hg_lb_param"][1, h * 128:(h + 1) * 128]
    lflag = np.full((128, 1), float(l), f)
    ch = slice(hh * 128, (hh + 1) * 128)
    lru_p = np.zeros((128, 8), f)
    lru_p[:, 0:4] = inp["lru_conv_w"][l][:, ch].T
    lru_p[:, 4] = inp["lru_conv_b"][l][ch]
    lru_p[:, 5] = inp["lru_ba"][l][ch]
    lru_p[:, 6] = inp["lru_bx"][l][ch]
    lru_p[:, 7] = inp["lru_lambda"][l][ch]
    wa_bd = np.zeros((128, 128), f)
    wx_bd = np.zeros((128, 128), f)
    for a in range(2):
        wa_bd[a * 64:(a + 1) * 64, a * 64:(a + 1) * 64] = inp["lru_wa"][l][2 * hh + a]
        wx_bd[a * 64:(a + 1) * 64, a * 64:(a + 1) * 64] = inp["lru_wx"][l][2 * hh + a]
    d = {"g": inp["mix_norm_g"][l], "wfm": wfm, "wtm": np.ascontiguousarray(wtm),
         "g_ret": np.ascontiguousarray(inp["ret_gn_g"][l][hh * 192:(hh + 1) * 192]),
         "g_hg": np.ascontiguousarray(inp["hg_norm_g"][l][hh * 192:(hh + 1) * 192]),
         "lbp": lbp, "lflag": lflag, "lru_p": lru_p, "wa_bd": wa_bd, "wx_bd": wx_bd}
    d.update(mixer_consts(hh))
    return d


def mixer_unpack(res_pair):
    r0, r1 = res_pair
    ret = np.concatenate([r0["mt"][:, 0:192], r1["mt"][:, 0:192]], axis=1)
    hg = np.concatenate([r0["mt"][:, 192:384], r1["mt"][:, 192:384]], axis=1)
    lru = np.concatenate([r0["lruT"].T, r1["lruT"].T], axis=1)
    return np.ascontiguousarray(np.concatenate([ret, hg, lru], axis=1))


_PROGS = {}


def _prog(name, builder):
    if name not in _PROGS:
        _PROGS[name] = builder()
    return _PROGS[name]


def _run(nc, in_maps):
    res = run_bass_kernel_spmd(nc, in_maps, core_ids=list(range(8)))
    return res.results


def kernel_unfused(**inp):
    inp = {k: np.asarray(v) for k, v in inp.items()}
    Bn, S, _ = inp["x"].shape
    NCORES = 8
    T = Bn * S // NCORES
    depth = inp["w_in"].shape[0]
    c32 = lambda a: np.ascontiguousarray(a, dtype=np.float32)
    x = c32(inp["x"]).reshape(NCORES, T, D)
    pos = np.ascontiguousarray(inp["positions"].astype(np.int32))
    for l in range(depth):
        nc = _prog("ffn", lambda: build_ffn(T, False))
        maps = [{"x": x[c], "g": c32(inp["ffn1_norm_g"][l]), "w1": c32(inp["ffn1_w1"][l]), "w3": c32(inp["ffn1_w3"][l]),
                 "w2": c32(inp["ffn1_w2"][l])} for c in range(NCORES)]
        r = _run(nc, maps)
        x = np.stack([r[c]["y"] for c in range(NCORES)], 0)
        nc = _prog("mixer", lambda: build_mixer(S))
        xfull = x.reshape(Bn, S, D)
        maps = []
        for b in range(Bn):
            for hh in range(2):
                d = mixer_pack(inp, l, hh)
                d["x"] = xfull[b]
                d["pos"] = pos[b]
                maps.append(d)
        r = _run(nc, maps)
        mixed = np.stack([mixer_unpack((r[2 * b], r[2 * b + 1])) for b in range(Bn)], 0).reshape(NCORES, T, D)
        nc = _prog("xattn", lambda: build_xattn(T))
        maps = [{"x": x[c], "mixed": mixed[c], "mem": c32(inp["mem"][c // 2]), "gx": c32(inp["xattn_norm_g"][l]),
                 "gm": c32(inp["xattn_mem_g"][l]), "w_out": c32(inp["w_out"][l]), "wq": c32(inp["xattn_wq"][l]),
                 "wkv": c32(inp["xattn_wkv"][l]), "wo": c32(inp["xattn_wo"][l])} for c in range(NCORES)]
        r = _run(nc, maps)
        x = np.stack([r[c]["y"] for c in range(NCORES)], 0)
        last = (l == depth - 1)
        nc = _prog("ffn_fin", lambda: build_ffn(T, True)) if last else _prog("ffn", lambda: build_ffn(T, False))
        maps = [{"x": x[c], "g": c32(inp["ffn2_norm_g"][l]), "w1": c32(inp["ffn2_w1"][l]), "w3": c32(inp["ffn2_w3"][l]),
                 "w2": c32(inp["ffn2_w2"][l])} for c in range(NCORES)]
        if last:
            for m in maps:
                m["gf"] = c32(inp["final_norm_g"])
        r = _run(nc, maps)
        x = np.stack([r[c]["y"] for c in range(NCORES)], 0)
    return x.reshape(Bn, S, D).astype(np.float32)


PAIRS = [[0, 1], [2, 3], [4, 5], [6, 7]]
_UID = [0]


class SCtx(Ctx):
    def sb(self, shape, dt, name=None):
        _UID[0] += 1
        return self.stack.enter_context(self.nc.sbuf_tensor(f"{name or 't'}_u{_UID[0]}", list(shape), dt))

    def ps(self, shape, dt, name=None):
        _UID[0] += 1
        return self.stack.enter_context(self.nc.psum_tensor(f"{name or 'p'}_u{_UID[0]}", list(shape), dt))


def stage_ffn(P, nc, ident, T, src, dst, g, w1, w3, w2, mode, gx=None, Hs=None, Hg=None):
    NG = T // 512
    with ExitStack() as sst:
        cx = SCtx(nc, sst)
        gbc = cx.sb([128, D], F32, "gbc")
        P.add("sp", lambda h: h.dma_start(out=gbc[:], in_=bcast_rows(g, 128)), writes=["gbc"], dsem="gbc")
        if mode != "plain":
            gxbc = cx.sb([128, D], F32, "gxbc")
            P.add("sp", lambda h: h.dma_start(out=gxbc[:], in_=bcast_rows(gx, 128)), writes=["gxbc"], dsem="gxbc")
        w1s, w3s, w2s = load_ffn_weights(P, cx, w1, w3, w2)
        xin = [cx.sb([128, D], F32, "xin") for _ in range(2)]
        hb = cx.sb([128, 4, D], BF16, "hb")
        hT = cx.sb([128, 8, 512], BF16, "hT")
        gT = cx.sb([128, NFC, 512], BF16, "gT")
        s1 = cx.sb([128, 512], F32, "s1")
        xr = [cx.sb([128, 512], F32, "xr") for _ in range(2)]
        yo = [cx.sb([128, D], F32, "yo") for _ in range(2)]
        st = [cx.sb([128, 4], F32, "st") for _ in range(2)]
        if mode != "plain":
            hsb = cx.sb([128, D], BF16, "hsb")
        pT = [cx.ps([128, D], BF16, "pT") for _ in range(2)]
        ps1 = [cx.ps([128, 512], F32, "ps1") for _ in range(2)]
        ps3 = [cx.ps([128, 512], F32, "ps3") for _ in range(2)]
        psy = [cx.ps([128, 512], F32, "psy") for _ in range(2)]
        P.excl.update(["pT0", "pT1"] + [f"psw1s{b}" for b in range(2)] + [f"psw3s{b}" for b in range(2)] + [f"psy{b}" for b in range(2)])
        ti = ui = yi = 0
        cntr = {"ti": 0, "tr": 0}

        def emit_norm(gi):
            for j in range(4):
                t0 = gi * 512 + j * 128
                b = cntr["ti"] % 2
                cntr["ti"] += 1
                xt = xin[b]
                P.add("sp", lambda h, xt=xt, t0=t0: h.dma_start(out=xt[:], in_=src[t0:t0 + 128, :]),
                      reads=["srcdram"], writes=[f"xin{b}"], dsem=f"xin{b}")
                ss = st[b][:, 0:1]
                rstd = st[b][:, 1:2]
                rms_rstd(P, cx, xt[:], f"xin{b}", D, hb[:, j, :], ss, rstd, f"n{b}", junk_keys=[f"hb{j}"])
                P.add("dve", lambda h, xt=xt, rstd=rstd, j=j: h.scalar_tensor_tensor(
                    out=hb[:, j, :], in0=xt[:], scalar=rstd, op0=ALU.mult, in1=gbc[:], op1=ALU.mult),
                    reads=[f"xin{b}", f"n{b}_rstd", "gbc", f"n{b}_junk"], writes=[f"hb{j}"])

        emit_norm(0)
        for gi in range(NG):
            for j in range(4):
                pb = cntr["tr"] % 2
                cntr["tr"] += 1
                for dc in range(8):
                    P.add("pe", lambda h, j=j, dc=dc, pb=pb: h.transpose(
                        out=pT[pb][:, dc * 128:(dc + 1) * 128], in_=hb[:, j, dc * 128:(dc + 1) * 128], identity=ident[:]),
                        reads=[f"hb{j}", "ident"], writes=[f"pT{pb}"])
                P.add("act", lambda h, j=j, pb=pb: h.copy(
                    out=hT[:, :, j * 128:(j + 1) * 128], in_=pT[pb][:].rearrange("p (c t) -> p c t", c=8)),
                    reads=[f"pT{pb}"], writes=["hT"])
            if gi + 1 < NG:
                emit_norm(gi + 1)
            for fc in range(NFC):
                b = ui % 2
                for (ws, ps, nm) in ((w1s, ps1[b], "w1s"), (w3s, ps3[b], "w3s")):
                    for dc in range(8):
                        P.add("pe", lambda h, ws=ws, ps=ps, dc=dc, fc=fc: h.matmul(
                            ps[:], lhsT=ws[:, dc, fc * 128:(fc + 1) * 128], rhs=hT[:, dc, :],
                            start=(dc == 0), stop=(dc == 7)),
                            reads=[ffn_w13_key(fc), "hT"], writes=[f"ps{nm}{b}"])
                P.add("act", lambda h, b=b: h.activation(out=s1[:], in_=ps1[b][:], func=AF.Silu),
                      reads=[f"psw1s{b}"], writes=["s1"])
                P.add("dve", lambda h, b=b, fc=fc: h.tensor_tensor(
                    out=gT[:, fc, :], in0=s1[:], in1=ps3[b][:], op=ALU.mult),
                    reads=["s1", f"psw3s{b}"], writes=["gT"])
                ui += 1
            for j in range(4):
                t0 = gi * 512 + j * 128
                ob = (gi * 4 + j) % 2
                for hf in range(2):
                    b = yi % 2
                    P.add("sp", lambda h, b=b, t0=t0, hf=hf: h.dma_start(
                        out=xr[b][:], in_=src[t0:t0 + 128, hf * 512:(hf + 1) * 512]),
                        reads=["srcdram"], writes=[f"xr{b}"], dsem=f"xr{b}")
                    for fc in range(NFC):
                        P.add("pe", lambda h, b=b, fc=fc, j=j, hf=hf: h.matmul(
                            psy[b][:], lhsT=gT[:, fc, j * 128:(j + 1) * 128], rhs=w2s[:, fc, hf * 512:(hf + 1) * 512],
                            start=(fc == 0), stop=(fc == NFC - 1)),
                            reads=["gT", ffn_w2_key(fc)], writes=[f"psy{b}"])
                    P.add("dve", lambda h, b=b, ob=ob, hf=hf: h.scalar_tensor_tensor(
                        out=yo[ob][:, hf * 512:(hf + 1) * 512], in0=psy[b][:], scalar=0.5, op0=ALU.mult,
                        in1=xr[b][:], op1=ALU.add),
                        reads=[f"psy{b}", f"xr{b}"], writes=[f"yo{ob}"])
                    yi += 1
                if mode == "final":
                    ss = st[ob][:, 2:3]
                    rstd = st[ob][:, 3:4]
                    rms_rstd(P, cx, yo[ob][:], f"yo{ob}", D, hsb[:], ss, rstd, f"f{ob}", junk_keys=["hsb"])
                    P.add("dve", lambda h, ob=ob, rstd=rstd: h.scalar_tensor_tensor(
                        out=yo[ob][:], in0=yo[ob][:], scalar=rstd, op0=ALU.mult, in1=gxbc[:], op1=ALU.mult),
                        reads=[f"yo{ob}", f"f{ob}_rstd", "gxbc"], writes=[f"yo{ob}"])
                P.add("sp", lambda h, ob=ob, t0=t0: h.dma_start(out=dst[t0:t0 + 128, :], in_=yo[ob][:]),
                      reads=[f"yo{ob}"], writes=["dstdram"], dsem=f"yst{ob}")
                if mode == "hs":
                    ss = st[ob][:, 2:3]
                    rstd = st[ob][:, 3:4]
                    rms_rstd(P, cx, yo[ob][:], f"yo{ob}", D, hsb[:], ss, rstd, f"f{ob}", junk_keys=["hsb"])
                    P.add("dve", lambda h, ob=ob, rstd=rstd: h.scalar_tensor_tensor(
                        out=hsb[:], in0=yo[ob][:], scalar=rstd, op0=ALU.mult, in1=gxbc[:], op1=ALU.mult),
                        reads=[f"yo{ob}", f"f{ob}_rstd", "gxbc", f"f{ob}_junk"], writes=["hsb"])
                    k = t0 // 1024
                    r0 = t0 % 1024
                    P.add("sp", lambda h, k=k, r0=r0: h.dma_start(out=Hs[k].ap()[r0:r0 + 128, :], in_=hsb[:]),
                          reads=["hsb"], writes=[f"Hs{k}"], dsem="hsst")
                    if r0 + 128 == 1024:
                        P.add("pool", lambda h, k=k: h.collective_compute(
                            "AllGather", ALU.bypass, replica_groups=PAIRS, ins=[Hs[k].ap().opt()], outs=[Hg[k].ap().opt()]),
                            reads=[f"Hs{k}"], writes=[f"Hg{k}"], dsem="cc", inc=1)
        P.barrier()
        P.emit()


W_OUT_PERM = np.concatenate([np.arange(0, 192), np.arange(384, 576), np.arange(768, 896),
                             np.arange(192, 384), np.arange(576, 768), np.arange(896, 1024)])


def build_fused(T=4096, S=8192, depth=2):
    nc = bass.Bass("TRN2", target_bir_lowering=False)
    ext = lambda name, shape, dt=F32: nc.dram_tensor(name, list(shape), dt, kind="ExternalInput").ap()
    x_in = ext("x", [T, D])
    pos = ext("pos", [S], I32)
    mem = ext("mem", [256, D])
    flags = ext("flags", [128, 2])
    gf = ext("gf", [D])
    y_out = nc.dram_tensor("y", [T, D], F32, kind="ExternalOutput").ap()
    cn = {k: ext(k, v.shape) for k, v in mixer_consts(0).items()}
    L = []
    for l in range(depth):
        d = {}
        for nm, shp in (("f1g", [D]), ("f1w1", [D, DFF]), ("f1w3", [D, DFF]), ("f1w2", [DFF, D]),
                        ("f2g", [D]), ("f2w1", [D, DFF]), ("f2w3", [D, DFF]), ("f2w2", [DFF, D]),
                        ("gmix", [D]), ("wfm", [D, NFM]), ("wtm", [D, NTM]), ("g_ret", [192]), ("g_hg", [192]),
                        ("lbp", [128, 4]), ("lflag", [128, 1]), ("lru_p", [128, 8]), ("wa_bd", [128, 128]), ("wx_bd", [128, 128]),
                        ("w_outp", [D, D]), ("gx", [D]), ("gm", [D]), ("wq", [D, D]), ("wkv", [D, 2 * D]), ("wo", [D, D])):
            d[nm] = ext(f"{nm}_{l}", shp)
        L.append(d)
    XA = nc.dram_tensor("XA", [T, D], F32)
    XB = nc.dram_tensor("XB", [T, D], F32)
    XC = nc.dram_tensor("XC", [T, D], F32)
    Hs = [nc.dram_tensor(f"Hs{k}", [1024, D], BF16) for k in range(4)]
    Hg = [nc.dram_tensor(f"Hg{k}", [2048, D], BF16) for k in range(4)]
    Mh = [nc.dram_tensor(f"Mh{k}", [2048, 512], BF16) for k in range(4)]
    Mg = [nc.dram_tensor(f"Mg{k}", [4096, 512], BF16) for k in range(4)]
    with ExitStack() as gstack:
        P = Prog(nc, gstack)
        gcx = SCtx(nc, gstack)
        ident = make_identity(P, gcx)
        flg = gcx.sb([128, 2], F32, "flg")
        P.add("sp", lambda h: h.dma_start(out=flg[:], in_=flags), writes=["flg"], dsem="flg")
        cur_in = x_in
        for l in range(depth):
            W = L[l]
            last = (l == depth - 1)
            stage_ffn(P, nc, ident, T, cur_in, XA.ap(), W["f1g"], W["f1w1"], W["f1w3"], W["f1w2"], "hs", gx=W["gmix"], Hs=Hs, Hg=Hg)
            with ExitStack() as sst:
                cx = SCtx(nc, sst)
                A = {"pos": pos, "wfm": W["wfm"], "wtm": W["wtm"], "g_ret": W["g_ret"], "g_hg": W["g_hg"], "lbp": W["lbp"],
                     "lflag": W["lflag"], "lru_p": W["lru_p"], "wa_bd": W["wa_bd"], "wx_bd": W["wx_bd"], "cn": cn,
                     "Hg": Hg, "Mh": Mh, "Mg": Mg}
                mixer_body(P, nc, cx, ident, S, A, True)
                P.barrier()
                P.emit()
            with ExitStack() as sst:
                cx = SCtx(nc, sst)
                A = {"x": XA.ap(), "mem": mem, "gx": W["gx"], "gm": W["gm"], "w_out": W["w_outp"], "wq": W["wq"], "wkv": W["wkv"],
                     "wo": W["wo"], "y": XB.ap(), "Mg": Mg, "flg": flg}
                xattn_body(P, nc, cx, ident, T, A, True)
                P.barrier()
                P.emit()
            if last:
                stage_ffn(P, nc, ident, T, XB.ap(), y_out, W["f2g"], W["f2w1"], W["f2w3"], W["f2w2"], "final", gx=gf)
            else:
                stage_ffn(P, nc, ident, T, XB.ap(), XC.ap(), W["f2g"], W["f2w1"], W["f2w3"], W["f2w2"], "plain")
                cur_in = XC.ap()
    return nc


def fused_inputs(inp, c):
    f = np.float32
    c32 = lambda a: np.ascontiguousarray(a, dtype=f)
    Bn, S, _ = inp["x"].shape
    T = Bn * S // 8
    b, r = c // 2, c % 2
    d = {"x": c32(inp["x"].reshape(8, T, D)[c]), "pos": np.ascontiguousarray(inp["positions"][b].astype(np.int32)),
         "mem": c32(inp["mem"][b]), "gf": c32(inp["final_norm_g"])}
    fl = np.zeros((128, 2), f)
    fl[:, r] = 1.0
    d["flags"] = fl
    d.update(mixer_consts(r))
    for l in range(inp["w_in"].shape[0]):
        mp = mixer_pack(inp, l, r)
        for k in ("wfm", "wtm", "g_ret", "g_hg", "lbp", "lflag", "lru_p", "wa_bd", "wx_bd"):
            d[f"{k}_{l}"] = c32(mp[k])
        d[f"gmix_{l}"] = c32(inp["mix_norm_g"][l])
        for a, bname in (("f1", "ffn1"), ("f2", "ffn2")):
            d[f"{a}g_{l}"] = c32(inp[f"{bname}_norm_g"][l])
            d[f"{a}w1_{l}"] = c32(inp[f"{bname}_w1"][l])
            d[f"{a}w3_{l}"] = c32(inp[f"{bname}_w3"][l])
            d[f"{a}w2_{l}"] = c32(inp[f"{bname}_w2"][l])
        d[f"w_outp_{l}"] = c32(inp["w_out"][l][W_OUT_PERM])
        d[f"gx_{l}"] = c32(inp["xattn_norm_g"][l])
        d[f"gm_{l}"] = c32(inp["xattn_mem_g"][l])
        d[f"wq_{l}"] = c32(inp["xattn_wq"][l])
        d[f"wkv_{l}"] = c32(inp["xattn_wkv"][l])
        d[f"wo_{l}"] = c32(inp["xattn_wo"][l])
    return d


def kernel(**inp):
    inp = {k: np.asarray(v) for k, v in inp.items()}
    Bn, S, _ = inp["x"].shape
    T = Bn * S // 8
    depth = inp["w_in"].shape[0]
    nc = _prog("fused", lambda: build_fused(T, S, depth))
    maps = [fused_inputs(inp, c) for c in range(8)]
    r = _run(nc, maps)
    out = np.stack([r[c]["y"] for c in range(8)], 0)
    return out.reshape(Bn, S, D).astype(np.float32)
```

```python
import math
from contextlib import ExitStack

import numpy as np
import concourse.bass as bass
import concourse.mybir as mybir
from concourse.bass_utils import run_bass_kernel_spmd

F32 = mybir.dt.float32
BF16 = mybir.dt.bfloat16
I32 = mybir.dt.int32
AF = mybir.ActivationFunctionType
ALU = mybir.AluOpType
AX = mybir.AxisListType

D = 1024
DFF = 2816
NFC = DFF // 128
EPS = 1e-6

ENGS = ("pe", "act", "dve", "pool", "sp")
import os as _os
SAME_ENGINE_SYNC = _os.environ.get("SES", "1") == "1"


class Prog:
    def __init__(self, nc, stack, same_engine_sync=SAME_ENGINE_SYNC):
        self.nc = nc
        self.stack = stack
        self.ops = []
        self.eng_ops = {e: [] for e in ENGS}
        self.last_write = {}
        self.readers = {}
        self.seen = {e: {} for e in ENGS}
        self.dsems = {}
        self.same_engine_sync = same_engine_sync
        self.n_dsem = 0
        self.excl = set()

    def dsem(self, name):
        if name not in self.dsems:
            h = self.stack.enter_context(self.nc.semaphore("d_" + name))
            self.dsems[name] = {"h": h, "val": 0, "name": name}
        return self.dsems[name]

    def add(self, eng, fn, reads=(), writes=(), dsem=None, after=(), inc=16):
        oid = len(self.ops)
        seq = len(self.eng_ops[eng])
        writes = list(writes) + [r for r in reads if r in self.excl and r not in writes]
        reads = [r for r in reads if r not in self.excl]
        deps = set()
        for r in reads:
            lw = self.last_write.get(r)
            if lw is not None:
                deps.add(lw)
        for w in writes:
            lw = self.last_write.get(w)
            if lw is not None:
                deps.add(lw)
            for rd in self.readers.get(w, {}).values():
                deps.add(rd)
        seen = self.seen[eng]
        waits = []
        deps.update(after)
        for d in sorted(deps):
            dop = self.ops[d]
            if dop["dsem"] is not None:
                key = ("d", dop["dsem"]["name"])
                val = dop["dval"]
            else:
                if dop["eng"] == eng and (eng == "pe" or not self.same_engine_sync) and d not in after:
                    continue
                key = ("e", dop["eng"])
                val = dop["seq"]
            if seen.get(key, -1) >= val:
                continue
            waits.append(d)
            dop["marked"] = True
            for k, v in dop["know"].items():
                if seen.get(k, -1) < v:
                    seen[k] = v
            if seen.get(key, -1) < val:
                seen[key] = val
        op = {"eng": eng, "fn": fn, "seq": seq, "waits": waits, "marked": False, "dsem": None, "dval": None}
        if dsem is not None:
            ds = self.dsem(dsem)
            ds["val"] += inc
            op["inc"] = inc
            op["dsem"] = ds
            op["dval"] = ds["val"]
            op["marked"] = True
            know = dict(seen)
            know[("d", ds["name"])] = ds["val"]
        else:
            know = dict(seen)
            know[("e", eng)] = seq
        op["know"] = know
        self.ops.append(op)
        self.eng_ops[eng].append(oid)
        for r in reads:
            self.readers.setdefault(r, {})[eng if dsem is None else ("dma", dsem)] = oid
        for w in writes:
            self.last_write[w] = oid
            self.readers[w] = {}
        return oid

    def mark(self, name):
        self.marks = getattr(self, "marks", {})
        self.marks.setdefault(name, len(self.ops))

    def barrier(self):
        start = getattr(self, "emitted", 0)
        deps = []
        for e in ENGS:
            if self.eng_ops[e] and self.eng_ops[e][-1] >= start:
                deps.append(self.eng_ops[e][-1])
        lastd = {}
        for oid in range(start, len(self.ops)):
            op = self.ops[oid]
            if op["dsem"] is not None:
                lastd[op["dsem"]["name"]] = oid
        deps += list(lastd.values())
        b = self.add("sp", lambda h: h.nop(), after=tuple(deps))
        for e in ENGS:
            if e != "sp":
                self.add(e, lambda h: h.nop(), after=(b,))
        self.last_write = {}
        self.readers = {}

    def emit(self, limit=None):
        nc = self.nc
        start = getattr(self, "emitted", 0)
        if limit is None:
            limit = len(self.ops)
        if not hasattr(self, "esem"):
            self.esem = {e: self.stack.enter_context(nc.semaphore("e_" + e)) for e in ENGS}
            self.ecnt = {e: 0 for e in ENGS}
        esem = self.esem
        for e in ENGS:
            for oid in self.eng_ops[e]:
                if oid < start:
                    continue
                op = self.ops[oid]
                if op["dsem"] is None and op["marked"]:
                    self.ecnt[e] += 1
                    op["cnt"] = self.ecnt[e]
        ops = self.ops
        eng_ops = self.eng_ops
        self.emitted = limit

        def run(e, h):
            for oid in eng_ops[e]:
                if oid < start:
                    continue
                if oid >= limit:
                    break
                op = ops[oid]
                for d in op["waits"]:
                    dop = ops[d]
                    if dop["dsem"] is not None:
                        h.wait_ge(dop["dsem"]["h"], dop["dval"])
                    else:
                        h.wait_ge(esem[dop["eng"]], dop["cnt"])
                inst = op["fn"](h)
                if op["dsem"] is not None:
                    inst.then_inc(op["dsem"]["h"], op["inc"])
                elif op["marked"]:
                    inst.then_inc(esem[e], 1)

        with nc.Block() as block:
            @block.tensor
            def _(h):
                run("pe", h)

            @block.scalar
            def _(h):
                run("act", h)

            @block.vector
            def _(h):
                run("dve", h)

            @block.gpsimd
            def _(h):
                run("pool", h)

            @block.sync
            def _(h):
                run("sp", h)

    def finish_wait(self, eng, keys):
        self.add(eng, lambda h: h.nop(), reads=tuple(keys))


class Ctx:
    def __init__(self, nc, stack):
        self.nc = nc
        self.stack = stack
        self.n = 0

    def sb(self, shape, dt, name=None):
        self.n += 1
        return self.stack.enter_context(self.nc.sbuf_tensor(f"{name or 't'}_{self.n}", list(shape), dt))

    def ps(self, shape, dt, name=None):
        self.n += 1
        return self.stack.enter_context(self.nc.psum_tensor(f"{name or 'p'}_{self.n}", list(shape), dt))


def bcast_rows(ap_1d, nparts):
    n = ap_1d.shape[0]
    return bass.AP(ap_1d.tensor, ap_1d.offset, [[0, nparts], [1, n]])


def make_identity(P, cx, dt=BF16):
    nc = cx.nc
    it = cx.sb([128, 128], I32, "iota")
    ident = cx.sb([128, 128], dt, "ident")
    P.add("pool", lambda h: h.iota(it[:], pattern=[[1, 128]], base=0, channel_multiplier=-1), writes=["iota_t"])
    P.add("dve", lambda h: h.tensor_scalar(out=ident[:], in0=it[:], scalar1=0.0, scalar2=None, op0=ALU.is_equal),
          reads=["iota_t"], writes=["ident"])
    return ident


def rms_rstd(P, cx, x_ap, xkey, n, scratch, ss, rstd, tag, junk_keys=()):
    P.add("act", lambda h: h.activation(out=scratch, in_=x_ap, func=AF.Square, accum_out=ss),
          reads=[xkey], writes=[tag + "_junk", tag + "_ss"] + list(junk_keys))
    P.add("dve", lambda h: h.tensor_scalar(out=ss, in0=ss, scalar1=1.0 / n, scalar2=EPS, op0=ALU.mult, op1=ALU.add),
          reads=[tag + "_ss"], writes=[tag + "_ss"])
    P.add("act", lambda h: h.activation(out=ss, in_=ss, func=AF.Sqrt), reads=[tag + "_ss"], writes=[tag + "_ss"])
    P.add("dve", lambda h: h.reciprocal(out=rstd, in_=ss), reads=[tag + "_ss"], writes=[tag + "_rstd"])


def ffn_w13_key(fc):
    return f"w13v{fc // 11}"


def ffn_w2_key(fc):
    return f"w2v{fc // 11}"


def load_ffn_weights(P, cx, w1, w3, w2, tag=""):
    w1s = cx.sb([128, 8, DFF], BF16, "w1s")
    w3s = cx.sb([128, 8, DFF], BF16, "w3s")
    w2s = cx.sb([128, NFC, D], BF16, "w2s")
    HALF = DFF // 2
    for hf in range(2):
        for (ws, w) in ((w1s, w1), (w3s, w3)):
            P.add("pool",
                  lambda h, ws=ws, w=w, hf=hf: h.dma_start(
                      out=ws[:, :, hf * HALF:(hf + 1) * HALF],
                      in_=w[:, hf * HALF:(hf + 1) * HALF].rearrange("(c p) n -> p c n", p=128)),
                  writes=[f"w13v{hf}"], dsem=f"w13v{hf}")
    for wv in range(2):
        P.add("pool", lambda h, wv=wv: h.dma_start(
            out=w2s[:, wv * 11:(wv + 1) * 11, :],
            in_=w2[wv * 11 * 128:(wv + 1) * 11 * 128, :].rearrange("(c p) n -> p c n", p=128)),
            writes=[f"w2v{wv}"], dsem=f"w2v{wv}")
    return w1s, w3s, w2s


def build_ffn(T, final_norm=False):
    nc = bass.Bass("TRN2", target_bir_lowering=False)
    x = nc.dram_tensor("x", [T, D], F32, kind="ExternalInput").ap()
    g = nc.dram_tensor("g", [D], F32, kind="ExternalInput").ap()
    w1 = nc.dram_tensor("w1", [D, DFF], F32, kind="ExternalInput").ap()
    w3 = nc.dram_tensor("w3", [D, DFF], F32, kind="ExternalInput").ap()
    w2 = nc.dram_tensor("w2", [DFF, D], F32, kind="ExternalInput").ap()
    gf = nc.dram_tensor("gf", [D], F32, kind="ExternalInput").ap() if final_norm else None
    y = nc.dram_tensor("y", [T, D], F32, kind="ExternalOutput").ap()
    with ExitStack() as stack:
        P = Prog(nc, stack)
        cx = Ctx(nc, stack)
        ident = make_identity(P, cx)
        stage_ffn(P, nc, ident, T, x, y, g, w1, w3, w2, "final" if final_norm else "plain", gx=gf)
    return nc


def load_w_bf16(P, cx, w, kin, nout, name, col0=0):
    kc = kin // 128
    ws = cx.sb([128, kc, nout], BF16, name)
    step = 1024
    for n0 in range(0, nout, step):
        n1 = min(nout, n0 + step)
        P.add("pool", lambda h, n0=n0, n1=n1: h.dma_start(
            out=ws[:, :, n0:n1], in_=w[:, col0 + n0:col0 + n1].rearrange("(c p) n -> p c n", p=128)),
            writes=[name], dsem=name)
    return ws


def xattn_body(P, nc, cx, ident, T, A, fused):
    x = A["x"]; mx_in = A.get("mixed"); mem = A["mem"]; gx = A["gx"]; gm = A["gm"]; w_out = A["w_out"]
    wq = A["wq"]; wkv = A["wkv"]; wo = A["wo"]; y = A["y"]
    NG = T // 512
    SC = 1.0 / 16.0
    if True:
        gxbc = cx.sb([128, D], F32, "gxbc")
        gmbc = cx.sb([128, D], F32, "gmbc")
        P.add("sp", lambda h: h.dma_start(out=gxbc[:], in_=bcast_rows(gx, 128)), writes=["gxbc"], dsem="gxbc")
        P.add("sp", lambda h: h.dma_start(out=gmbc[:], in_=bcast_rows(gm, 128)), writes=["gmbc"], dsem="gmbc")
        wkvs = load_w_bf16(P, cx, wkv, D, 2 * D, "wkvs")
        w_outs = load_w_bf16(P, cx, w_out, D, D, "w_outs")
        wqs = load_w_bf16(P, cx, wq, D, D, "wqs")
        wos = load_w_bf16(P, cx, wo, D, D, "wos")

        xin = [cx.sb([128, D], F32, "xin") for _ in range(2)]
        junk = cx.sb([128, D], BF16, "junk")
        st = [cx.sb([128, 16], F32, "st") for _ in range(2)]
        hb = [cx.sb([128, D], BF16, "hb") for _ in range(2)]
        pT = [cx.ps([128, D], BF16, "pT") for _ in range(2)]
        pA = [cx.ps([128, 512], F32, "pA") for _ in range(2)]
        pS = cx.ps([128, 1024], F32, "pS")
        pY = [cx.ps([128, 512], F32, "pY") for _ in range(2)]
        memT = cx.sb([128, 8, 256], BF16, "memT")
        kT = cx.sb([128, 8, 256], BF16, "kT")
        vv = cx.sb([128, 2, D], BF16, "vv")
        cnt = {"tr": 0, "a": 0, "y": 0}
        P.excl.update(["pT0", "pT1", "pA0", "pA1", "pS", "pY0", "pY1"])

        def transpose8(src_fn, srckeys, dst_fn, dstkey):
            b = cnt["tr"] % 2
            cnt["tr"] += 1
            for dc in range(8):
                P.add("pe", lambda h, dc=dc, b=b: h.transpose(
                    out=pT[b][:, dc * 128:(dc + 1) * 128], in_=src_fn(dc), identity=ident[:]),
                    reads=list(srckeys) + ["ident"], writes=[f"pT{b}"])
            P.add("act", lambda h, b=b: h.copy(out=dst_fn(), in_=pT[b][:].rearrange("p (c t) -> p c t", c=8)),
                  reads=[f"pT{b}"], writes=[dstkey])

        for mt in range(2):
            b = mt
            P.add("sp", lambda h, b=b, mt=mt: h.dma_start(out=xin[b][:], in_=mem[mt * 128:(mt + 1) * 128, :]),
                  writes=[f"xin{b}"], dsem=f"xin{b}")
            rms_rstd(P, cx, xin[b][:], f"xin{b}", D, junk[:], st[b][:, 0:1], st[b][:, 1:2], f"n{b}")
            P.add("dve", lambda h, b=b: h.scalar_tensor_tensor(
                out=hb[b][:], in0=xin[b][:], scalar=st[b][:, 1:2], op0=ALU.mult, in1=gmbc[:], op1=ALU.mult),
                reads=[f"xin{b}", f"n{b}_rstd", "gmbc"], writes=[f"hb{b}"])
            transpose8(lambda dc, b=b: hb[b][:, dc * 128:(dc + 1) * 128], [f"hb{b}"],
                       lambda mt=mt: memT[:, :, mt * 128:(mt + 1) * 128], "memT")
        for c in range(8):
            b = cnt["a"] % 2
            cnt["a"] += 1
            for dc in range(8):
                P.add("pe", lambda h, b=b, c=c, dc=dc: h.matmul(
                    pA[b][:, 0:256], lhsT=wkvs[:, dc, c * 128:(c + 1) * 128], rhs=memT[:, dc, :],
                    start=(dc == 0), stop=(dc == 7)), reads=["wkvs", "memT"], writes=[f"pA{b}"])
            P.add("act", lambda h, b=b, c=c: h.copy(out=kT[:, c, :], in_=pA[b][:, 0:256]),
                  reads=[f"pA{b}"], writes=["kT"])
        for mc in range(2):
            for hf in range(2):
                b = cnt["a"] % 2
                cnt["a"] += 1
                for dc in range(8):
                    P.add("pe", lambda h, b=b, mc=mc, hf=hf, dc=dc: h.matmul(
                        pA[b][:], lhsT=memT[:, dc, mc * 128:(mc + 1) * 128],
                        rhs=wkvs[:, dc, D + hf * 512:D + (hf + 1) * 512],
                        start=(dc == 0), stop=(dc == 7)), reads=["wkvs", "memT"], writes=[f"pA{b}"])
                P.add("act", lambda h, b=b, mc=mc, hf=hf: h.copy(
                    out=vv[:, mc, hf * 512:(hf + 1) * 512], in_=pA[b][:]),
                    reads=[f"pA{b}"], writes=["vv"])

        xg = cx.sb([128, 4, D], F32, "xg")
        hbn = [cx.sb([128, D], BF16, f"hbn{i}") for i in range(4)]
        stn = cx.sb([128, 8], F32, "stn")
        if fused:
            cand = [cx.sb([128, 2, D], BF16, f"cand{i}") for i in range(2)]
        mT = cx.sb([128, 8, 512], BF16, "mT")
        hT = cx.sb([128, 8, 512], BF16, "hT")
        qT = cx.sb([128, 8, 512], BF16, "qT")
        ppT = cx.sb([128, 8, 512], BF16, "ppT")
        oT = cx.sb([128, 8, 512], BF16, "oT")
        pe_ = cx.sb([128, 4, 256], F32, "pexp")
        pn2 = [cx.sb([128, 4, 256], BF16, f"pn{i}") for i in range(2)]
        yo = [cx.sb([128, D], F32, "yo") for _ in range(2)]
        ti = 0
        tcnt = {"m": 0}

        def emit_mixed(gi):
            for j in range(4):
                t0 = gi * 512 + j * 128
                b = tcnt["m"] % 2
                tcnt["m"] += 1
                if not fused:
                    P.add("sp", lambda h, b=b, t0=t0: h.dma_start(out=xin[b][:], in_=mx_in[t0:t0 + 128, :]),
                          writes=[f"xin{b}"], dsem=f"xin{b}")
                    P.add("dve", lambda h, b=b: h.tensor_copy(out=hb[b][:], in_=xin[b][:]),
                          reads=[f"xin{b}"], writes=[f"hb{b}"])
                else:
                    k0 = t0 // 2048
                    row = t0 % 2048
                    for f_ in range(2):
                        for r_ in range(2):
                            P.add("sp", lambda h, b=b, f_=f_, r_=r_, k0=k0, row=row: h.dma_start(
                                out=cand[b][:, f_, r_ * 512:(r_ + 1) * 512],
                                in_=A["Mg"][2 * f_ + k0].ap()[r_ * 2048 + row:r_ * 2048 + row + 128, :]),
                                reads=[f"Mg{2 * f_ + k0}"], writes=[f"cand{b}"], dsem=f"cand{b}")
                    P.add("dve", lambda h, b=b: h.tensor_scalar(out=cand[b][:, 0, :], in0=cand[b][:, 0, :], scalar1=A["flg"][:, 0:1],
                                                                scalar2=None, op0=ALU.mult), reads=[f"cand{b}", "flg"], writes=[f"cand{b}"])
                    P.add("dve", lambda h, b=b: h.scalar_tensor_tensor(out=hb[b][:], in0=cand[b][:, 1, :], scalar=A["flg"][:, 1:2],
                                                                       op0=ALU.mult, in1=cand[b][:, 0, :], op1=ALU.add),
                          reads=[f"cand{b}", "flg"], writes=[f"hb{b}"])
                transpose8(lambda dc, b=b: hb[b][:, dc * 128:(dc + 1) * 128], [f"hb{b}"],
                           lambda j=j: mT[:, :, j * 128:(j + 1) * 128], "mT")

        emit_mixed(0)
        for gi in range(NG):
            for j in range(4):
                t0 = gi * 512 + j * 128
                P.add("sp", lambda h, j=j, t0=t0: h.dma_start(out=xg[:, j, :], in_=x[t0:t0 + 128, :]),
                      reads=["srcdram"], writes=[f"xg{j}"], dsem=f"xg{j}")
            for j in range(4):
                for hf in range(2):
                    b = cnt["y"] % 2
                    cnt["y"] += 1
                    for fc in range(8):
                        P.add("pe", lambda h, b=b, fc=fc, j=j, hf=hf: h.matmul(
                            pY[b][:], lhsT=mT[:, fc, j * 128:(j + 1) * 128], rhs=w_outs[:, fc, hf * 512:(hf + 1) * 512],
                            start=(fc == 0), stop=(fc == 7)), reads=["mT", "w_outs"], writes=[f"pY{b}"])
                    P.add("dve", lambda h, b=b, j=j, hf=hf: h.tensor_tensor(
                        out=xg[:, j, hf * 512:(hf + 1) * 512], in0=pY[b][:], in1=xg[:, j, hf * 512:(hf + 1) * 512],
                        op=ALU.add), reads=[f"pY{b}", f"xg{j}"], writes=[f"xg{j}"])
                rms_rstd(P, cx, xg[:, j, :], f"xg{j}", D, junk[:], stn[:, 2 * j:2 * j + 1], stn[:, 2 * j + 1:2 * j + 2], f"nn{j}")
                P.add("dve", lambda h, j=j: h.scalar_tensor_tensor(
                    out=hbn[j][:], in0=xg[:, j, :], scalar=stn[:, 2 * j + 1:2 * j + 2], op0=ALU.mult, in1=gxbc[:], op1=ALU.mult),
                    reads=[f"xg{j}", f"nn{j}_rstd", "gxbc"], writes=[f"hbn{j}"])
            for j in range(4):
                transpose8(lambda dc, j=j: hbn[j][:, dc * 128:(dc + 1) * 128], [f"hbn{j}"],
                           lambda j=j: hT[:, :, j * 128:(j + 1) * 128], "hT")
            for c in range(8):
                b = cnt["a"] % 2
                cnt["a"] += 1
                for dc in range(8):
                    P.add("pe", lambda h, b=b, c=c, dc=dc: h.matmul(
                        pA[b][:], lhsT=wqs[:, dc, c * 128:(c + 1) * 128], rhs=hT[:, dc, :],
                        start=(dc == 0), stop=(dc == 7)), reads=["wqs", "hT"], writes=[f"pA{b}"])
                P.add("act", lambda h, b=b, c=c: h.copy(out=qT[:, c, :], in_=pA[b][:]),
                      reads=[f"pA{b}"], writes=["qT"])
            def emit_scores(j):
                for hd in range(4):
                    for cc in range(2):
                        P.add("pe", lambda h, hd=hd, cc=cc, j=j: h.matmul(
                            pS[:, hd * 256:(hd + 1) * 256], lhsT=qT[:, 2 * hd + cc, j * 128:(j + 1) * 128],
                            rhs=kT[:, 2 * hd + cc, :], start=(cc == 0), stop=(cc == 1)),
                            reads=["qT", "kT"], writes=["pS"])

            emit_scores(0)
            for j in range(4):
                b = ti % 2
                ti += 1
                pnb = pn2[j % 2]
                mxs = st[b][:, 4:5]
                nb = st[b][:, 5:6]
                sm = st[b][:, 8:12]
                rs = st[b][:, 12:16]
                P.add("dve", lambda h, mxs=mxs: h.tensor_reduce(out=mxs, in_=pS[:], op=ALU.max, axis=AX.X),
                      reads=["pS"], writes=[f"mx{b}"])
                P.add("dve", lambda h, mxs=mxs, nb=nb: h.tensor_scalar(
                    out=nb, in0=mxs, scalar1=-SC, scalar2=None, op0=ALU.mult),
                    reads=[f"mx{b}"], writes=[f"nb{b}"])
                P.add("act", lambda h, nb=nb: h.activation(
                    out=pe_[:].rearrange("p a m -> p (a m)"), in_=pS[:], func=AF.Exp, bias=nb, scale=SC),
                    reads=["pS", f"nb{b}"], writes=["pexp"])
                if j + 1 < 4:
                    emit_scores(j + 1)
                P.add("dve", lambda h, sm=sm: h.tensor_reduce(out=sm, in_=pe_[:], op=ALU.add, axis=AX.X),
                      reads=["pexp"], writes=[f"sm{b}"])
                P.add("dve", lambda h, sm=sm, rs=rs: h.reciprocal(out=rs, in_=sm),
                      reads=[f"sm{b}"], writes=[f"rs{b}"])
                for hd in range(4):
                    P.add("dve", lambda h, hd=hd, rs=rs, pnb=pnb: h.tensor_scalar(
                        out=pnb[:, hd, :], in0=pe_[:, hd, :], scalar1=rs[:, hd:hd + 1], scalar2=None, op0=ALU.mult),
                        reads=["pexp", f"rs{b}"], writes=[f"pn{j % 2}"])
                transpose8(lambda c, pnb=pnb: pnb[:, c // 2, (c % 2) * 128:(c % 2 + 1) * 128], [f"pn{j % 2}"],
                           lambda j=j: ppT[:, :, j * 128:(j + 1) * 128], "ppT")
            for c in range(8):
                b = cnt["a"] % 2
                cnt["a"] += 1
                hd = c // 2
                for mc in range(2):
                    P.add("pe", lambda h, b=b, c=c, mc=mc, hd=hd: h.matmul(
                        pA[b][:], lhsT=vv[:, mc, c * 128:(c + 1) * 128], rhs=ppT[:, hd * 2 + mc, :],
                        start=(mc == 0), stop=(mc == 1)), reads=["vv", "ppT"], writes=[f"pA{b}"])
                P.add("act", lambda h, b=b, c=c: h.copy(out=oT[:, c, :], in_=pA[b][:]),
                      reads=[f"pA{b}"], writes=["oT"])
            if gi + 1 < NG:
                emit_mixed(gi + 1)
            for j in range(4):
                t0 = gi * 512 + j * 128
                ob = (gi * 4 + j) % 2
                for hf in range(2):
                    b = cnt["y"] % 2
                    cnt["y"] += 1
                    for c in range(8):
                        P.add("pe", lambda h, b=b, c=c, j=j, hf=hf: h.matmul(
                            pY[b][:], lhsT=oT[:, c, j * 128:(j + 1) * 128], rhs=wos[:, c, hf * 512:(hf + 1) * 512],
                            start=(c == 0), stop=(c == 7)), reads=["oT", "wos"], writes=[f"pY{b}"])
                    P.add("dve", lambda h, b=b, j=j, hf=hf, ob=ob: h.tensor_tensor(
                        out=yo[ob][:, hf * 512:(hf + 1) * 512], in0=pY[b][:], in1=xg[:, j, hf * 512:(hf + 1) * 512],
                        op=ALU.add), reads=[f"pY{b}", f"xg{j}"], writes=[f"yo{ob}"])
                P.add("sp", lambda h, ob=ob, t0=t0: h.dma_start(out=y[t0:t0 + 128, :], in_=yo[ob][:]),
                      reads=[f"yo{ob}"], writes=["ydram"], dsem=f"yst{ob}")
        if not fused:
            P.add("sp", lambda h: h.nop(), reads=[], writes=["yo0", "yo1"])


def build_xattn(T):
    nc = bass.Bass("TRN2", target_bir_lowering=False)
    di = lambda name, shape: nc.dram_tensor(name, list(shape), F32, kind="ExternalInput").ap()
    A = {"x": di("x", [T, D]), "mixed": di("mixed", [T, D]), "mem": di("mem", [256, D]), "gx": di("gx", [D]), "gm": di("gm", [D]),
         "w_out": di("w_out", [D, D]), "wq": di("wq", [D, D]), "wkv": di("wkv", [D, 2 * D]), "wo": di("wo", [D, D])}
    A["y"] = nc.dram_tensor("y", [T, D], F32, kind="ExternalOutput").ap()
    with ExitStack() as stack:
        P = Prog(nc, stack)
        cx = Ctx(nc, stack)
        ident = make_identity(P, cx)
        xattn_body(P, nc, cx, ident, T, A, False)
        P.emit()
    return nc


TWO_PI = 2.0 * math.pi
C1 = 6.28125
C2 = TWO_PI - C1
MAGIC = 12582912.0
GELU_C = 2.0 * math.sqrt(2.0 / math.pi)


def mixer_consts(hh):
    f = np.float32
    half = 48
    inv_freq = np.exp(-math.log(10000.0) * np.arange(half, dtype=np.float64) / half)
    invf = np.zeros((128, 1), f)
    for p in range(128):
        j = p % 64
        if j < half:
            invf[p, 0] = inv_freq[j]
    lg = np.log1p(-np.exp2(-5.0 - np.arange(4, dtype=np.float64)))
    hs = [2 * hh, 2 * hh + 1]
    idx = np.arange(128)
    same = (idx[:, None] // 64) == (idx[None, :] // 64)
    dist = np.abs(idx[:, None] - idx[None, :])
    dmask = np.zeros((128, 2, 128), f)
    for a, h in enumerate(hs):
        dmask[:, a, :] = np.where(same, np.exp(lg[h] * dist), 0.0) * (96 ** -0.5)
    t = np.arange(512)
    qin = np.zeros((128, 512), f)
    cd = np.zeros((128, 1), f)
    for p in range(128):
        h = hs[p // 64]
        qin[p] = np.exp(lg[h] * ((t % 64) + 1.0)) * (96 ** -0.5)
        cd[p, 0] = np.exp(lg[h] * 64.0)
    kout = np.zeros((128, 192), f)
    for a, h in enumerate(hs):
        kout[:, a * 96:(a + 1) * 96] = np.exp(lg[h] * (63.0 - (idx % 64)))[:, None]
    maskL = (same & (idx[None, :] >= idx[:, None])).astype(f)
    maskU = (same & (idx[None, :] < idx[:, None])).astype(f)
    reset = np.ones((128, 512), f)
    reset[:, ::64] = 0.0
    return {"c_invf": invf, "c_dmask": dmask, "c_qin": qin, "c_cd": cd, "c_kout": kout,
            "c_maskL": maskL, "c_maskU": maskU, "c_reset": reset}


NFM = 1280
NTM = 768


def mixer_body(P, nc, cx, ident, S, A, fused):
    x = A.get("x"); pos = A["pos"]; g = A.get("g"); wfm = A["wfm"]; wtm = A["wtm"]
    g_ret = A["g_ret"]; g_hg = A["g_hg"]; lbp = A["lbp"]; lflag = A["lflag"]; lru_p = A["lru_p"]
    wa_bd = A["wa_bd"]; wx_bd = A["wx_bd"]; cn = A["cn"]; mt = A.get("mt"); lruT = A.get("lruT")
    NB = S // 512
    if True:
        def const_tile(ap, shape, name, dt=F32):
            t = cx.sb(shape, dt, name)
            P.add("sp", lambda h: h.dma_start(out=t[:], in_=ap), writes=[name], dsem=name)
            return t

        gbc = None if fused else const_tile(bcast_rows(g, 128), [128, D], "gbc")
        gretbc = const_tile(bcast_rows(g_ret, 128), [128, 192], "gretbc")
        ghgbc = const_tile(bcast_rows(g_hg, 128), [128, 192], "ghgbc")
        invf = const_tile(cn["c_invf"], [128, 1], "invf")
        dmask = const_tile(cn["c_dmask"], [128, 2, 128], "dmask")
        qin = const_tile(cn["c_qin"], [128, 512], "qin")
        cdv = const_tile(cn["c_cd"], [128, 1], "cdv")
        kout = const_tile(cn["c_kout"], [128, 192], "kout")
        maskL = const_tile(cn["c_maskL"], [128, 128], "maskL")
        maskU = const_tile(cn["c_maskU"], [128, 128], "maskU")
        reset = const_tile(cn["c_reset"], [128, 512], "reset")
        lbp_t = const_tile(lbp, [128, 4], "lbp")
        lflag_t = const_tile(lflag, [128, 1], "lflag")
        lrup = const_tile(lru_p, [128, 8], "lrup")
        wfs = load_w_bf16(P, cx, wfm, D, NFM, "wfs")
        wts = load_w_bf16(P, cx, wtm, D, NTM, "wts")
        wab = cx.sb([128, 128], BF16, "wab")
        wxb = cx.sb([128, 128], BF16, "wxb")
        P.add("pool", lambda h: h.dma_start(out=wab[:], in_=wa_bd), writes=["wab"], dsem="wab")
        P.add("pool", lambda h: h.dma_start(out=wxb[:], in_=wx_bd), writes=["wxb"], dsem="wxb")

        sm = cx.sb([128, 16], F32, "smallp")
        for hd in range(2):
            P.add("dve", lambda h, hd=hd: h.tensor_tensor(out=sm[:, hd:hd + 1], in0=lbp_t[:, 2 * hd + 1:2 * hd + 2],
                                                           in1=lbp_t[:, 2 * hd:2 * hd + 1], op=ALU.subtract),
                  reads=["lbp"], writes=["sm_lb"])
        P.add("act", lambda h: h.activation(out=sm[:, 0:2], in_=sm[:, 0:2], func=AF.Sigmoid), reads=["sm_lb"], writes=["sm_lb"])
        P.add("dve", lambda h: h.tensor_scalar(out=sm[:, 0:2], in0=sm[:, 0:2], scalar1=lflag_t[:, 0:1], scalar2=None,
                                               op0=ALU.mult), reads=["sm_lb", "lflag"], writes=["sm_lb"])
        P.add("dve", lambda h: h.tensor_scalar(out=sm[:, 2:4], in0=sm[:, 0:2], scalar1=-1.0, scalar2=1.0,
                                               op0=ALU.mult, op1=ALU.add), reads=["sm_lb"], writes=["sm_oml"])
        P.add("act", lambda h: h.activation(out=sm[:, 4:5], in_=lrup[:, 7:8], func=AF.Exp, scale=-1.0),
              reads=["lrup"], writes=["sm_c"])
        P.add("dve", lambda h: h.tensor_scalar(out=sm[:, 4:5], in0=sm[:, 4:5], scalar1=1.0, scalar2=None, op0=ALU.add),
              reads=["sm_c"], writes=["sm_c"])
        P.add("act", lambda h: h.activation(out=sm[:, 4:5], in_=sm[:, 4:5], func=AF.Ln), reads=["sm_c"], writes=["sm_c"])
        P.add("dve", lambda h: h.tensor_scalar(out=sm[:, 4:5], in0=sm[:, 4:5], scalar1=-8.0, scalar2=None, op0=ALU.mult),
              reads=["sm_c"], writes=["sm_c"])

        if not fused:
            xin = [cx.sb([128, D], F32, "xin") for _ in range(2)]
            junk = cx.sb([128, D], BF16, "junk")
        else:
            mo4 = [cx.sb([128, 4, 512], BF16, f"mo4_{i}") for i in range(2)]
            lob = cx.sb([128, 512], BF16, "lob")
        st = [cx.sb([128, 4], F32, "st") for _ in range(2)]
        hb = [cx.sb([128, D], BF16, "hb") for _ in range(2)]
        hbq = [cx.sb([128, D], BF16, f"hbq{i}") for i in range(4)] if fused else None

        def emit_hload(Bx):
            for j in range(4):
                t0 = Bx * 512 + j * 128
                rk = t0 // (S // 2)
                ii_ = t0 % (S // 2)
                kk_ = ii_ // 1024
                row = rk * 1024 + ii_ % 1024
                P.add("sp", lambda h, j=j, kk_=kk_, row=row: h.dma_start(out=hbq[j][:], in_=A["Hg"][kk_].ap()[row:row + 128, :]),
                      reads=[f"Hg{kk_}"], writes=[f"hbq{j}"], dsem=f"hbld{j}")

        hT = cx.sb([128, 8, 512], BF16, "hT")
        pT = cx.ps([128, D], BF16, "pT")
        pF = [cx.ps([128, 512], F32, "pF") for _ in range(2)]
        pSr_ = cx.ps([128, 512], F32, "pSr")
        pSh_ = cx.ps([128, 512], F32, "pSh")
        pO_ = cx.ps([128, 512], F32, "pO")
        pKr_ = cx.ps([128, 512], F32, "pKr")
        pKh_ = cx.ps([128, 512], F32, "pKh")
        pSr = pSr_[:, 0:256].rearrange("p (a t) -> p a t", a=2)
        pSh = pSh_[:, 0:512].rearrange("p (a t) -> p a t", a=4)
        pO = pO_[:, 0:384].rearrange("p (a t) -> p a t", a=4)
        pKr = pKr_[:, 0:384].rearrange("p (a t) -> p a t", a=2)
        pKh = pKh_[:, 0:192].rearrange("p (a t) -> p a t", a=2)
        cnt = {"f": 0, "x": 0, "o": 0}
        P.excl.update(["pT", "pF0", "pF1", "pSr", "pSh", "pO", "pKr", "pKh"])
        fb = lambda name: cx.sb([128, 512], F32, name)
        bb = lambda name: cx.sb([128, 512], BF16, name)
        posi = cx.sb([128, 512], I32, "posi")
        ang = fb("ang"); angc = fb("angc"); tk = fb("tk"); r1 = fb("r1"); cosT = fb("cosT"); sinT = fb("sinT")
        raw = [fb(f"raw{i}") for i in range(4)]
        t1 = fb("t1"); t2 = fb("t2"); qrA = fb("qrA"); qrB = fb("qrB")
        t3 = fb("t3"); t4 = fb("t4")
        qsA = bb("qsA"); qsB = bb("qsB"); qiA = bb("qiA"); qiB = bb("qiB"); kbA = bb("kbA"); kbB = bb("kbB")
        qh = [fb(f"qh{i}") for i in range(2)]
        fg = [fb(f"fg{i}") for i in range(2)]
        lf = fb("lf"); kk = fb("kk"); cum = fb("cum"); cm = fb("cm"); eA = fb("eA"); eB = fb("eB")
        eE = [fb(f"eE{i}") for i in range(2)]
        eD = fb("eD")
        hqA = [bb(f"hqA{i}") for i in range(2)]; hqB = [bb(f"hqB{i}") for i in range(2)]
        hkA = [bb(f"hkA{i}") for i in range(2)]; hkB = [bb(f"hkB{i}") for i in range(2)]
        hqC = [bb(f"hqC{i}") for i in range(2)]; hkD = [bb(f"hkD{i}") for i in range(2)]
        lxb = cx.sb([128, 3 + 512], F32, "lxb")
        lgb = fb("lgb")
        xc = fb("xc"); xcb = bb("xcb"); rr = fb("rr"); ii = fb("ii"); aa = fb("aa"); a2 = fb("a2"); uu = fb("uu")
        hh_ = [fb(f"hscan{i}") for i in range(2)]
        gl = fb("gl"); lo = [fb(f"lo{i}") for i in range(2)]
        vb = [cx.sb([128, 192], BF16, f"vb{j}") for j in range(4)]
        vdec = [cx.sb([128, 192], BF16, f"vdec{j}") for j in range(4)]
        ib = [cx.sb([128, 192], BF16, f"ib{j}") for j in range(4)]
        gate = [cx.sb([128, 384], F32, f"gate{j}") for j in range(4)]
        ktok = [cx.sb([128, 2, 128], BF16, f"ktok{j}") for j in range(4)]
        kdtok = [cx.sb([128, 2, 128], BF16, f"kdtok{j}") for j in range(4)]
        scR = cx.sb([128, 2, 128], BF16, "scR")
        scH = cx.sb([128, 2, 128], BF16, "scH")
        mtmp = cx.sb([128, 4, 128], F32, "mtmp")
        SA = cx.sb([128, 2, 192], F32, "SA")
        SAb = [cx.sb([128, 2, 192], BF16, f"SAb{i}") for i in range(3)]
        SH = cx.sb([128, 2, 96], F32, "SH")
        SHb = [cx.sb([128, 2, 96], BF16, f"SHb{i}") for i in range(3)]
        gs = [cx.sb([128, 16], F32, f"gs{i}") for i in range(2)]
        sq = cx.sb([128, 384], F32, "sq")
        yn = cx.sb([128, 384], F32, "yn")
        mo = [cx.sb([128, 384], F32, f"mo{i}") for i in range(2)]
        P.add("dve", lambda h: h.memset(SA[:], 0.0), writes=["SA"])
        P.add("dve", lambda h: h.memset(SH[:], 0.0), writes=["SH"])
        P.add("pool", lambda h: h.memset(SAb[0][:], 0.0), writes=["SAb0"])
        P.add("pool", lambda h: h.memset(SHb[0][:], 0.0), writes=["SHb0"])
        P.add("pool", lambda h: h.memset(lxb[:, 0:3], 0.0), writes=["lxb"])

        def transpose_to(src_fn, n, srckeys, dst_ap_fn, dstkey, eng="act"):
            for i in range(n):
                P.add("pe", lambda h, i=i: h.transpose(out=pT[:, i * 128:(i + 1) * 128], in_=src_fn(i), identity=ident[:]),
                      reads=list(srckeys) + ["ident"], writes=["pT"])
            if eng == "act":
                P.add("act", lambda h: h.copy(out=dst_ap_fn(), in_=pT[:, 0:n * 128].rearrange("p (c t) -> p c t", c=n)),
                      reads=["pT"], writes=[dstkey])
            else:
                P.add("dve", lambda h: h.tensor_copy(out=dst_ap_fn(), in_=pT[:, 0:n * 128].rearrange("p (c t) -> p c t", c=n)),
                      reads=["pT"], writes=[dstkey])

        chunk_no = 0
        tile_no = 0
        def lru_gen(B):
            T0 = B * 512
            hb_i = B % 2
            P.add("dve", lambda h: h.tensor_scalar(out=xc[:], in0=lxb[:, 0:512], scalar1=lrup[:, 0:1], scalar2=lrup[:, 4:5], op0=ALU.mult, op1=ALU.add),
                  reads=["lxb", "lrup"], writes=["xc"])
            for w in range(1, 4):
                P.add("dve", lambda h, w=w: h.scalar_tensor_tensor(out=xc[:], in0=lxb[:, w:w + 512], scalar=lrup[:, w:w + 1], op0=ALU.mult, in1=xc[:], op1=ALU.add),
                      reads=["lxb", "lrup", "xc"], writes=["xc"])
            P.add("pool", lambda h: h.tensor_copy(out=lxb[:, 0:3], in_=lxb[:, 512:515]), reads=["lxb", "xc"], writes=["lxb"])
            P.add("pool", lambda h: h.tensor_tensor(out=gl[:], in0=lgb[:], in1=lgb[:], op=ALU.mult), reads=["lgb"], writes=["gl"])
            P.add("pool", lambda h: h.tensor_scalar(out=gl[:], in0=gl[:], scalar1=0.044715, scalar2=1.0, op0=ALU.mult, op1=ALU.add), reads=["gl"], writes=["gl"])
            P.add("pool", lambda h: h.tensor_tensor(out=gl[:], in0=gl[:], in1=lgb[:], op=ALU.mult), reads=["gl", "lgb"], writes=["gl"])
            yield
            P.add("act", lambda h: h.copy(out=xcb[:], in_=xc[:]), reads=["xc"], writes=["xcb"])
            for (wb, dst, bcol, nm) in ((wab, rr, 5, "rr"), (wxb, ii, 6, "ii")):
                b = cnt["f"] % 2
                cnt["f"] += 1
                P.add("pe", lambda h, b=b, wb=wb: h.matmul(pF[b][:], lhsT=wb[:], rhs=xcb[:], start=True, stop=True),
                      reads=["wab", "wxb", "xcb"], writes=[f"pF{b}"])
                P.add("act", lambda h, b=b, dst=dst, bcol=bcol: h.activation(out=dst[:], in_=pF[b][:], func=AF.Sigmoid, bias=lrup[:, bcol:bcol + 1]),
                      reads=[f"pF{b}", "lrup"], writes=[nm])
            P.add("act", lambda h: h.activation(out=aa[:], in_=rr[:], func=AF.Exp, scale=sm[:, 4:5]), reads=["rr", "sm_c"], writes=["aa"])
            P.add("pool", lambda h: h.tensor_tensor(out=a2[:], in0=aa[:], in1=aa[:], op=ALU.mult), reads=["aa"], writes=["a2"])
            P.add("pool", lambda h: h.tensor_scalar(out=a2[:], in0=a2[:], scalar1=-1.0, scalar2=1.0, op0=ALU.mult, op1=ALU.add), reads=["a2"], writes=["a2"])
            yield
            P.add("act", lambda h: h.activation(out=gl[:], in_=gl[:], func=AF.Sigmoid, scale=GELU_C), reads=["gl"], writes=["gl"])
            P.add("pool", lambda h: h.tensor_tensor(out=gl[:], in0=gl[:], in1=lgb[:], op=ALU.mult), reads=["gl", "lgb"], writes=["gl"])
            yield
            P.add("act", lambda h: h.activation(out=a2[:], in_=a2[:], func=AF.Sqrt), reads=["a2"], writes=["a2"])
            P.add("pool", lambda h: h.tensor_tensor(out=uu[:], in0=ii[:], in1=xc[:], op=ALU.mult), reads=["ii", "xc"], writes=["uu"])
            P.add("pool", lambda h: h.tensor_tensor(out=uu[:], in0=uu[:], in1=a2[:], op=ALU.mult), reads=["uu", "a2"], writes=["uu"])
            prev = hh_[1 - hb_i]
            init = 0.0 if B == 0 else prev[:, 511:512]
            P.add("dve", lambda h, hb_i=hb_i, init=init: h.tensor_tensor_scan(
                out=hh_[hb_i][:], data0=aa[:], data1=uu[:], initial=init, op0=ALU.mult, op1=ALU.add),
                reads=["aa", "uu", f"hs{1 - hb_i}"], writes=[f"hs{hb_i}"])
            P.add("pool", lambda h, hb_i=hb_i: h.tensor_tensor(out=lo[hb_i][:], in0=gl[:], in1=hh_[hb_i][:], op=ALU.mult),
                  reads=["gl", f"hs{hb_i}"], writes=[f"lo{hb_i}"])
            yield
            if not fused:
                P.add("sp", lambda h, hb_i=hb_i, T0=T0: h.dma_start(out=lruT[:, T0:T0 + 512], in_=lo[hb_i][:]),
                      reads=[f"lo{hb_i}"], writes=["lrudram"], dsem=f"lost{hb_i}")
            else:
                P.add("act", lambda h, hb_i=hb_i: h.copy(out=lob[:], in_=lo[hb_i][:]), reads=[f"lo{hb_i}"], writes=["lob"])
                transpose_to(lambda i: lob[:, i * 128:(i + 1) * 128], 4, ["lob"],
                             lambda B=B: mo4[B % 2][:, :, 384:512], f"mo4_{B % 2}", eng="dve")
                kM = T0 // 2048
                rM = T0 % 2048
                P.add("sp", lambda h, B=B, kM=kM, rM=rM: h.dma_start(
                    out=A["Mh"][kM].ap()[rM:rM + 512, :].rearrange("(j p) c -> p j c", p=128), in_=mo4[B % 2][:]),
                    reads=[f"mo4_{B % 2}"], writes=[f"Mh{kM}"], dsem=f"most{B % 2}")
                if rM + 512 == 2048:
                    P.add("pool", lambda h, kM=kM: h.collective_compute(
                        "AllGather", ALU.bypass, replica_groups=PAIRS, ins=[A["Mh"][kM].ap().opt()], outs=[A["Mg"][kM].ap().opt()]),
                        reads=[f"Mh{kM}"], writes=[f"Mg{kM}"], dsem="cc", inc=1)

        lru_state = {"g": None}

        def lru_step():
            if lru_state["g"] is not None:
                try:
                    next(lru_state["g"])
                except StopIteration:
                    lru_state["g"] = None

        if fused:
            emit_hload(0)
        P.mark('blocks')
        for B in range(NB):
            T0 = B * 512
            for j in range(4):
                t0 = T0 + j * 128
                b = cnt["x"] % 2
                cnt["x"] += 1
                if not fused:
                    P.add("sp", lambda h, b=b, t0=t0: h.dma_start(out=xin[b][:], in_=x[t0:t0 + 128, :]),
                          writes=[f"xin{b}"], dsem=f"xin{b}")
                    rms_rstd(P, cx, xin[b][:], f"xin{b}", D, junk[:], st[b][:, 0:1], st[b][:, 1:2], f"n{b}")
                    P.add("dve", lambda h, b=b: h.scalar_tensor_tensor(
                        out=hb[b][:], in0=xin[b][:], scalar=st[b][:, 1:2], op0=ALU.mult, in1=gbc[:], op1=ALU.mult),
                        reads=[f"xin{b}", f"n{b}_rstd", "gbc"], writes=[f"hb{b}"])
                if fused:
                    transpose_to(lambda i, j=j: hbq[j][:, i * 128:(i + 1) * 128], 8, [f"hbq{j}"],
                                 lambda j=j: hT[:, :, j * 128:(j + 1) * 128], "hT")
                else:
                    transpose_to(lambda i, b=b: hb[b][:, i * 128:(i + 1) * 128], 8, [f"hb{b}"],
                                 lambda j=j: hT[:, :, j * 128:(j + 1) * 128], "hT")
            if B > 0:
                lru_state["g"] = lru_gen(B - 1)
                lru_step()
            P.mark('rope_tables')
            P.mark('fm_proj')
            for c in range(10):
                b = cnt["f"] % 2
                cnt["f"] += 1
                for dc in range(8):
                    P.add("pe", lambda h, b=b, c=c, dc=dc: h.matmul(
                        pF[b][:], lhsT=wfs[:, dc, c * 128:(c + 1) * 128], rhs=hT[:, dc, :],
                        start=(dc == 0), stop=(dc == 7)), reads=["wfs", "hT"], writes=[f"pF{b}"])
                if c < 4:
                    P.add("act", lambda h, b=b, c=c: h.copy(out=raw[c][:], in_=pF[b][:]), reads=[f"pF{b}"], writes=[f"raw{c}"])
                elif c < 6:
                    P.add("act", lambda h, b=b, c=c: h.activation(out=qh[c - 4][:], in_=pF[b][:], func=AF.Silu),
                          reads=[f"pF{b}"], writes=[f"qh{c - 4}"])
                elif c < 8:
                    P.add("act", lambda h, b=b, c=c: h.activation(out=fg[c - 6][:], in_=pF[b][:], func=AF.Sigmoid),
                          reads=[f"pF{b}"], writes=[f"fg{c - 6}"])
                elif c == 8:
                    P.add("act", lambda h, b=b: h.copy(out=lxb[:, 3:515], in_=pF[b][:]), reads=[f"pF{b}"], writes=["lxb"])
                else:
                    P.add("act", lambda h, b=b: h.copy(out=lgb[:], in_=pF[b][:]), reads=[f"pF{b}"], writes=["lgb"])
                if c == 1:
                    P.add("sp", lambda h, T0=T0: h.dma_start(out=posi[:], in_=bcast_rows(pos[T0:T0 + 512], 128)),
                          writes=["posi"], dsem="posi")
                    P.add("dve", lambda h: h.tensor_copy(out=ang[:], in_=posi[:]), reads=["posi"], writes=["ang"])
                    P.add("dve", lambda h: h.tensor_scalar(out=ang[:], in0=ang[:], scalar1=invf[:, 0:1], scalar2=None, op0=ALU.mult),
                          reads=["ang", "invf"], writes=["ang"])
                    P.add("dve", lambda h: h.tensor_scalar(out=angc[:], in0=ang[:], scalar1=math.pi / 2, scalar2=None, op0=ALU.add),
                          reads=["ang"], writes=["angc"])
                    for (dst, src, skey, key) in ((sinT, ang, "ang", "sinT"), (cosT, angc, "angc", "cosT")):
                        P.add("dve", lambda h, src=src: h.tensor_scalar(
                            out=tk[:], in0=src[:], scalar1=1.0 / TWO_PI, scalar2=MAGIC, op0=ALU.mult, op1=ALU.add),
                            reads=[skey], writes=["tk"])
                        P.add("dve", lambda h: h.tensor_scalar(out=tk[:], in0=tk[:], scalar1=-MAGIC, scalar2=None, op0=ALU.add),
                              reads=["tk"], writes=["tk"])
                        P.add("dve", lambda h, src=src: h.scalar_tensor_tensor(
                            out=r1[:], in0=tk[:], scalar=-C1, op0=ALU.mult, in1=src[:], op1=ALU.add),
                            reads=["tk", skey], writes=["r1"])
                        P.add("dve", lambda h: h.scalar_tensor_tensor(
                            out=r1[:], in0=tk[:], scalar=-C2, op0=ALU.mult, in1=r1[:], op1=ALU.add),
                            reads=["tk", "r1"], writes=["r1"])
                        P.add("dve", lambda h: h.tensor_scalar(
                            out=r1[:], in0=r1[:], scalar1=math.pi, scalar2=-math.pi, op0=ALU.min, op1=ALU.max),
                            reads=["r1"], writes=["r1"])
                        P.add("act", lambda h, dst=dst: h.activation(out=dst[:], in_=r1[:], func=AF.Sin),
                              reads=["r1"], writes=[key])
                if c in (1, 7, 8):
                    lru_step()
                if c == 4:
                    def rope(eng, a, bq, oA, oB, ta, tb, keys_out):
                        E = lambda fn, reads, writes: P.add(eng, fn, reads=reads, writes=writes)
                        E(lambda h: h.tensor_tensor(out=ta[:], in0=raw[a][:], in1=cosT[:], op=ALU.mult), [f"raw{a}", "cosT"], [keys_out + "ta"])
                        E(lambda h: h.tensor_tensor(out=tb[:], in0=raw[bq][:], in1=sinT[:], op=ALU.mult), [f"raw{bq}", "sinT"], [keys_out + "tb"])
                        E(lambda h: h.tensor_tensor(out=oA[:], in0=ta[:], in1=tb[:], op=ALU.subtract), [keys_out + "ta", keys_out + "tb"], [keys_out + "A"])
                        E(lambda h: h.tensor_tensor(out=ta[:], in0=raw[a][:], in1=sinT[:], op=ALU.mult), [f"raw{a}", "sinT"], [keys_out + "ta"])
                        E(lambda h: h.tensor_tensor(out=tb[:], in0=raw[bq][:], in1=cosT[:], op=ALU.mult), [f"raw{bq}", "cosT"], [keys_out + "tb"])
                        E(lambda h: h.tensor_tensor(out=oB[:], in0=ta[:], in1=tb[:], op=ALU.add), [keys_out + "ta", keys_out + "tb"], [keys_out + "B"])
                    rope("dve", 0, 1, qrA, qrB, t1, t2, "qr")
                    P.add("dve", lambda h: h.tensor_copy(out=qsA[:], in_=qrA[:]), reads=["qrA"], writes=["qsA"])
                    P.add("dve", lambda h: h.tensor_copy(out=qsB[:], in_=qrB[:]), reads=["qrB"], writes=["qsB"])
                    P.add("dve", lambda h: h.tensor_tensor(out=qiA[:], in0=qrA[:], in1=qin[:], op=ALU.mult), reads=["qrA", "qin"], writes=["qiA"])
                    P.add("dve", lambda h: h.tensor_tensor(out=qiB[:], in0=qrB[:], in1=qin[:], op=ALU.mult), reads=["qrB", "qin"], writes=["qiB"])
                    rope("dve", 2, 3, kbA, kbB, t3, t4, "kb")
            P.mark('tm_proj')
            for j in range(4):
                for gidx in range(2):
                    b = cnt["f"] % 2
                    cnt["f"] += 1
                    for dc in range(8):
                        P.add("pe", lambda h, b=b, j=j, gidx=gidx, dc=dc: h.matmul(
                            pF[b][:, 0:384], lhsT=hT[:, dc, j * 128:(j + 1) * 128], rhs=wts[:, dc, gidx * 384:(gidx + 1) * 384],
                            start=(dc == 0), stop=(dc == 7)), reads=["wts", "hT"], writes=[f"pF{b}"])
                    if gidx == 0:
                        P.add("act", lambda h, b=b, j=j: h.copy(out=vb[j][:], in_=pF[b][:, 0:192]),
                              reads=[f"pF{b}"], writes=[f"vb{j}"])
                        P.add("dve", lambda h, b=b, j=j: h.tensor_tensor(out=vdec[j][:], in0=pF[b][:, 0:192], in1=kout[:], op=ALU.mult),
                              reads=[f"pF{b}", "kout"], writes=[f"vdec{j}"])
                        P.add("act", lambda h, b=b, j=j: h.copy(out=ib[j][:], in_=pF[b][:, 192:384]),
                              reads=[f"pF{b}"], writes=[f"ib{j}"])
                    else:
                        P.add("act", lambda h, b=b, j=j: h.activation(out=gate[j][:], in_=pF[b][:, 0:384], func=AF.Silu),
                              reads=[f"pF{b}"], writes=[f"gate{j}"])
                        if j == 1:
                            lru_step()
            if fused and B + 1 < NB:
                emit_hload(B + 1)
            P.mark('rope')
            for j in range(4):
                transpose_to(lambda i, j=j: (kbA if i == 0 else kbB)[:, j * 128:(j + 1) * 128], 2, ["kbA", "kbB"],
                             lambda j=j: ktok[j][:], f"ktok{j}", eng="dve")
            P.mark('hg_elem')
            for hd in range(2):
                P.add("dve", lambda h, hd=hd: h.tensor_scalar(
                    out=fg[hd][:], in0=fg[hd][:], scalar1=sm[:, 2 + hd:3 + hd], scalar2=sm[:, hd:hd + 1], op0=ALU.mult, op1=ALU.add),
                    reads=[f"fg{hd}", "sm_lb", "sm_oml"], writes=[f"fg{hd}"])
                P.add("act", lambda h, hd=hd: h.activation(out=lf[:], in_=fg[hd][:], func=AF.Ln), reads=[f"fg{hd}"], writes=["lf"])
                P.add("dve", lambda h, hd=hd: h.tensor_scalar(out=kk[:], in0=fg[hd][:], scalar1=-1.0, scalar2=1.0, op0=ALU.mult, op1=ALU.add),
                      reads=[f"fg{hd}"], writes=["kk"])
                P.add("dve", lambda h: h.tensor_tensor_scan(out=cum[:], data0=reset[:], data1=lf[:], initial=0.0, op0=ALU.mult, op1=ALU.add),
                      reads=["reset", "lf"], writes=["cum"])
                c3 = cum[:].rearrange("p (c t) -> p c t", t=64)
                mid_b = bass.AP(cum[:].tensor, cum[:, 32:33].offset, [list(cum[:].ap[0]), [64, 8], [0, 64]])
                last_b = bass.AP(cum[:].tensor, cum[:, 63:64].offset, [list(cum[:].ap[0]), [64, 8], [0, 64]])
                P.add("dve", lambda h, c3=c3, mid_b=mid_b: h.tensor_tensor(
                    out=cm[:].rearrange("p (c t) -> p c t", t=64), in0=c3, in1=mid_b, op=ALU.subtract),
                    reads=["cum"], writes=["cm"])
                P.add("act", lambda h: h.activation(out=eA[:], in_=cm[:], func=AF.Exp), reads=["cm"], writes=["eA"])
                P.add("act", lambda h: h.activation(out=eB[:], in_=cm[:], func=AF.Exp, scale=-1.0), reads=["cm"], writes=["eB"])
                P.add("act", lambda h, hd=hd: h.activation(out=eE[hd][:], in_=cum[:], func=AF.Exp), reads=["cum"], writes=[f"eE{hd}"])
                P.add("dve", lambda h, c3=c3, last_b=last_b: h.tensor_tensor(
                    out=cm[:].rearrange("p (c t) -> p c t", t=64), in0=c3, in1=last_b, op=ALU.subtract),
                    reads=["cum", "eA", "eB"], writes=["cm"])
                P.add("act", lambda h: h.activation(out=eD[:], in_=cm[:], func=AF.Exp, scale=-1.0), reads=["cm"], writes=["eD"])
                P.add("dve", lambda h, hd=hd: h.tensor_tensor(out=hqA[hd][:], in0=qh[hd][:], in1=eA[:], op=ALU.mult), reads=[f"qh{hd}", "eA"], writes=[f"hqA{hd}"])
                P.add("dve", lambda h, hd=hd: h.tensor_tensor(out=hqB[hd][:], in0=qh[hd][:], in1=eB[:], op=ALU.mult), reads=[f"qh{hd}", "eB"], writes=[f"hqB{hd}"])
                P.add("dve", lambda h, hd=hd: h.tensor_tensor(out=hkA[hd][:], in0=kk[:], in1=eA[:], op=ALU.mult), reads=["kk", "eA"], writes=[f"hkA{hd}"])
                P.add("dve", lambda h, hd=hd: h.tensor_tensor(out=hkB[hd][:], in0=kk[:], in1=eB[:], op=ALU.mult), reads=["kk", "eB"], writes=[f"hkB{hd}"])
                P.add("dve", lambda h, hd=hd: h.tensor_tensor(out=hqC[hd][:], in0=qh[hd][:], in1=eE[hd][:], op=ALU.mult), reads=[f"qh{hd}", f"eE{hd}"], writes=[f"hqC{hd}"])
                P.add("dve", lambda h, hd=hd: h.tensor_tensor(out=hkD[hd][:], in0=kk[:], in1=eD[:], op=ALU.mult), reads=["kk", "eD"], writes=[f"hkD{hd}"])
            for j in range(4):
                transpose_to(lambda i, j=j: hkD[i][:, j * 128:(j + 1) * 128], 2, ["hkD0", "hkD1"],
                             lambda j=j: kdtok[j][:], f"kdtok{j}", eng="dve")
            P.mark('tiles')
            for j in range(4):
                tsl = slice(j * 128, (j + 1) * 128)
                t0 = T0 + j * 128
                prev_mm = ()
                for a in range(2):
                    psl = slice(a * 64, (a + 1) * 64)
                    P.add("pe", lambda h, a=a, psl=psl, tsl=tsl: h.matmul(pSr[:, a, :], lhsT=kbA[psl, tsl], rhs=qsA[psl, tsl], start=True, stop=False),
                          reads=["kbA", "qsA"], writes=["pSr"], after=prev_mm)
                    o2 = P.add("pe", lambda h, a=a, psl=psl, tsl=tsl: h.matmul(pSr[:, a, :], lhsT=kbB[psl, tsl], rhs=qsB[psl, tsl], start=False, stop=True),
                               reads=["kbB", "qsB"], writes=["pSr"])
                    prev_mm = (o2,)
                P.add("dve", lambda h: h.tensor_tensor(out=scR[:], in0=pSr, in1=dmask[:], op=ALU.mult),
                      reads=["pSr", "dmask"], writes=["scR"])
                for hd in range(2):
                    P.add("pe", lambda h, hd=hd, tsl=tsl: h.matmul(pSh[:, 2 * hd, :], lhsT=hkB[hd][:, tsl], rhs=hqA[hd][:, tsl], start=True, stop=True),
                          reads=[f"hkB{hd}", f"hqA{hd}"], writes=["pSh"])
                    P.add("pe", lambda h, hd=hd, tsl=tsl: h.matmul(pSh[:, 2 * hd + 1, :], lhsT=hkA[hd][:, tsl], rhs=hqB[hd][:, tsl], start=True, stop=True),
                          reads=[f"hkA{hd}", f"hqB{hd}"], writes=["pSh"])
                for hd in range(2):
                    P.add("dve", lambda h, hd=hd: h.tensor_tensor(out=mtmp[:, 2 * hd, :], in0=pSh[:, 2 * hd, :], in1=maskL[:], op=ALU.mult),
                          reads=["pSh", "maskL"], writes=["mtmp"])
                    P.add("dve", lambda h, hd=hd: h.tensor_tensor(out=mtmp[:, 2 * hd + 1, :], in0=pSh[:, 2 * hd + 1, :], in1=maskU[:], op=ALU.mult),
                          reads=["pSh", "maskU"], writes=["mtmp"])
                    P.add("pool", lambda h, hd=hd: h.tensor_tensor(out=scH[:, hd, :], in0=mtmp[:, 2 * hd, :], in1=mtmp[:, 2 * hd + 1, :], op=ALU.add),
                          reads=["mtmp"], writes=["scH"])
                for ci in range(2):
                    n = chunk_no + ci
                    cur, nxt = n % 3, (n + 1) % 3
                    csl = slice(ci * 64, (ci + 1) * 64)
                    ctok = slice(j * 128 + ci * 64, j * 128 + (ci + 1) * 64)
                    if ci == 0:
                        pass
                    P.add("pe", lambda h, csl=csl, j=j: h.matmul(pKr[:, 0, :], lhsT=ktok[j][csl, 0, :], rhs=vdec[j][csl, :], start=True, stop=True),
                          reads=[f"ktok{j}", f"vdec{j}"], writes=["pKr"])
                    P.add("pe", lambda h, csl=csl, j=j: h.matmul(pKr[:, 1, :], lhsT=ktok[j][csl, 1, :], rhs=vdec[j][csl, :], start=True, stop=True),
                          reads=[f"ktok{j}", f"vdec{j}"], writes=["pKr"])
                    for hd in range(2):
                        P.add("pe", lambda h, csl=csl, j=j, hd=hd: h.matmul(pKh[:, hd, :], lhsT=kdtok[j][csl, hd, :], rhs=ib[j][csl, hd * 96:(hd + 1) * 96], start=True, stop=True),
                              reads=[f"kdtok{j}", f"ib{j}"], writes=["pKh"])
                    P.add("dve", lambda h: h.scalar_tensor_tensor(out=SA[:], in0=SA[:], scalar=cdv[:, 0:1], op0=ALU.mult, in1=pKr, op1=ALU.add),
                          reads=["SA", "cdv", "pKr"], writes=["SA"])
                    P.add("act", lambda h, nxt=nxt: h.copy(out=SAb[nxt][:], in_=SA[:]), reads=["SA"], writes=[f"SAb{nxt}"])
                    for hd in range(2):
                        lastcol = j * 128 + ci * 64 + 63
                        P.add("dve", lambda h, hd=hd, lastcol=lastcol: h.scalar_tensor_tensor(
                            out=SH[:, hd, :], in0=SH[:, hd, :], scalar=eE[hd][:, lastcol:lastcol + 1], op0=ALU.mult, in1=pKh[:, hd, :], op1=ALU.add),
                            reads=["SH", f"eE{hd}", "pKh"], writes=["SH"])
                    P.add("act", lambda h, nxt=nxt: h.copy(out=SHb[nxt][:], in_=SH[:]), reads=["SH"], writes=[f"SHb{nxt}"])
                for a in range(2):
                    P.add("pe", lambda h, a=a, j=j: h.matmul(pO[:, a, :], lhsT=scR[:, a, :], rhs=vb[j][:, a * 96:(a + 1) * 96], start=True, stop=False),
                          reads=["scR", f"vb{j}"], writes=["pO"])
                    for ci in range(2):
                        n = chunk_no + ci
                        cur = n % 3
                        psl = slice(a * 64, (a + 1) * 64)
                        ctok = slice(j * 128 + ci * 64, j * 128 + (ci + 1) * 64)
                        osl = slice(ci * 64, (ci + 1) * 64)
                        last = (ci == 1)
                        P.add("pe", lambda h, a=a, cur=cur, psl=psl, ctok=ctok, osl=osl: h.matmul(
                            pO[osl, a, :], lhsT=qiA[psl, ctok], rhs=SAb[cur][psl, 0, a * 96:(a + 1) * 96], start=False, stop=False),
                            reads=["qiA", f"SAb{cur}"], writes=["pO"])
                        P.add("pe", lambda h, a=a, cur=cur, psl=psl, ctok=ctok, osl=osl, last=last: h.matmul(
                            pO[osl, a, :], lhsT=qiB[psl, ctok], rhs=SAb[cur][psl, 1, a * 96:(a + 1) * 96], start=False, stop=last),
                            reads=["qiB", f"SAb{cur}"], writes=["pO"])
                for hd in range(2):
                    P.add("pe", lambda h, hd=hd, j=j: h.matmul(pO[:, 2 + hd, :], lhsT=scH[:, hd, :], rhs=ib[j][:, hd * 96:(hd + 1) * 96], start=True, stop=False),
                          reads=["scH", f"ib{j}"], writes=["pO"])
                    for ci in range(2):
                        n = chunk_no + ci
                        cur = n % 3
                        ctok = slice(j * 128 + ci * 64, j * 128 + (ci + 1) * 64)
                        osl = slice(ci * 64, (ci + 1) * 64)
                        P.add("pe", lambda h, hd=hd, cur=cur, ctok=ctok, osl=osl, ci=ci: h.matmul(
                            pO[osl, 2 + hd, :], lhsT=hqC[hd][:, ctok], rhs=SHb[cur][:, hd, :], start=False, stop=(ci == 1)),
                            reads=[f"hqC{hd}", f"SHb{cur}"], writes=["pO"])
                chunk_no += 2
                ob = tile_no % 2
                tile_no += 1
                G = gs[ob]
                P.add("dve", lambda h, G=G: h.tensor_reduce(out=G[:, 0:2], in_=pO[:, 0:2, :], op=ALU.add, axis=AX.X),
                      reads=["pO"], writes=[f"G{ob}"])
                P.add("act", lambda h: h.activation(out=sq[:], in_=pO.rearrange("p a e -> p (a e)"), func=AF.Square),
                      reads=["pO"], writes=["sq"])
                P.add("dve", lambda h, G=G: h.tensor_reduce(out=G[:, 4:8], in_=sq[:].rearrange("p (a e) -> p a e", a=4), op=ALU.add, axis=AX.X),
                      reads=["sq"], writes=[f"G{ob}"])
                P.add("dve", lambda h, G=G: h.tensor_scalar(out=G[:, 0:2], in0=G[:, 0:2], scalar1=1.0 / 96, scalar2=None, op0=ALU.mult),
                      reads=[f"G{ob}"], writes=[f"G{ob}"])
                P.add("dve", lambda h, G=G: h.tensor_tensor(out=G[:, 2:4], in0=G[:, 0:2], in1=G[:, 0:2], op=ALU.mult),
                      reads=[f"G{ob}"], writes=[f"G{ob}"])
                P.add("dve", lambda h, G=G: h.tensor_scalar(out=G[:, 4:8], in0=G[:, 4:8], scalar1=1.0 / 96, scalar2=EPS, op0=ALU.mult, op1=ALU.add),
                      reads=[f"G{ob}"], writes=[f"G{ob}"])
                P.add("dve", lambda h, G=G: h.tensor_tensor(out=G[:, 4:6], in0=G[:, 4:6], in1=G[:, 2:4], op=ALU.subtract),
                      reads=[f"G{ob}"], writes=[f"G{ob}"])
                P.add("act", lambda h, G=G: h.activation(out=G[:, 4:8], in_=G[:, 4:8], func=AF.Sqrt), reads=[f"G{ob}"], writes=[f"G{ob}"])
                P.add("dve", lambda h, G=G: h.reciprocal(out=G[:, 8:12], in_=G[:, 4:8]), reads=[f"G{ob}"], writes=[f"G{ob}"])
                for a in range(2):
                    P.add("dve", lambda h, G=G, a=a: h.tensor_scalar(
                        out=yn[:, a * 96:(a + 1) * 96], in0=pO[:, a, :], scalar1=G[:, a:a + 1], scalar2=G[:, 8 + a:9 + a],
                        op0=ALU.subtract, op1=ALU.mult), reads=["pO", f"G{ob}"], writes=["yn"])
                for hd in range(2):
                    P.add("dve", lambda h, G=G, hd=hd: h.tensor_scalar(
                        out=yn[:, 192 + hd * 96:192 + (hd + 1) * 96], in0=pO[:, 2 + hd, :], scalar1=G[:, 10 + hd:11 + hd], scalar2=None,
                        op0=ALU.mult), reads=["pO", f"G{ob}"], writes=["yn"])
                P.add("pool", lambda h: h.tensor_tensor(out=yn[:, 0:192], in0=yn[:, 0:192], in1=gretbc[:], op=ALU.mult),
                      reads=["yn", "gretbc"], writes=["yn"])
                P.add("pool", lambda h: h.tensor_tensor(out=yn[:, 192:384], in0=yn[:, 192:384], in1=ghgbc[:], op=ALU.mult),
                      reads=["yn", "ghgbc"], writes=["yn"])
                if not fused:
                    P.add("pool", lambda h, ob=ob, j=j: h.tensor_tensor(out=mo[ob][:], in0=yn[:], in1=gate[j][:], op=ALU.mult),
                          reads=["yn", f"gate{j}"], writes=[f"mo{ob}"])
                    P.add("sp", lambda h, ob=ob, t0=t0: h.dma_start(out=mt[t0:t0 + 128, :], in_=mo[ob][:]),
                          reads=[f"mo{ob}"], writes=["mtdram"], dsem=f"most{ob}")
                else:
                    P.add("pool", lambda h, j=j, B=B: h.tensor_tensor(out=mo4[B % 2][:, j, 0:384], in0=yn[:], in1=gate[j][:], op=ALU.mult),
                          reads=["yn", f"gate{j}"], writes=[f"mo4_{B % 2}"])
        for _ in lru_gen(NB - 1):
            pass
        if not fused:
            P.add("sp", lambda h: h.nop(), reads=[], writes=["mo0", "mo1", "lo0", "lo1"])


def build_mixer(S):
    nc = bass.Bass("TRN2", target_bir_lowering=False)
    dt_in = lambda name, shape, dt=F32: nc.dram_tensor(name, list(shape), dt, kind="ExternalInput").ap()
    A = {"x": dt_in("x", [S, D]), "pos": dt_in("pos", [S], I32), "g": dt_in("g", [D]), "wfm": dt_in("wfm", [D, NFM]),
         "wtm": dt_in("wtm", [D, NTM]), "g_ret": dt_in("g_ret", [192]), "g_hg": dt_in("g_hg", [192]),
         "lbp": dt_in("lbp", [128, 4]), "lflag": dt_in("lflag", [128, 1]), "lru_p": dt_in("lru_p", [128, 8]),
         "wa_bd": dt_in("wa_bd", [128, 128]), "wx_bd": dt_in("wx_bd", [128, 128])}
    A["cn"] = {k: dt_in(k, v.shape) for k, v in mixer_consts(0).items()}
    A["mt"] = nc.dram_tensor("mt", [S, 384], F32, kind="ExternalOutput").ap()
    A["lruT"] = nc.dram_tensor("lruT", [128, S], F32, kind="ExternalOutput").ap()
    with ExitStack() as stack:
        P = Prog(nc, stack)
        cx = Ctx(nc, stack)
        ident = make_identity(P, cx)
        mixer_body(P, nc, cx, ident, S, A, False)
        P.emit()
    return nc


def mixer_pack(inp, l, hh):
    f = np.float32
    w_in = inp["w_in"][l]
    o_rq, o_rk, o_rv, o_rg, o_hq, o_hf, o_hi, o_hg, o_lx, o_lg = 0, 384, 768, 1152, 1536, 2048, 2560, 2944, 3328, 3584
    wfm = np.zeros((D, NFM), f)
    for a in range(2):
        h = 2 * hh + a
        for (base, cA, cB) in ((o_rq, 0, 1), (o_rk, 2, 3)):
            wfm[:, cA * 128 + a * 64: cA * 128 + a * 64 + 48] = w_in[:, base + h * 96: base + h * 96 + 48]
            wfm[:, cB * 128 + a * 64: cB * 128 + a * 64 + 48] = w_in[:, base + h * 96 + 48: base + h * 96 + 96]
        wfm[:, (4 + a) * 128:(5 + a) * 128] = w_in[:, o_hq + h * 128: o_hq + (h + 1) * 128]
        wfm[:, (6 + a) * 128:(7 + a) * 128] = w_in[:, o_hf + h * 128: o_hf + (h + 1) * 128]
    wfm[:, 8 * 128:9 * 128] = w_in[:, o_lx + hh * 128: o_lx + (hh + 1) * 128]
    wfm[:, 9 * 128:10 * 128] = w_in[:, o_lg + hh * 128: o_lg + (hh + 1) * 128]
    wtm = np.concatenate([w_in[:, o_rv + hh * 192: o_rv + (hh + 1) * 192], w_in[:, o_hi + hh * 192: o_hi + (hh + 1) * 192],
                          w_in[:, o_rg + hh * 192: o_rg + (hh + 1) * 192], w_in[:, o_hg + hh * 192: o_hg + (hh + 1) * 192]], axis=1)
    lbp = np.zeros((128, 4), f)
    for a in range(2):
        h = 2 * hh + a
        lbp[:, 2 * a] = inp["hg_lb_param"][0, h * 128:(h + 1) * 128]
        lbp[:, 2 * a + 1] = inp["hg_lb_param"][1, h * 128:(h + 1) * 128]
    lflag = np.full((128, 1), float(l), f)
    ch = slice(hh * 128, (hh + 1) * 128)
    lru_p = np.zeros((128, 8), f)
    lru_p[:, 0:4] = inp["lru_conv_w"][l][:, ch].T
    lru_p[:, 4] = inp["lru_conv_b"][l][ch]
    lru_p[:, 5] = inp["lru_ba"][l][ch]
    lru_p[:, 6] = inp["lru_bx"][l][ch]
    lru_p[:, 7] = inp["lru_lambda"][l][ch]
    wa_bd = np.zeros((128, 128), f)
    wx_bd = np.zeros((128, 128), f)
    for a in range(2):
        wa_bd[a * 64:(a + 1) * 64, a * 64:(a + 1) * 64] = inp["lru_wa"][l][2 * hh + a]
        wx_bd[a * 64:(a + 1) * 64, a * 64:(a + 1) * 64] = inp["lru_wx"][l][2 * hh + a]
    d = {"g": inp["mix_norm_g"][l], "wfm": wfm, "wtm": np.ascontiguousarray(wtm),
         "g_ret": np.ascontiguousarray(inp["ret_gn_g"][l][hh * 192:(hh + 1) * 192]),
         "g_hg": np.ascontiguousarray(inp["hg_norm_g"][l][hh * 192:(hh + 1) * 192]),
         "lbp": lbp, "lflag": lflag, "lru_p": lru_p, "wa_bd": wa_bd, "wx_bd": wx_bd}
    d.update(mixer_consts(hh))
    return d


def mixer_unpack(res_pair):
    r0, r1 = res_pair
    ret = np.concatenate([r0["mt"][:, 0:192], r1["mt"][:, 0:192]], axis=1)
    hg = np.concatenate([r0["mt"][:, 192:384], r1["mt"][:, 192:384]], axis=1)
    lru = np.concatenate([r0["lruT"].T, r1["lruT"].T], axis=1)
    return np.ascontiguousarray(np.concatenate([ret, hg, lru], axis=1))


_PROGS = {}


def _prog(name, builder):
    if name not in _PROGS:
        _PROGS[name] = builder()
    return _PROGS[name]


def _run(nc, in_maps):
    res = run_bass_kernel_spmd(nc, in_maps, core_ids=list(range(8)))
    return res.results


def kernel_unfused(**inp):
    inp = {k: np.asarray(v) for k, v in inp.items()}
    Bn, S, _ = inp["x"].shape
    NCORES = 8
    T = Bn * S // NCORES
    depth = inp["w_in"].shape[0]
    c32 = lambda a: np.ascontiguousarray(a, dtype=np.float32)
    x = c32(inp["x"]).reshape(NCORES, T, D)
    pos = np.ascontiguousarray(inp["positions"].astype(np.int32))
    for l in range(depth):
        nc = _prog("ffn", lambda: build_ffn(T, False))
        maps = [{"x": x[c], "g": c32(inp["ffn1_norm_g"][l]), "w1": c32(inp["ffn1_w1"][l]), "w3": c32(inp["ffn1_w3"][l]),
                 "w2": c32(inp["ffn1_w2"][l])} for c in range(NCORES)]
        r = _run(nc, maps)
        x = np.stack([r[c]["y"] for c in range(NCORES)], 0)
        nc = _prog("mixer", lambda: build_mixer(S))
        xfull = x.reshape(Bn, S, D)
        maps = []
        for b in range(Bn):
            for hh in range(2):
                d = mixer_pack(inp, l, hh)
                d["x"] = xfull[b]
                d["pos"] = pos[b]
                maps.append(d)
        r = _run(nc, maps)
        mixed = np.stack([mixer_unpack((r[2 * b], r[2 * b + 1])) for b in range(Bn)], 0).reshape(NCORES, T, D)
        nc = _prog("xattn", lambda: build_xattn(T))
        maps = [{"x": x[c], "mixed": mixed[c], "mem": c32(inp["mem"][c // 2]), "gx": c32(inp["xattn_norm_g"][l]),
                 "gm": c32(inp["xattn_mem_g"][l]), "w_out": c32(inp["w_out"][l]), "wq": c32(inp["xattn_wq"][l]),
                 "wkv": c32(inp["xattn_wkv"][l]), "wo": c32(inp["xattn_wo"][l])} for c in range(NCORES)]
        r = _run(nc, maps)
        x = np.stack([r[c]["y"] for c in range(NCORES)], 0)
        last = (l == depth - 1)
        nc = _prog("ffn_fin", lambda: build_ffn(T, True)) if last else _prog("ffn", lambda: build_ffn(T, False))
        maps = [{"x": x[c], "g": c32(inp["ffn2_norm_g"][l]), "w1": c32(inp["ffn2_w1"][l]), "w3": c32(inp["ffn2_w3"][l]),
                 "w2": c32(inp["ffn2_w2"][l])} for c in range(NCORES)]
        if last:
            for m in maps:
                m["gf"] = c32(inp["final_norm_g"])
        r = _run(nc, maps)
        x = np.stack([r[c]["y"] for c in range(NCORES)], 0)
    return x.reshape(Bn, S, D).astype(np.float32)


PAIRS = [[0, 1], [2, 3], [4, 5], [6, 7]]
_UID = [0]


class SCtx(Ctx):
    def sb(self, shape, dt, name=None):
        _UID[0] += 1
        return self.stack.enter_context(self.nc.sbuf_tensor(f"{name or 't'}_u{_UID[0]}", list(shape), dt))

    def ps(self, shape, dt, name=None):
        _UID[0] += 1
        return self.stack.enter_context(self.nc.psum_tensor(f"{name or 'p'}_u{_UID[0]}", list(shape), dt))


def stage_ffn(P, nc, ident, T, src, dst, g, w1, w3, w2, mode, gx=None, Hs=None, Hg=None):
    NG = T // 512
    with ExitStack() as sst:
        cx = SCtx(nc, sst)
        gbc = cx.sb([128, D], F32, "gbc")
        P.add("sp", lambda h: h.dma_start(out=gbc[:], in_=bcast_rows(g, 128)), writes=["gbc"], dsem="gbc")
        if mode != "plain":
            gxbc = cx.sb([128, D], F32, "gxbc")
            P.add("sp", lambda h: h.dma_start(out=gxbc[:], in_=bcast_rows(gx, 128)), writes=["gxbc"], dsem="gxbc")
        w1s, w3s, w2s = load_ffn_weights(P, cx, w1, w3, w2)
        xin = [cx.sb([128, D], F32, "xin") for _ in range(2)]
        hb = cx.sb([128, 4, D], BF16, "hb")
        hT = cx.sb([128, 8, 512], BF16, "hT")
        gT = cx.sb([128, NFC, 512], BF16, "gT")
        s1 = cx.sb([128, 512], F32, "s1")
        xr = [cx.sb([128, 512], F32, "xr") for _ in range(2)]
        yo = [cx.sb([128, D], F32, "yo") for _ in range(2)]
        st = [cx.sb([128, 4], F32, "st") for _ in range(2)]
        if mode != "plain":
            hsb = cx.sb([128, D], BF16, "hsb")
        pT = [cx.ps([128, D], BF16, "pT") for _ in range(2)]
        ps1 = [cx.ps([128, 512], F32, "ps1") for _ in range(2)]
        ps3 = [cx.ps([128, 512], F32, "ps3") for _ in range(2)]
        psy = [cx.ps([128, 512], F32, "psy") for _ in range(2)]
        P.excl.update(["pT0", "pT1"] + [f"psw1s{b}" for b in range(2)] + [f"psw3s{b}" for b in range(2)] + [f"psy{b}" for b in range(2)])
        ti = ui = yi = 0
        cntr = {"ti": 0, "tr": 0}

        def emit_norm(gi):
            for j in range(4):
                t0 = gi * 512 + j * 128
                b = cntr["ti"] % 2
                cntr["ti"] += 1
                xt = xin[b]
                P.add("sp", lambda h, xt=xt, t0=t0: h.dma_start(out=xt[:], in_=src[t0:t0 + 128, :]),
                      reads=["srcdram"], writes=[f"xin{b}"], dsem=f"xin{b}")
                ss = st[b][:, 0:1]
                rstd = st[b][:, 1:2]
                rms_rstd(P, cx, xt[:], f"xin{b}", D, hb[:, j, :], ss, rstd, f"n{b}", junk_keys=[f"hb{j}"])
                P.add("dve", lambda h, xt=xt, rstd=rstd, j=j: h.scalar_tensor_tensor(
                    out=hb[:, j, :], in0=xt[:], scalar=rstd, op0=ALU.mult, in1=gbc[:], op1=ALU.mult),
                    reads=[f"xin{b}", f"n{b}_rstd", "gbc", f"n{b}_junk"], writes=[f"hb{j}"])

        emit_norm(0)
        for gi in range(NG):
            for j in range(4):
                pb = cntr["tr"] % 2
                cntr["tr"] += 1
                for dc in range(8):
                    P.add("pe", lambda h, j=j, dc=dc, pb=pb: h.transpose(
                        out=pT[pb][:, dc * 128:(dc + 1) * 128], in_=hb[:, j, dc * 128:(dc + 1) * 128], identity=ident[:]),
                        reads=[f"hb{j}", "ident"], writes=[f"pT{pb}"])
                P.add("act", lambda h, j=j, pb=pb: h.copy(
                    out=hT[:, :, j * 128:(j + 1) * 128], in_=pT[pb][:].rearrange("p (c t) -> p c t", c=8)),
                    reads=[f"pT{pb}"], writes=["hT"])
            if gi + 1 < NG:
                emit_norm(gi + 1)
            for fc in range(NFC):
                b = ui % 2
                for (ws, ps, nm) in ((w1s, ps1[b], "w1s"), (w3s, ps3[b], "w3s")):
                    for dc in range(8):
                        P.add("pe", lambda h, ws=ws, ps=ps, dc=dc, fc=fc: h.matmul(
                            ps[:], lhsT=ws[:, dc, fc * 128:(fc + 1) * 128], rhs=hT[:, dc, :],
                            start=(dc == 0), stop=(dc == 7)),
                            reads=[ffn_w13_key(fc), "hT"], writes=[f"ps{nm}{b}"])
                P.add("act", lambda h, b=b: h.activation(out=s1[:], in_=ps1[b][:], func=AF.Silu),
                      reads=[f"psw1s{b}"], writes=["s1"])
                P.add("dve", lambda h, b=b, fc=fc: h.tensor_tensor(
                    out=gT[:, fc, :], in0=s1[:], in1=ps3[b][:], op=ALU.mult),
                    reads=["s1", f"psw3s{b}"], writes=["gT"])
                ui += 1
            for j in range(4):
                t0 = gi * 512 + j * 128
                ob = (gi * 4 + j) % 2
                for hf in range(2):
                    b = yi % 2
                    P.add("sp", lambda h, b=b, t0=t0, hf=hf: h.dma_start(
                        out=xr[b][:], in_=src[t0:t0 + 128, hf * 512:(hf + 1) * 512]),
                        reads=["srcdram"], writes=[f"xr{b}"], dsem=f"xr{b}")
                    for fc in range(NFC):
                        P.add("pe", lambda h, b=b, fc=fc, j=j, hf=hf: h.matmul(
                            psy[b][:], lhsT=gT[:, fc, j * 128:(j + 1) * 128], rhs=w2s[:, fc, hf * 512:(hf + 1) * 512],
                            start=(fc == 0), stop=(fc == NFC - 1)),
                            reads=["gT", ffn_w2_key(fc)], writes=[f"psy{b}"])
                    P.add("dve", lambda h, b=b, ob=ob, hf=hf: h.scalar_tensor_tensor(
                        out=yo[ob][:, hf * 512:(hf + 1) * 512], in0=psy[b][:], scalar=0.5, op0=ALU.mult,
                        in1=xr[b][:], op1=ALU.add),
                        reads=[f"psy{b}", f"xr{b}"], writes=[f"yo{ob}"])
                    yi += 1
                if mode == "final":
                    ss = st[ob][:, 2:3]
                    rstd = st[ob][:, 3:4]
                    rms_rstd(P, cx, yo[ob][:], f"yo{ob}", D, hsb[:], ss, rstd, f"f{ob}", junk_keys=["hsb"])
                    P.add("dve", lambda h, ob=ob, rstd=rstd: h.scalar_tensor_tensor(
                        out=yo[ob][:], in0=yo[ob][:], scalar=rstd, op0=ALU.mult, in1=gxbc[:], op1=ALU.mult),
                        reads=[f"yo{ob}", f"f{ob}_rstd", "gxbc"], writes=[f"yo{ob}"])
                P.add("sp", lambda h, ob=ob, t0=t0: h.dma_start(out=dst[t0:t0 + 128, :], in_=yo[ob][:]),
                      reads=[f"yo{ob}"], writes=["dstdram"], dsem=f"yst{ob}")
                if mode == "hs":
                    ss = st[ob][:, 2:3]
                    rstd = st[ob][:, 3:4]
                    rms_rstd(P, cx, yo[ob][:], f"yo{ob}", D, hsb[:], ss, rstd, f"f{ob}", junk_keys=["hsb"])
                    P.add("dve", lambda h, ob=ob, rstd=rstd: h.scalar_tensor_tensor(
                        out=hsb[:], in0=yo[ob][:], scalar=rstd, op0=ALU.mult, in1=gxbc[:], op1=ALU.mult),
                        reads=[f"yo{ob}", f"f{ob}_rstd", "gxbc", f"f{ob}_junk"], writes=["hsb"])
                    k = t0 // 1024
                    r0 = t0 % 1024
                    P.add("sp", lambda h, k=k, r0=r0: h.dma_start(out=Hs[k].ap()[r0:r0 + 128, :], in_=hsb[:]),
                          reads=["hsb"], writes=[f"Hs{k}"], dsem="hsst")
                    if r0 + 128 == 1024:
                        P.add("pool", lambda h, k=k: h.collective_compute(
                            "AllGather", ALU.bypass, replica_groups=PAIRS, ins=[Hs[k].ap().opt()], outs=[Hg[k].ap().opt()]),
                            reads=[f"Hs{k}"], writes=[f"Hg{k}"], dsem="cc", inc=1)
        P.barrier()
        P.emit()


W_OUT_PERM = np.concatenate([np.arange(0, 192), np.arange(384, 576), np.arange(768, 896),
                             np.arange(192, 384), np.arange(576, 768), np.arange(896, 1024)])


def build_fused(T=4096, S=8192, depth=2):
    nc = bass.Bass("TRN2", target_bir_lowering=False)
    ext = lambda name, shape, dt=F32: nc.dram_tensor(name, list(shape), dt, kind="ExternalInput").ap()
    x_in = ext("x", [T, D])
    pos = ext("pos", [S], I32)
    mem = ext("mem", [256, D])
    flags = ext("flags", [128, 2])
    gf = ext("gf", [D])
    y_out = nc.dram_tensor("y", [T, D], F32, kind="ExternalOutput").ap()
    cn = {k: ext(k, v.shape) for k, v in mixer_consts(0).items()}
    L = []
    for l in range(depth):
        d = {}
        for nm, shp in (("f1g", [D]), ("f1w1", [D, DFF]), ("f1w3", [D, DFF]), ("f1w2", [DFF, D]),
                        ("f2g", [D]), ("f2w1", [D, DFF]), ("f2w3", [D, DFF]), ("f2w2", [DFF, D]),
                        ("gmix", [D]), ("wfm", [D, NFM]), ("wtm", [D, NTM]), ("g_ret", [192]), ("g_hg", [192]),
                        ("lbp", [128, 4]), ("lflag", [128, 1]), ("lru_p", [128, 8]), ("wa_bd", [128, 128]), ("wx_bd", [128, 128]),
                        ("w_outp", [D, D]), ("gx", [D]), ("gm", [D]), ("wq", [D, D]), ("wkv", [D, 2 * D]), ("wo", [D, D])):
            d[nm] = ext(f"{nm}_{l}", shp)
        L.append(d)
    XA = nc.dram_tensor("XA", [T, D], F32)
    XB = nc.dram_tensor("XB", [T, D], F32)
    XC = nc.dram_tensor("XC", [T, D], F32)
    Hs = [nc.dram_tensor(f"Hs{k}", [1024, D], BF16) for k in range(4)]
    Hg = [nc.dram_tensor(f"Hg{k}", [2048, D], BF16) for k in range(4)]
    Mh = [nc.dram_tensor(f"Mh{k}", [2048, 512], BF16) for k in range(4)]
    Mg = [nc.dram_tensor(f"Mg{k}", [4096, 512], BF16) for k in range(4)]
    with ExitStack() as gstack:
        P = Prog(nc, gstack)
        gcx = SCtx(nc, gstack)
        ident = make_identity(P, gcx)
        flg = gcx.sb([128, 2], F32, "flg")
        P.add("sp", lambda h: h.dma_start(out=flg[:], in_=flags), writes=["flg"], dsem="flg")
        cur_in = x_in
        for l in range(depth):
            W = L[l]
            last = (l == depth - 1)
            stage_ffn(P, nc, ident, T, cur_in, XA.ap(), W["f1g"], W["f1w1"], W["f1w3"], W["f1w2"], "hs", gx=W["gmix"], Hs=Hs, Hg=Hg)
            with ExitStack() as sst:
                cx = SCtx(nc, sst)
                A = {"pos": pos, "wfm": W["wfm"], "wtm": W["wtm"], "g_ret": W["g_ret"], "g_hg": W["g_hg"], "lbp": W["lbp"],
                     "lflag": W["lflag"], "lru_p": W["lru_p"], "wa_bd": W["wa_bd"], "wx_bd": W["wx_bd"], "cn": cn,
                     "Hg": Hg, "Mh": Mh, "Mg": Mg}
                mixer_body(P, nc, cx, ident, S, A, True)
                P.barrier()
                P.emit()
            with ExitStack() as sst:
                cx = SCtx(nc, sst)
                A = {"x": XA.ap(), "mem": mem, "gx": W["gx"], "gm": W["gm"], "w_out": W["w_outp"], "wq": W["wq"], "wkv": W["wkv"],
                     "wo": W["wo"], "y": XB.ap(), "Mg": Mg, "flg": flg}
                xattn_body(P, nc, cx, ident, T, A, True)
                P.barrier()
                P.emit()
            if last:
                stage_ffn(P, nc, ident, T, XB.ap(), y_out, W["f2g"], W["f2w1"], W["f2w3"], W["f2w2"], "final", gx=gf)
            else:
                stage_ffn(P, nc, ident, T, XB.ap(), XC.ap(), W["f2g"], W["f2w1"], W["f2w3"], W["f2w2"], "plain")
                cur_in = XC.ap()
    return nc


def fused_inputs(inp, c):
    f = np.float32
    c32 = lambda a: np.ascontiguousarray(a, dtype=f)
    Bn, S, _ = inp["x"].shape
    T = Bn * S // 8
    b, r = c // 2, c % 2
    d = {"x": c32(inp["x"].reshape(8, T, D)[c]), "pos": np.ascontiguousarray(inp["positions"][b].astype(np.int32)),
         "mem": c32(inp["mem"][b]), "gf": c32(inp["final_norm_g"])}
    fl = np.zeros((128, 2), f)
    fl[:, r] = 1.0
    d["flags"] = fl
    d.update(mixer_consts(r))
    for l in range(inp["w_in"].shape[0]):
        mp = mixer_pack(inp, l, r)
        for k in ("wfm", "wtm", "g_ret", "g_hg", "lbp", "lflag", "lru_p", "wa_bd", "wx_bd"):
            d[f"{k}_{l}"] = c32(mp[k])
        d[f"gmix_{l}"] = c32(inp["mix_norm_g"][l])
        for a, bname in (("f1", "ffn1"), ("f2", "ffn2")):
            d[f"{a}g_{l}"] = c32(inp[f"{bname}_norm_g"][l])
            d[f"{a}w1_{l}"] = c32(inp[f"{bname}_w1"][l])
            d[f"{a}w3_{l}"] = c32(inp[f"{bname}_w3"][l])
            d[f"{a}w2_{l}"] = c32(inp[f"{bname}_w2"][l])
        d[f"w_outp_{l}"] = c32(inp["w_out"][l][W_OUT_PERM])
        d[f"gx_{l}"] = c32(inp["xattn_norm_g"][l])
        d[f"gm_{l}"] = c32(inp["xattn_mem_g"][l])
        d[f"wq_{l}"] = c32(inp["xattn_wq"][l])
        d[f"wkv_{l}"] = c32(inp["xattn_wkv"][l])
        d[f"wo_{l}"] = c32(inp["xattn_wo"][l])
    return d


def kernel(**inp):
    inp = {k: np.asarray(v) for k, v in inp.items()}
    Bn, S, _ = inp["x"].shape
    T = Bn * S // 8
    depth = inp["w_in"].shape[0]
    nc = _prog("fused", lambda: build_fused(T, S, depth))
    maps = [fused_inputs(inp, c) for c in range(8)]
    r = _run(nc, maps)
    out = np.stack([r[c]["y"] for c in range(8)], 0)
    return out.reshape(Bn, S, D).astype(np.float32)
```
